# Optimizing a Trainium2 kernel written in Bass

```python
import jax, jax.numpy as jnp
from jax import lax
import numpy as np

D_MODEL = 1024
BATCH = 16
SEQ = 2048
DEPTH = 2
DEC_BATCH = 4
DEC_SEQ = 4096
PAST_LEN = 128

HEAD_DIM = 64
MIX_WIDTH = D_MODEL
ATTN_HEADS = (MIX_WIDTH // 2) // HEAD_DIM
ATTN_WIDTH = ATTN_HEADS * HEAD_DIM
GLA_HEADS = (MIX_WIDTH - ATTN_WIDTH) // HEAD_DIM
GLA_DV = HEAD_DIM
GLA_DK = HEAD_DIM // 2
GLA_KW = GLA_HEADS * GLA_DK
GLA_VW = GLA_HEADS * GLA_DV
GATE_RANK = 16
GATE_TAU = 16.0
GLA_CHUNK = 32
DIL_PAIRS = ((128, 1), (512, 4), (2048, 16))
D_FF = ((8 * D_MODEL // 3 + 127) // 128) * 128
CONV_WIDTH = 3
EPS = 1e-6
NEG = -1e30
IN_SIZES = (ATTN_WIDTH, ATTN_WIDTH, ATTN_WIDTH, GLA_KW, GLA_KW, GLA_VW, GLA_VW, GATE_RANK)
IN_WIDTH = sum(IN_SIZES)
IN_OFFSETS = [int(v) for v in np.cumsum(IN_SIZES)[:-1]]

kernel_name = "hybrid_dilated_gla_encoder"


def rmsnorm(x, g):
    xf = x.astype(jnp.float32)
    y = xf * lax.rsqrt(jnp.mean(xf * xf, axis=-1, keepdims=True) + EPS)
    return (y * g.astype(jnp.float32)).astype(x.dtype)


def alibi_slopes(n):
    return jnp.exp2(-8.0 * jnp.arange(1, n + 1, dtype=jnp.float32) / n)


def dilated_branch(q, k, v, window, dilation, slopes):
    B, S, H, Dh = q.shape
    n_side = window // (2 * dilation)
    blk = n_side
    L = S // dilation
    nb = -(-L // blk)
    Lp = nb * blk

    def to_blocks(t):
        t = t.reshape(B, L, dilation, H, Dh)
        t = jnp.pad(t, ((0, 0), (0, Lp - L), (0, 0), (0, 0), (0, 0)))
        return t.reshape(B, nb, blk, dilation, H, Dh)

    def neighbours(t):
        tp = jnp.pad(t, ((0, 0), (1, 1), (0, 0), (0, 0), (0, 0), (0, 0)))
        return jnp.concatenate([tp[:, :-2], tp[:, 1:-1], tp[:, 2:]], axis=2)

    qb = to_blocks(q)
    kn = neighbours(to_blocks(k))
    vn = neighbours(to_blocks(v))
    s = jnp.einsum('bnqrhd,bnkrhd->bnrhqk', qb, kn).astype(jnp.float32) * (Dh ** -0.5)
    qi = jnp.arange(blk)
    kj = jnp.arange(3 * blk)
    delta = kj[None, :] - blk - qi[:, None]
    key_m = jnp.arange(nb)[:, None, None] * blk + delta[None]
    valid = (jnp.abs(delta) <= n_side)[None] & (key_m >= 0) & (key_m < L)
    dist = (jnp.abs(delta) * dilation).astype(jnp.float32)
    bias = -slopes[:, None, None] * dist[None]
    s = jnp.where(valid[None, :, None, None], s + bias[None, None, None], NEG)
    m = jnp.max(s, axis=-1, keepdims=True)
    p = jnp.exp(s - m)
    den = jnp.sum(p, axis=-1, keepdims=True)
    lse = (m + jnp.log(den))[..., 0]
    o = jnp.einsum('bnrhqk,bnkrhd->bnqrhd', (p / den).astype(v.dtype), vn)
    o = o.reshape(B, Lp, dilation, H, Dh)[:, :L].reshape(B, S, H, Dh)
    lse = lse.transpose(0, 1, 4, 2, 3).reshape(B, Lp, dilation, H)[:, :L].reshape(B, S, H)
    return o, lse


def dilated_attention(q, k, v):
    slopes = alibi_slopes(q.shape[2])
    outs, lses = [], []
    for window, dilation in DIL_PAIRS:
        o, l = dilated_branch(q, k, v, window, dilation, slopes)
        outs.append(o.astype(jnp.float32))
        lses.append(l)
    w = jax.nn.softmax(jnp.stack(lses, axis=0), axis=0)
    return jnp.sum(w[..., None] * jnp.stack(outs, axis=0), axis=0)


def gla_direction(q, k, v, log_a):
    B, S, H, dk = q.shape
    dv = v.shape[-1]
    C = GLA_CHUNK
    N = S // C
    qc = q.reshape(B, N, C, H, dk)
    kc = k.reshape(B, N, C, H, dk)
    vc = v.reshape(B, N, C, H, dv)
    b = jnp.cumsum(log_a.reshape(B, N, C, H, dk), axis=2)
    b_last = b[:, :, -1]
    q_in = qc * jnp.exp(b)
    k_intra = kc * jnp.exp(-b)
    k_state = kc * jnp.exp(b_last[:, :, None] - b)
    causal = jnp.tril(jnp.ones((C, C), dtype=bool))
    att = jnp.where(causal, jnp.einsum('bnihk,bnjhk->bnhij', q_in, k_intra), 0.0)
    o_intra = jnp.einsum('bnhij,bnjhv->bnihv', att, vc)
    u = jnp.einsum('bnjhk,bnjhv->bnhkv', k_state, vc)

    def step(state, inp):
        decay, u_n = inp
        return decay[..., None] * state + u_n, state

    _, s_prev = lax.scan(step, jnp.zeros((B, H, dk, dv), jnp.float32),
                         (jnp.exp(b_last).transpose(1, 0, 2, 3), u.transpose(1, 0, 2, 3, 4)))
    s_prev = s_prev.transpose(1, 0, 2, 3, 4)
    o_inter = jnp.einsum('bnihk,bnhkv->bnihv', q_in, s_prev)
    return (o_intra + o_inter).reshape(B, S, H, dv)


def gla_mixer(q, k, v, r, lr, wg_f, bg_f, wg_b, bg_b, g_norm):
    B, S, _ = q.shape
    f32 = jnp.float32
    qh = q.astype(f32).reshape(B, S, GLA_HEADS, GLA_DK) * (GLA_DK ** -0.5)
    kh = k.astype(f32).reshape(B, S, GLA_HEADS, GLA_DK)
    vh = v.astype(f32).reshape(B, S, GLA_HEADS, GLA_DV)
    la_f = (jax.nn.log_sigmoid((lr @ wg_f + bg_f).astype(f32)) / GATE_TAU).reshape(B, S, GLA_HEADS, GLA_DK)
    la_b = (jax.nn.log_sigmoid((lr @ wg_b + bg_b).astype(f32)) / GATE_TAU).reshape(B, S, GLA_HEADS, GLA_DK)
    o_f = gla_direction(qh, kh, vh, la_f)
    o_b = jnp.flip(gla_direction(jnp.flip(qh, 1), jnp.flip(kh, 1), jnp.flip(vh, 1), jnp.flip(la_b, 1)), 1)
    o = rmsnorm(o_f + o_b, g_norm)
    o = o * jax.nn.silu(r.astype(f32).reshape(B, S, GLA_HEADS, GLA_DV))
    return o.reshape(B, S, GLA_VW)


def dwconv_centred(a, w, b):
    S = a.shape[1]
    pad = CONV_WIDTH // 2
    ap = jnp.pad(a, ((0, 0), (pad, pad), (0, 0)))
    y = b
    for j in range(CONV_WIDTH):
        y = y + ap[:, j:j + S] * w[j]
    return y


def encoder_layer(x, n_pre, w_in, wg_f, bg_f, wg_b, bg_b, g_gla, w_out, n_post,
                  n_ffn_pre, w_up, conv_w, conv_b, w_down, n_ffn_post):
    B, S, _ = x.shape
    h = rmsnorm(x, n_pre)
    aq, ak, av, gq, gk, gv, gr, glr = jnp.split(h @ w_in, IN_OFFSETS, axis=-1)
    heads = lambda t: t.reshape(B, S, ATTN_HEADS, HEAD_DIM)
    o_attn = dilated_attention(heads(aq), heads(ak), heads(av)).reshape(B, S, ATTN_WIDTH)
    o_gla = gla_mixer(gq, gk, gv, gr, glr, wg_f, bg_f, wg_b, bg_b, g_gla)
    mix = jnp.concatenate([o_attn, o_gla], axis=-1).astype(x.dtype) @ w_out
    x = x + rmsnorm(mix, n_post)
    h = rmsnorm(x, n_ffn_pre)
    a, g = jnp.split(h @ w_up, 2, axis=-1)
    a = dwconv_centred(a, conv_w, conv_b)
    f = (jax.nn.gelu(a, approximate=True) * g) @ w_down
    return x + rmsnorm(f, n_ffn_post)


def setup_inputs(seed: int = 0) -> dict:
    key = jax.random.key(seed)
    ks = jax.random.split(key, 20)
    nrm = lambda k, shape, s: jax.random.normal(k, shape, jnp.float32) * s
    gain = lambda k, shape: 1.0 + 0.05 * jax.random.normal(k, shape, jnp.float32)
    return {
        "x_prompt": nrm(ks[0], (BATCH, SEQ, D_MODEL), 1.0),
        "x_sample": nrm(ks[1], (DEC_BATCH, DEC_SEQ, D_MODEL), 1.0),
        "norm_mix_pre": gain(ks[2], (DEPTH, D_MODEL)),
        "w_in": nrm(ks[3], (DEPTH, D_MODEL, IN_WIDTH), D_MODEL ** -0.5),
        "w_gate_fwd": nrm(ks[4], (DEPTH, GATE_RANK, GLA_KW), GATE_RANK ** -0.5),
        "b_gate_fwd": nrm(ks[5], (DEPTH, GLA_KW), 0.1),
        "w_gate_bwd": nrm(ks[6], (DEPTH, GATE_RANK, GLA_KW), GATE_RANK ** -0.5),
        "b_gate_bwd": nrm(ks[7], (DEPTH, GLA_KW), 0.1),
        "gla_norm": gain(ks[8], (DEPTH, GLA_DV)),
        "w_out": nrm(ks[9], (DEPTH, MIX_WIDTH, D_MODEL), MIX_WIDTH ** -0.5),
        "norm_mix_post": gain(ks[10], (DEPTH, D_MODEL)),
        "norm_ffn_pre": gain(ks[11], (DEPTH, D_MODEL)),
        "w_up": nrm(ks[12], (DEPTH, D_MODEL, 2 * D_FF), D_MODEL ** -0.5),
        "conv_w": nrm(ks[13], (DEPTH, CONV_WIDTH, D_FF), CONV_WIDTH ** -0.5),
        "conv_b": nrm(ks[14], (DEPTH, D_FF), 0.02),
        "w_down": nrm(ks[15], (DEPTH, D_FF, D_MODEL), D_FF ** -0.5),
        "norm_ffn_post": gain(ks[16], (DEPTH, D_MODEL)),
    }


def reference(x_prompt, x_sample, norm_mix_pre, w_in, w_gate_fwd, b_gate_fwd, w_gate_bwd, b_gate_bwd,
              gla_norm, w_out, norm_mix_post, norm_ffn_pre, w_up, conv_w, conv_b, w_down, norm_ffn_post):
    def trunk(x):
        for l in range(DEPTH):
            x = encoder_layer(x, norm_mix_pre[l], w_in[l], w_gate_fwd[l], b_gate_fwd[l], w_gate_bwd[l],
                              b_gate_bwd[l], gla_norm[l], w_out[l], norm_mix_post[l], norm_ffn_pre[l],
                              w_up[l], conv_w[l], conv_b[l], w_down[l], norm_ffn_post[l])
        return x

    y_prompt = trunk(x_prompt)
    y_sample = trunk(x_sample)
    return (y_prompt, y_sample)
```

```python
import os
import numpy as np
from contextlib import ExitStack
import concourse.bass as bass
import concourse.mybir as mybir
from concourse.bass_utils import run_bass_kernel_spmd

F32 = mybir.dt.float32
BF16 = mybir.dt.bfloat16
AF = mybir.ActivationFunctionType
ALU = mybir.AluOpType
AX = mybir.AxisListType
F_INC = os.environ.get("F_INC", "1") == "1"
F_CONC = os.environ.get("F_CONC", "1") == "1"
F_ACC = os.environ.get("F_ACC", "1") == "1"
F_INCT = os.environ.get("F_INCT", "1") == "1"
F_INCA = os.environ.get("F_INCA", "1") == "1"

DM = 1024
SEG = 2048
PAD = 1024
INW = 3088
DFF = 2816
NFT = DFF // 128
NCORES = 8
EPS = 1e-6
BIG = 1.0e30
GT = (96, 96, 64)
BRANCHES = ((1, 1, 16), (4, 4, 4), (16, 16, 1))
SLOPES = [2.0 ** (-(h + 1)) for h in range(8)]
C_DA, C_DB, C_LTF, C_LUF, C_LTB, C_LLB, C_MF, C_MB, C_ID, C_END = 0, 128, 256, 384, 512, 640, 768, 896, 1024, 1152


def sl_(start, n, step=1):
    return slice(start, start + step * (n - 1) + 1, step)


class Buf:
    __slots__ = ("name", "w", "r", "pr", "pw", "wc")

    def __init__(self, name=""):
        self.name = name
        self.w = {}
        self.r = {}
        self.pr = {}
        self.pw = {}
        self.wc = True


class Tracker:
    ROLL = 30000

    def __init__(self, nc):
        self.nc = nc
        self.engs = {"pe": nc.tensor, "act": nc.scalar, "dve": nc.vector, "pool": nc.gpsimd, "sp": nc.sync}
        self.sems = []
        self.cur = {}
        self.cnt = {}
        self.seen = {e: {} for e in self.engs}
        for e in ("pe", "act", "dve", "pool"):
            self._new_sem(e)
        self.dq = {}
        for q, n in (("sp", 12), ("pool", 2), ("act", 2)):
            ks = []
            for i in range(n):
                k = len(self.sems)
                self.sems.append(nc.alloc_semaphore(name=f"d_{q}{i}"))
                self.cnt[k] = 0
                ks.append(k)
            self.dq[q] = [ks, 0]
        self.n_inst = 0
        self.n_wait = 0
        self.pend = {}

    def _new_sem(self, e):
        k = len(self.sems)
        self.sems.append(self.nc.alloc_semaphore(name=f"c_{e}{k}"))
        self.cnt[k] = 0
        self.cur[e] = k

    def _wait(self, eng, deps):
        seen = self.seen[eng]
        best = {}
        for (k, v) in deps:
            if best.get(k, 0) < v:
                best[k] = v
        for k, v in best.items():
            if seen.get(k, 0) < v:
                self.engs[eng].wait_ge(self.sems[k], v)
                seen[k] = v
                self.n_wait += 1

    @staticmethod
    def _deps(reads, writes, conc):
        deps = []
        for b in reads:
            deps.extend(b.w.items())
        for b in writes:
            if b.r:
                deps.extend(b.r.items())
                deps.extend(b.w.items())
            else:
                deps.extend(b.pr.items())
                deps.extend(b.pw.items())
                if not (conc and b.wc):
                    deps.extend(b.w.items())
        return deps

    @staticmethod
    def _commit(ev, reads, writes, conc):
        k, v = ev
        for b in reads:
            if b.r.get(k, 0) < v:
                b.r[k] = v
        for b in writes:
            if b.r:
                b.pr, b.pw = b.r, {}
                b.r = {}
                b.w = {k: v}
                b.wc = conc
            elif conc and b.wc:
                if b.w.get(k, 0) < v:
                    b.w[k] = v
            elif conc:
                b.pw = b.w
                b.w = {k: v}
                b.wc = True
            else:
                b.w = {k: v}
                b.wc = False
                b.pr, b.pw = {}, {}

    def op(self, eng, fn, reads=(), writes=(), inc=True, conc=False):
        if not F_INC:
            inc = True
        if not F_CONC:
            conc = False
        pend = self.pend.setdefault(eng, [[], []])
        if inc and not pend[0] and not pend[1] and self.cnt[self.cur[eng]] >= self.ROLL:
            self._new_sem(eng)
        self._wait(eng, self._deps(reads, writes, conc))
        inst = fn()
        self.n_inst += 1
        if not inc:
            pend[0].extend(reads)
            pend[1].extend(writes)
            return inst
        k = self.cur[eng]
        inst.then_inc(self.sems[k], 1)
        self.cnt[k] += 1
        self._commit((k, self.cnt[k]), list(reads) + pend[0], list(writes) + pend[1], conc)
        pend[0].clear()
        pend[1].clear()
        return inst

    def dma(self, q, out, in_, reads=(), writes=(), conc=False, **kw):
        if not F_CONC:
            conc = False
        ks, i = self.dq[q]
        k = ks[i % len(ks)]
        self.dq[q][1] = i + 1
        deps = self._deps(reads, writes, conc)
        if self.cnt[k] > 0:
            deps.append((k, self.cnt[k]))
        self._wait(q, deps)
        inst = self.engs[q].dma_start(out=out, in_=in_, **kw)
        inst.then_inc(self.sems[k], 16)
        self.cnt[k] += 16
        self.n_inst += 1
        self._commit((k, self.cnt[k]), reads, writes, conc)
        return inst

    def barrier(self):
        allev = [(k, v) for k, v in self.cnt.items() if v > 0]
        for e in self.engs:
            self._wait(e, allev)


class Builder:
    def __init__(self, nseg=3, nlayers=2, debug=(), phases=None):
        self.nseg = nseg
        self.T = nseg * SEG
        self.nlayers = nlayers
        self.debug = set(debug)
        self.phases = phases
        self.nc = bass.Bass("TRN2", target_bir_lowering=False)
        self.uid = 0

    def din(self, name, shape, dt=F32):
        return self.nc.dram_tensor(name, list(shape), dt, kind="ExternalInput").ap()

    def dscr(self, name, shape, dt):
        kind = "ExternalOutput" if name in self.debug else "Internal"
        return self.nc.dram_tensor(name, list(shape), dt, kind=kind).ap()

    def sbt(self, es, name, shape, dt):
        self.uid += 1
        t = es.enter_context(self.nc.sbuf_tensor(f"{name}_{self.uid}", list(shape), dt))
        return t.ap() if hasattr(t, "ap") else t

    def build(self):
        nc = self.nc
        T_ = self.T
        L = self.nlayers
        ns = self.nseg
        self.tr = Tracker(nc)
        tr = self.tr
        I = {}
        I["x"] = self.din("x", [T_, DM])
        I["flg"] = self.din("flg", [128, ns + 1])
        I["atab"] = self.din("atab", [128, (1 + 3 * ns) * 256])
        I["xden"] = self.din("xden", [ns, 8, 1024])
        I["consts"] = self.din("consts", [128, C_END])
        I["w_in"] = self.din("w_in", [L, DM, INW])
        I["w_out"] = self.din("w_out", [L, DM, DM])
        I["w_up"] = self.din("w_up", [L, DM, 2 * DFF])
        I["w_down"] = self.din("w_down", [L, DFF, DM])
        I["g_pre"] = self.din("g_pre", [L, 128, 8])
        I["g_fpre"] = self.din("g_fpre", [L, 128, 8])
        I["n_post"] = self.din("n_post", [L, 128, DM])
        I["n_fpost"] = self.din("n_fpost", [L, 128, DM])
        I["gnorm"] = self.din("gnorm", [L, 128, 512])
        I["wg_f"] = self.din("wg_f", [L, 17, 256])
        I["wg_b"] = self.din("wg_b", [L, 17, 256])
        I["cw"] = self.din("cw", [L, 128, NFT * 4])
        self.I = I
        self.y = nc.dram_tensor("y", [T_, DM], F32, kind="ExternalOutput").ap()
        S = {}
        S["qT"] = self.dscr("qT_s", [512, T_], BF16)
        S["kT"] = self.dscr("kT_s", [512, T_ + 2 * PAD], BF16)
        S["v"] = self.dscr("v_s", [4, T_ + 2 * PAD, 128], BF16)
        S["gqT"] = self.dscr("gqT_s", [3, 96, T_], F32)
        S["gkT"] = self.dscr("gkT_s", [3, 96, T_], F32)
        S["gk"] = self.dscr("gk_s", [T_, 256], F32)
        S["gv"] = self.dscr("gv_s", [T_, 512], BF16)
        S["gr"] = self.dscr("gr_s", [T_, 512], F32)
        S["lrh"] = self.dscr("lrh_s", [16, T_], BF16)
        S["lrl"] = self.dscr("lrl_s", [16, T_], BF16)
        S["of"] = self.dscr("of_s", [T_, 512], F32)
        S["mixT"] = self.dscr("mixT_s", [DM, T_], BF16)
        S["x1"] = self.dscr("x1_s", [T_, DM], F32)
        S["xT2"] = self.dscr("xT2_s", [DM, T_ + 2], BF16)
        S["xs"] = self.dscr("xs_s", [T_, DM], F32)
        self.S = S
        self.ps = [nc.alloc_psum_tensor(f"psb{i}", [128, 512], F32).ap() for i in range(8)]
        self.psb = [p.bitcast(BF16) for p in self.ps]

        with ExitStack() as es:
            self.flg = self.sbt(es, "flg", [128, ns + 1], F32)
            self.ident = self.sbt(es, "ident", [128, 128], BF16)
            self.B_const = Buf("const")
            zero = self.sbt(es, "zero", [128, 1024], BF16)
            idf = self.sbt(es, "idf", [128, 128], F32)
            tr.dma("sp", self.flg, I["flg"], writes=[self.B_const])
            tr.dma("sp", idf, I["consts"][:, C_ID:C_ID + 128], writes=[self.B_const])
            tr.op("dve", lambda: nc.vector.tensor_copy(out=self.ident, in_=idf), reads=[self.B_const], writes=[self.B_const])
            Bz = Buf("zero")
            tr.op("pool", lambda: nc.gpsimd.memset(zero, 0.0), writes=[Bz])
            for i in range(4):
                tr.dma("sp", S["kT"][i * 128:(i + 1) * 128, 0:PAD], zero, reads=[Bz])
                tr.dma("sp", S["kT"][i * 128:(i + 1) * 128, PAD + T_:PAD + T_ + PAD], zero, reads=[Bz])
                for j in range(PAD // 128):
                    tr.dma("sp", S["v"][i, j * 128:(j + 1) * 128, :], zero[:, 0:128], reads=[Bz])
                    tr.dma("sp", S["v"][i, PAD + T_ + j * 128:PAD + T_ + (j + 1) * 128, :], zero[:, 0:128], reads=[Bz])
            for kc in range(8):
                tr.dma("sp", S["xT2"][kc * 128:(kc + 1) * 128, 0:1], zero[:, 0:1], reads=[Bz], allow_slow_non_contiguous=True)
                tr.dma("sp", S["xT2"][kc * 128:(kc + 1) * 128, T_ + 1:T_ + 2], zero[:, 0:1], reads=[Bz], allow_slow_non_contiguous=True)
            zf = zero.bitcast(F32)
            for c0 in range(0, T_, 512):
                tr.dma("sp", S["gqT"][2, 64:96, c0:c0 + 512], zf[0:32, :], reads=[Bz])
                tr.dma("sp", S["gkT"][2, 64:96, c0:c0 + 512], zf[0:32, :], reads=[Bz])
            tr.barrier()

            want = self.phases
            for l in range(L):
                x_src = I["x"] if l == 0 else S["xs"]
                x_dst = self.y if l == L - 1 else S["xs"]
                if want is None or "p1" in want:
                    self.phase_inproj(l, x_src)
                    tr.barrier()
                if want is None or "p2" in want:
                    self.phase_attn(l)
                    tr.barrier()
                with ExitStack() as esl:
                    full = want is None
                    wup = bg = None
                    with ExitStack() as es_stg:
                        if full:
                            wup = self.sbt(esl, "wup", [128, 8, 2 * DFF], BF16)
                            CHB = 1408
                            stg = [self.sbt(es_stg, "bstg", [128, CHB], F32) for _ in range(2)]
                            Bs = [Buf("bstg") for _ in range(2)]
                            gsb = self.sbt(es_stg, "bgsb", [128, 8], F32)
                            Bg = Buf("bgsb")
                            tr.dma("sp", gsb, I["g_fpre"][l], writes=[Bg])
                            bg = self.weight_steps(wup, I["w_up"][l], DM, 2 * DFF, gsb, Bg, stg, Bs, CHB)
                        if want is None or "p3" in want:
                            self.phase_gla(l, bg)
                        if bg is not None:
                            for _ in bg:
                                pass
                        tr.barrier()
                    if want is None or "p4" in want:
                        self.phase_outproj(l, x_src)
                        tr.barrier()
                        self.phase_ffn(l, x_dst, wup)
                        tr.barrier()
            tr.barrier()
        return nc

    def weight_steps(self, wbf, wdram, K, N, gsb, Bg, stg, Bs, CH):
        nc, tr = self.nc, self.tr
        nk = K // 128
        ns_ = len(stg)
        i = 0
        for kc in range(nk):
            for c0 in range(0, N, CH):
                cw = min(CH, N - c0)
                s = i % ns_
                tr.dma("sp", stg[s][:, 0:cw], wdram[kc * 128:(kc + 1) * 128, c0:c0 + cw], writes=[Bs[s]])
                o_ap = wbf[:, kc, c0:c0 + cw]
                i_ap = stg[s][:, 0:cw]
                if gsb is None:
                    fn = lambda o_ap=o_ap, i_ap=i_ap: nc.scalar.copy(out=o_ap, in_=i_ap)
                else:
                    g_ap = gsb[:, kc:kc + 1]
                    fn = lambda o_ap=o_ap, i_ap=i_ap, g_ap=g_ap: nc.scalar.activation(out=o_ap, in_=i_ap, func=AF.Copy, scale=g_ap)
                tr.op("act", fn, reads=[Bs[s], Bg], writes=[])
                i += 1
                yield

    def load_weight(self, es_outer, wdram, K, N, gsrc, name):
        nc, tr = self.nc, self.tr
        nk = K // 128
        wbf = self.sbt(es_outer, name, [128, nk, N], BF16)
        Bw = Buf(name)
        CH = 2816 if N > 2816 else N
        with ExitStack() as es:
            stg = [self.sbt(es, "wstg", [128, CH], F32) for _ in range(3)]
            Bs = [Buf("wstg") for _ in range(3)]
            gsb = None
            if gsrc is not None:
                gsb = self.sbt(es, "gsb", [128, nk], F32)
                tr.dma("sp", gsb, gsrc, writes=[Bw])
            for _ in self.weight_steps(wbf, wdram, K, N, gsb, Bw, stg, Bs, CH):
                pass
            tr.barrier()
        return wbf, Bw

    def phase_inproj(self, l, x_src):
        nc, tr, S, I = self.nc, self.tr, self.S, self.I
        T_ = self.T
        NG = T_ // 512
        ps, psb = self.ps, self.psb
        with ExitStack() as es:
            wbf, Bw = self.load_weight(es, I["w_in"][l], DM, INW, I["g_pre"][l], "win")
            self.eps_ap = self.sbt(es, "eps", [128, 1], F32)
            Beps = Buf("eps")
            tr.op("pool", lambda: nc.gpsimd.memset(self.eps_ap, EPS), writes=[Beps])
            xg = [self.sbt(es, "xg", [128, 4, DM], F32) for _ in range(2)]
            Bxg = [Buf("xg") for _ in range(2)]
            xn = [self.sbt(es, "xn", [128, 4, DM], BF16) for _ in range(2)]
            Bxn = [Buf("xn") for _ in range(2)]
            xT = [self.sbt(es, "xT", [128, 8, 512], BF16) for _ in range(2)]
            BxT = [Buf("xT") for _ in range(2)]
            junk = [self.sbt(es, "junk", [128, DM], BF16) for _ in range(4)]
            Bjunk = [Buf("junk") for _ in range(4)]
            ss = [self.sbt(es, "ss", [128, 4], F32) for _ in range(2)]
            rs = [self.sbt(es, "rs", [128, 4], F32) for _ in range(2)]
            Bss = [Buf("ss") for _ in range(2)]
            Brs = [Buf("rs") for _ in range(2)]
            NST = 6
            stb = [self.sbt(es, "stb", [128, 512], BF16) for _ in range(NST)]
            stf = [self.sbt(es, "stf", [128, 512], F32) for _ in range(NST)]
            Bstb = [Buf("stb") for _ in range(NST)]
            Bstf = [Buf("stf") for _ in range(NST)]
            Bps = [Buf(f"ps{i}") for i in range(8)]
            BpsT = [Buf(f"psT{i}") for i in range(4)]
            cnt = {"stb": 0, "stf": 0, "ps": 0, "ev": 0, "pt": 0}

            def load(g):
                tr.dma("sp", xg[g % 2], x_src[g * 512:(g + 1) * 512, :].rearrange("(j p) d -> p j d", p=128), writes=[Bxg[g % 2]])

            def norm(g):
                s = g % 2
                for j in range(4):
                    tr.op("act", lambda j=j: nc.scalar.activation(out=junk[j], in_=xg[s][:, j, :], func=AF.Square, accum_out=ss[s][:, j:j + 1]),
                          reads=[Bxg[s]], writes=[Bjunk[j], Bss[s]])
                tr.op("act", lambda: nc.scalar.activation(out=rs[s], in_=ss[s], func=AF.Ln, scale=1.0 / DM, bias=self.eps_ap),
                      reads=[Bss[s], Beps], writes=[Brs[s]])
                tr.op("act", lambda: nc.scalar.activation(out=rs[s], in_=rs[s], func=AF.Exp, scale=-0.5), reads=[Brs[s]], writes=[Brs[s]])
                for j in range(4):
                    tr.op("act", lambda j=j: nc.scalar.activation(out=xn[s][:, j, :], in_=xg[s][:, j, :], func=AF.Copy, scale=rs[s][:, j:j + 1]),
                          reads=[Bxg[s], Brs[s]], writes=[Bxn[s]], conc=(j > 0))

            def transposes(g):
                s = g % 2
                for kp in range(4):
                    pt = cnt["pt"] % 2
                    cnt["pt"] += 1
                    pv = psb[pt]
                    for k2 in range(2):
                        kc = 2 * kp + k2
                        for j in range(4):
                            o_ap = pv[:, k2 * 512 + j * 128:k2 * 512 + (j + 1) * 128]
                            tr.op("pe", lambda j=j, kc=kc, o_ap=o_ap: nc.tensor.transpose(out=o_ap, in_=xn[s][:, j, kc * 128:(kc + 1) * 128], identity=self.ident),
                                  reads=[Bxn[s], self.B_const], writes=[BpsT[pt]], inc=(not F_INCT) or (k2 == 1 and j == 3))
                    src = pv.rearrange("p (k t) -> p k t", k=2)
                    if kp % 2 == 0:
                        tr.op("act", lambda kp=kp, src=src: nc.scalar.copy(out=xT[s][:, 2 * kp:2 * kp + 2, :], in_=src), reads=[BpsT[pt]], writes=[BxT[s]])
                    else:
                        tr.op("dve", lambda kp=kp, src=src: nc.vector.tensor_copy(out=xT[s][:, 2 * kp:2 * kp + 2, :], in_=src), reads=[BpsT[pt]], writes=[BxT[s]])

            def proj_tile(g, kind, lhs_fn, rhs_fn, M, N, dst, dt, scale=None):
                s = g % 2
                pi = 2 + cnt["ps"] % 6
                cnt["ps"] += 1
                pv = ps[pi][0:M, 0:N]
                for kc in range(8):
                    tr.op("pe", lambda kc=kc: nc.tensor.matmul(pv, lhsT=lhs_fn(kc), rhs=rhs_fn(kc), start=(kc == 0), stop=(kc == 7)),
                          reads=[BxT[s], Bw], writes=[Bps[pi]], inc=(kc == 7))
                if dt == BF16:
                    si = cnt["stb"] % NST
                    cnt["stb"] += 1
                    st, Bst = stb[si], Bstb[si]
                else:
                    si = cnt["stf"] % NST
                    cnt["stf"] += 1
                    st, Bst = stf[si], Bstf[si]
                sv = st[0:M, 0:N]
                ev = cnt["ev"] % 2
                cnt["ev"] += 1
                if ev == 0:
                    if scale is None:
                        tr.op("act", lambda: nc.scalar.copy(out=sv, in_=pv), reads=[Bps[pi]], writes=[Bst])
                    else:
                        tr.op("act", lambda: nc.scalar.mul(out=sv, in_=pv, mul=scale), reads=[Bps[pi]], writes=[Bst])
                else:
                    if scale is None:
                        tr.op("dve", lambda: nc.vector.tensor_copy(out=sv, in_=pv), reads=[Bps[pi]], writes=[Bst])
                    else:
                        tr.op("dve", lambda: nc.vector.tensor_scalar_mul(out=sv, in0=pv, scalar1=scale), reads=[Bps[pi]], writes=[Bst])
                return sv, Bst

            def feat_major(g):
                s = g % 2
                t0 = g * 512
                tiles = []
                for i in range(4):
                    tiles.append((i * 128, 128, S["qT"][i * 128:(i + 1) * 128, t0:t0 + 512], BF16, 0.125))
                for i in range(4):
                    tiles.append((512 + i * 128, 128, S["kT"][i * 128:(i + 1) * 128, PAD + t0:PAD + t0 + 512], BF16, None))
                c = 0
                for j in range(3):
                    tiles.append((1536 + c, GT[j], S["gqT"][j, 0:GT[j], t0:t0 + 512], F32, None))
                    tiles.append((1792 + c, GT[j], S["gkT"][j, 0:GT[j], t0:t0 + 512], F32, None))
                    c += GT[j]
                for (c0, M, dst, dt, sc) in tiles:
                    sv, Bst = proj_tile(g, "f", lambda kc, c0=c0, M=M: wbf[:, kc, c0:c0 + M], lambda kc: xT[s][:, kc, :], M, 512, None, dt, sc)
                    tr.dma("sp", dst, sv, reads=[Bst])
                pi = 2 + cnt["ps"] % 6
                cnt["ps"] += 1
                pv = ps[pi][0:16, 0:512]
                for kc in range(8):
                    tr.op("pe", lambda kc=kc: nc.tensor.matmul(pv, lhsT=wbf[:, kc, 3072:3088], rhs=xT[s][:, kc, :], start=(kc == 0), stop=(kc == 7)),
                          reads=[BxT[s], Bw], writes=[Bps[pi]], inc=(kc == 7))
                s1 = cnt["stb"] % NST
                s2_ = (cnt["stb"] + 1) % NST
                cnt["stb"] += 2
                tr.op("act", lambda: nc.scalar.copy(out=stb[s1][0:16, :], in_=pv), reads=[Bps[pi]], writes=[Bstb[s1]])
                tr.op("dve", lambda: nc.vector.tensor_tensor(out=stb[s2_][0:16, :], in0=pv, in1=stb[s1][0:16, :], op=ALU.subtract),
                      reads=[Bps[pi], Bstb[s1]], writes=[Bstb[s2_]])
                tr.dma("sp", S["lrh"][:, t0:t0 + 512], stb[s1][0:16, :], reads=[Bstb[s1]])
                tr.dma("sp", S["lrl"][:, t0:t0 + 512], stb[s2_][0:16, :], reads=[Bstb[s2_]])

            def tok_major(g):
                s = g % 2
                for j in range(4):
                    r0 = g * 512 + j * 128
                    specs = [
                        (1024, 512, S["v"][:, PAD + r0:PAD + r0 + 128, :].rearrange("h t f -> t h f"), BF16, "v"),
                        (1792, 256, S["gk"][r0:r0 + 128, :], F32, ""),
                        (2048, 512, S["gv"][r0:r0 + 128, :], BF16, ""),
                        (2560, 512, S["gr"][r0:r0 + 128, :], F32, ""),
                    ]
                    for (c0, N, dst, dt, kind) in specs:
                        sv, Bst = proj_tile(g, "t", lambda kc, j=j: xT[s][:, kc, j * 128:(j + 1) * 128],
                                            lambda kc, c0=c0, N=N: wbf[:, kc, c0:c0 + N], 128, N, None, dt, None)
                        src = sv.rearrange("t (h f) -> t h f", h=4) if kind == "v" else sv
                        tr.dma("sp", dst, src, reads=[Bst])

            load(0)
            if NG > 1:
                load(1)
            norm(0)
            transposes(0)
            for g in range(NG):
                feat_major(g)
                if g + 1 < NG:
                    norm(g + 1)
                    transposes(g + 1)
                tok_major(g)
                if g + 2 < NG:
                    load(g + 2)

    def phase_attn(self, l):
        nc, tr, S, I = self.nc, self.tr, self.S, self.I
        ns = self.nseg
        ps = self.ps
        with ExitStack() as es:
            Bc = Buf("attc")
            ntab = 1 + 3 * ns
            tabs = self.sbt(es, "tabs", [128, ntab, 256], F32)
            tr.dma("sp", tabs, I["atab"].rearrange("p (n c) -> p n c", c=256), writes=[Bc])
            xd = [self.sbt(es, "xd", [128, 2, 1024], F32) for _ in range(3)]
            Bxd = [Buf("xd") for _ in range(3)]
            for i in range(3):
                tr.op("pool", lambda i=i: nc.gpsimd.memset(xd[i][0:64], 0.0), writes=[Bxd[i]])
            QT = [self.sbt(es, "QT", [128, SEG], BF16) for _ in range(2)]
            KT = [self.sbt(es, "KT", [128, SEG + 2 * PAD], BF16) for _ in range(2)]
            BQK = [Buf("QK") for _ in range(2)]
            NVBIG, NVS, LA = 2, 8, 5
            Vbig = [self.sbt(es, "Vbig", [128, 17, 2, 128], BF16) for _ in range(NVBIG)]
            Vsml = [self.sbt(es, "Vsml", [128, 5, 2, 128], BF16) for _ in range(NVS)]
            BVbig = [Buf("Vbig") for _ in range(NVBIG)]
            BVsml = [Buf("Vsml") for _ in range(NVS)]
            for i in range(NVBIG):
                tr.op("pool", lambda i=i: nc.gpsimd.memset(Vbig[i], 1.0), writes=[BVbig[i]])
            for i in range(NVS):
                tr.op("pool", lambda i=i: nc.gpsimd.memset(Vsml[i], 1.0), writes=[BVsml[i]])
            acc = [self.sbt(es, "acc", [128, 2, SEG], F32) for _ in range(2)]
            Bacc = [Buf("acc") for _ in range(2)]
            rb = self.sbt(es, "rb", [64, 2, SEG], F32)
            Brb = [Buf("rb0"), Buf("rb1")]
            ob = [self.sbt(es, "ob", [64, 2, SEG], BF16) for _ in range(2)]
            Bob = [Buf("ob") for _ in range(2)]
            NS3 = 3
            tmp = [self.sbt(es, "tmp", [128, 2, 256], F32) for _ in range(NS3)]
            PT = [self.sbt(es, "PT", [128, 2, 256], BF16) for _ in range(NS3)]
            Btmp = [Buf("tmp") for _ in range(NS3)]
            BPT = [Buf("PT") for _ in range(NS3)]
            BpsS = [Buf("psS") for _ in range(2)]
            BpsO = [Buf("psO") for _ in range(4)]
            osb = [self.sbt(es, "osb", [128, 2, 128], F32) for _ in range(4)]
            Bosb = [Buf("osb") for _ in range(4)]

            units = [(s, hp) for s in range(ns) for hp in range(4)]
            groups = []
            for ui, (s, hp) in enumerate(units):
                for (d, nres, ntile) in BRANCHES:
                    for r in range(nres):
                        groups.append((ui, d, r, ntile))
            gslot = []
            small_ids = []
            for gi, (ui, d, r, ntile) in enumerate(groups):
                if d == 1:
                    gslot.append((Vbig[ui % NVBIG], BVbig[ui % NVBIG]))
                else:
                    si = len(small_ids)
                    small_ids.append(gi)
                    gslot.append((Vsml[si % NVS], BVsml[si % NVS]))
            small_pos = {gi: si for si, gi in enumerate(small_ids)}
            items = []
            for gi, (ui, d, r, ntile) in enumerate(groups):
                for t in range(ntile):
                    items.append((gi, t))

            def load_unit(ui):
                s, hp = units[ui]
                b = ui % 2
                tr.dma("sp", QT[b], S["qT"][hp * 128:(hp + 1) * 128, s * SEG:(s + 1) * SEG], writes=[BQK[b]], conc=True)
                tr.dma("sp", KT[b], S["kT"][hp * 128:(hp + 1) * 128, s * SEG:s * SEG + SEG + 2 * PAD], writes=[BQK[b]], conc=True)
                tr.dma("sp", xd[ui % 3][64:128], I["xden"][s, 2 * hp:2 * hp + 2, :].partition_broadcast(64), writes=[Bxd[ui % 3]], conc=True)

            def load_group(gi):
                ui, d, r, ntile = groups[gi]
                s, hp = units[ui]
                Vt, BVt = gslot[gi]
                nch = ntile + 1
                R0 = PAD + s * SEG + r - 64 * d
                c0 = 0
                while c0 < nch:
                    n = min(6, nch - c0)
                    base = R0 + d * 128 * c0
                    for e in range(2):
                        src = S["v"][hp, sl_(base, 128 * n, d), e * 64:(e + 1) * 64].rearrange("(c i) f -> i c f", i=128)
                        tr.dma("sp", Vt[:, c0:c0 + n, e, 0:64], src, writes=[BVt], conc=True)
                    c0 += n

            def tab_index(s, d, t, ntile):
                if ntile == 1:
                    return 1 + 2 * ns + s
                if t == 0:
                    return 1 + s
                if t == ntile - 1:
                    return 1 + ns + s
                return 0

            def stageA(ii):
                gi, t = items[ii]
                ui, d, r, ntile = groups[gi]
                b = ui % 2
                sl = ii % 3
                sp_ = ii % 2
                q0 = r + d * 128 * t
                for e in range(2):
                    qs = QT[b][e * 64:(e + 1) * 64, sl_(q0, 128, d)]
                    for c in range(2):
                        k0 = PAD + r + d * (-64 + 128 * (t + c))
                        ks = KT[b][e * 64:(e + 1) * 64, sl_(k0, 128, d)]
                        out = ps[2 * sp_ + e][:, c * 128:(c + 1) * 128]
                        tr.op("pe", lambda out=out, ks=ks, qs=qs: nc.tensor.matmul(out, lhsT=ks, rhs=qs, start=True, stop=True),
                              reads=[BQK[b]], writes=[BpsS[sp_]], inc=(not F_INCA) or (e == 1 and c == 1))

            def stageB(ii):
                gi, t = items[ii]
                ui, d, r, ntile = groups[gi]
                s, hp = units[ui]
                sl = ii % 3
                sp_ = ii % 2
                ti = tab_index(s, d, t, ntile)
                for e in range(2):
                    h = hp * 2 + e
                    cneg = -SLOPES[h] * d
                    tr.op("dve", lambda e=e, cneg=cneg: nc.vector.scalar_tensor_tensor(out=tmp[sl][:, e, :], in0=tabs[:, ti, :], scalar=cneg,
                                                                                       in1=ps[2 * sp_ + e][:, 0:256], op0=ALU.mult, op1=ALU.add),
                          reads=[Bc, BpsS[sp_]], writes=[Btmp[sl]], conc=(e == 1))
                tr.op("act", lambda: nc.scalar.activation(out=PT[sl], in_=tmp[sl], func=AF.Exp), reads=[Btmp[sl]], writes=[BPT[sl]])

            def stageC(ii):
                gi, t = items[ii]
                ui, d, r, ntile = groups[gi]
                b = ui % 2
                sl = ii % 3
                so = ii % 4
                Vt, BVt = gslot[gi]
                po = ps[4 + so][:, 0:256]
                for e in range(2):
                    for c in range(2):
                        tr.op("pe", lambda e=e, c=c: nc.tensor.matmul(po[:, e * 128:(e + 1) * 128], lhsT=Vt[:, t + c, e, :],
                                                                       rhs=PT[sl][:, e, c * 128:(c + 1) * 128], start=(c == 0), stop=(c == 1)),
                              reads=[BVt, BPT[sl]], writes=[BpsO[so]], inc=(not F_INCA) or (e == 1 and c == 1))
                q0 = r + d * 128 * t
                dst = acc[b][:, :, sl_(q0, 128, d)]
                src = po.rearrange("p (e q) -> p e q", e=2)
                if not F_ACC:
                    if d == 1:
                        tr.op("dve", lambda: nc.vector.tensor_copy(out=dst, in_=src), reads=[BpsO[so]], writes=[Bacc[b]])
                    else:
                        tr.op("dve", lambda: nc.vector.tensor_tensor(out=dst, in0=dst, in1=src, op=ALU.add), reads=[BpsO[so], Bacc[b]], writes=[Bacc[b]])
                elif d == 1 and q0 >= 1024:
                    xs = xd[ui % 3][:, :, q0 - 1024:q0 - 1024 + 128]
                    tr.op("dve", lambda: nc.vector.tensor_tensor(out=dst, in0=src, in1=xs, op=ALU.add), reads=[BpsO[so], Bxd[ui % 3]], writes=[Bacc[b]])
                elif d == 1:
                    tr.op("act", lambda: nc.scalar.copy(out=dst, in_=src), reads=[BpsO[so]], writes=[Bacc[b]])
                else:
                    ot = osb[so]
                    tr.op("act", lambda: nc.scalar.copy(out=ot, in_=src), reads=[BpsO[so]], writes=[Bosb[so]])
                    tr.op("pool", lambda: nc.gpsimd.tensor_tensor(out=dst, in0=dst, in1=ot, op=ALU.add), reads=[Bosb[so], Bacc[b]], writes=[Bacc[b]])

            def finalize(ui):
                s, hp = units[ui]
                b = ui % 2
                for e in range(2):
                    tr.op("act", lambda e=e: nc.scalar.activation(out=rb[:, e, :], in_=acc[b][64:128, e, :], func=AF.Ln), reads=[Bacc[b]], writes=[Brb[e]])
                    tr.op("act", lambda e=e: nc.scalar.activation(out=rb[:, e, :], in_=rb[:, e, :], func=AF.Exp, scale=-1.0), reads=[Brb[e]], writes=[Brb[e]])
                for e in range(2):
                    tr.op("pool", lambda e=e: nc.gpsimd.tensor_tensor(out=ob[b][:, e, :], in0=acc[b][0:64, e, :], in1=rb[:, e, :], op=ALU.mult),
                          reads=[Bacc[b], Brb[e]], writes=[Bob[b]], conc=(e == 1))
                    h = hp * 2 + e
                    tr.dma("sp", S["mixT"][h * 64:(h + 1) * 64, s * SEG:(s + 1) * SEG], ob[b][:, e, :], reads=[Bob[b]])

            NI = len(items)
            first_item_of_group = {}
            for ii, (gi, t) in enumerate(items):
                first_item_of_group.setdefault(gi, ii)
            load_unit(0)
            load_group(0)
            for si in range(min(LA, len(small_ids))):
                load_group(small_ids[si])
            last_item_of_unit = {}
            for ii, (gi, t) in enumerate(items):
                last_item_of_unit[groups[gi][0]] = ii

            def pre(ii):
                gi, t = items[ii]
                if first_item_of_group[gi] != ii:
                    return
                ui, d = groups[gi][0], groups[gi][1]
                if d == 1:
                    if ui + 1 < len(units):
                        load_unit(ui + 1)
                else:
                    si = small_pos[gi]
                    if si + LA < len(small_ids):
                        load_group(small_ids[si + LA])
                    if groups[gi - 1][1] == 1 and ui + 1 < len(units):
                        nxt = gi - 1 + sum(n for (_, n, _) in BRANCHES)
                        load_group(nxt)

            for step in range(NI + 2):
                if step < NI:
                    pre(step)
                    stageA(step)
                if 0 <= step - 1 < NI:
                    stageB(step - 1)
                if 0 <= step - 2 < NI:
                    stageC(step - 2)
                    ui = groups[items[step - 2][0]][0]
                    if last_item_of_unit[ui] == step - 2:
                        finalize(ui)

    def phase_gla(self, l, bg=None):
        nc, tr, S, I = self.nc, self.tr, self.S, self.I
        T_ = self.T
        NT = T_ // 128
        ps, psb = self.ps, self.psb
        TPS = SEG // 128
        with ExitStack() as es:
            cst = self.sbt(es, "gcst", [128, 768], F32)
            Bc = Buf("gcst")
            tr.dma("sp", cst, I["consts"][:, C_LTF:C_LTF + 768], writes=[Bc])
            Lb = self.sbt(es, "Lb", [128, 512], BF16)
            tr.op("dve", lambda: nc.vector.tensor_copy(out=Lb, in_=cst[:, 0:512]), reads=[Bc], writes=[Bc])
            LtF, LuF, LtB, LlB = Lb[:, 0:128], Lb[:, 128:256], Lb[:, 256:384], Lb[:, 384:512]
            MF, MB = cst[:, 512:640], cst[:, 640:768]
            wg = [self.sbt(es, "wg", [17, 256], F32) for _ in range(2)]
            wgh = [self.sbt(es, "wgh", [17, 256], BF16) for _ in range(2)]
            wgl = [self.sbt(es, "wgl", [17, 256], BF16) for _ in range(2)]
            tr.dma("sp", wg[0], I["wg_f"][l], writes=[Bc], conc=True)
            tr.dma("sp", wg[1], I["wg_b"][l], writes=[Bc], conc=True)
            for d_ in range(2):
                tr.op("dve", lambda d_=d_: nc.vector.tensor_copy(out=wgh[d_], in_=wg[d_]), reads=[Bc], writes=[Bc])
                tr.op("dve", lambda d_=d_: nc.vector.tensor_tensor(out=wgl[d_], in0=wg[d_], in1=wgh[d_], op=ALU.subtract), reads=[Bc], writes=[Bc])
            gn = self.sbt(es, "gn", [128, 512], F32)
            tr.dma("sp", gn, I["gnorm"][l], writes=[Bc], conc=True)
            eps = self.sbt(es, "eps", [128, 1], F32)
            tr.op("pool", lambda: nc.gpsimd.memset(eps, EPS), writes=[Bc], conc=True)

            NL = 5
            lrh = [self.sbt(es, "lrh", [17, 128], BF16) for _ in range(NL)]
            lrl = [self.sbt(es, "lrl", [17, 128], BF16) for _ in range(NL)]
            qT3 = [self.sbt(es, "qT3", [96, 3, 128], F32) for _ in range(NL)]
            kT3 = [self.sbt(es, "kT3", [96, 3, 128], F32) for _ in range(NL)]
            ktk = [self.sbt(es, "ktk", [128, 256], F32) for _ in range(NL)]
            vtk = [self.sbt(es, "vtk", [128, 512], BF16) for _ in range(NL)]
            rtk = [self.sbt(es, "rtk", [128, 512], F32) for _ in range(NL)]
            oft = [self.sbt(es, "oft", [128, 512], F32) for _ in range(NL)]
            Bld = [Buf("gld") for _ in range(NL)]
            Bld2 = [Buf("gld2") for _ in range(NL)]
            for i in range(NL):
                tr.op("pool", lambda i=i: nc.gpsimd.memset(lrh[i][0:1, :], 1.0), writes=[Bld[i]], conc=True)
                tr.op("pool", lambda i=i: nc.gpsimd.memset(lrl[i][0:1, :], 0.0), writes=[Bld[i]], conc=True)
            e1 = [self.sbt(es, "e1", [128, 256], F32) for _ in range(2)]
            lsp = [self.sbt(es, "lsp", [128, 256], F32) for _ in range(2)]
            Bg1 = [Buf("g1") for _ in range(2)]
            lsh = [self.sbt(es, "lsh", [128, 288], BF16) for _ in range(2)]
            lsl = [self.sbt(es, "lsl", [128, 288], BF16) for _ in range(2)]
            Bls = [Buf("ls") for _ in range(2)]
            for i in range(2):
                tr.op("pool", lambda i=i: nc.gpsimd.memset(lsh[i][:, 256:288], 0.0), writes=[Bls[i]], conc=True)
                tr.op("pool", lambda i=i: nc.gpsimd.memset(lsl[i][:, 256:288], 0.0), writes=[Bls[i]], conc=True)
            E2T = [self.sbt(es, "E2T", [96, 3, 128], F32) for _ in range(2)]
            E3 = [self.sbt(es, "E3", [128, 256], F32) for _ in range(2)]
            Bg2 = [Buf("g2") for _ in range(2)]
            E1T = [self.sbt(es, "E1T", [96, 3, 128], F32) for _ in range(3)]
            qin = [self.sbt(es, "qin", [96, 3, 128], BF16) for _ in range(3)]
            kin = [self.sbt(es, "kin", [96, 3, 128], BF16) for _ in range(3)]
            kst = [self.sbt(es, "kst", [128, 256], BF16) for _ in range(3)]
            Bgo = [Buf("gGo") for _ in range(3)]
            attT = [self.sbt(es, "attT", [128, 8, 128], BF16) for _ in range(2)]
            Batt = [Buf("attT") for _ in range(2)]
            Sst = self.sbt(es, "Sst", [96, 3, 64], F32)
            Sbf = [self.sbt(es, "Sbd", [96, 3, 192], BF16) for _ in range(2)]
            BS = Buf("S")
            BSb = [Buf("Sbf") for _ in range(2)]
            osb = [self.sbt(es, "osb", [128, 512], F32) for _ in range(3)]
            Bosb = [Buf("osb") for _ in range(3)]
            sq = [self.sbt(es, "sq", [128, 512], F32) for _ in range(2)]
            ssg = [self.sbt(es, "ssg", [128, 8], F32) for _ in range(2)]
            rsg = [self.sbt(es, "rsg", [128, 8], F32) for _ in range(2)]
            sil = [self.sbt(es, "sil", [128, 512], F32) for _ in range(2)]
            ogb = [self.sbt(es, "ogb", [128, 512], BF16) for _ in range(2)]
            ogT = [self.sbt(es, "ogT", [128, 4, 128], BF16) for _ in range(2)]
            BogT = [Buf("ogT") for _ in range(2)]
            Bfin = [Buf("gfin") for _ in range(2)]
            Bpz, BpbT, Bpb3, BpA, BpO, BpU = (Buf("pz"), Buf("pbT"), Buf("pb3"), Buf("pA"), Buf("pO"), Buf("pU"))
            BpT = BpU

            def loads(idx, t, bwd):
                sl = idx % NL
                tk = slice(t * 128, (t + 1) * 128)
                tr.dma("sp", lrh[sl][1:17, :], S["lrh"][:, tk], writes=[Bld[sl]], conc=True)
                tr.dma("sp", lrl[sl][1:17, :], S["lrl"][:, tk], writes=[Bld[sl]], conc=True)
                tr.dma("sp", qT3[sl], S["gqT"][:, :, tk].rearrange("j p t -> p j t"), writes=[Bld[sl]], conc=True)
                tr.dma("sp", kT3[sl], S["gkT"][:, :, tk].rearrange("j p t -> p j t"), writes=[Bld[sl]], conc=True)
                tr.dma("sp", ktk[sl], S["gk"][tk, :], writes=[Bld[sl]], conc=True)
                tr.dma("sp", vtk[sl], S["gv"][tk, :], writes=[Bld[sl]], conc=True)
                if bwd:
                    tr.dma("sp", rtk[sl], S["gr"][tk, :], writes=[Bld2[sl]], conc=True)
                    tr.dma("sp", oft[sl], S["of"][tk, :], writes=[Bld2[sl]], conc=True)

            def G1(idx, t, bwd):
                sl, s2 = idx % NL, idx % 2
                d_ = 1 if bwd else 0
                zps = ps[0][:, 0:256]
                tr.op("pe", lambda: nc.tensor.matmul(zps, lhsT=lrh[sl], rhs=wgh[d_], start=True, stop=False), reads=[Bld[sl], Bc], writes=[Bpz], inc=False)
                tr.op("pe", lambda: nc.tensor.matmul(zps, lhsT=lrh[sl], rhs=wgl[d_], start=False, stop=False), reads=[Bld[sl], Bc], writes=[Bpz], inc=False)
                tr.op("pe", lambda: nc.tensor.matmul(zps, lhsT=lrl[sl], rhs=wgh[d_], start=False, stop=True), reads=[Bld[sl], Bc], writes=[Bpz])
                tr.op("act", lambda: nc.scalar.activation(out=e1[s2], in_=zps, func=AF.Exp, scale=-1.0), reads=[Bpz], writes=[Bg1[s2]])
                tr.op("act", lambda: nc.scalar.activation(out=lsp[s2], in_=e1[s2], func=AF.Ln, bias=1.0), reads=[Bg1[s2]], writes=[Bg1[s2]])
                tr.op("act", lambda: nc.scalar.copy(out=lsh[s2][:, 0:256], in_=lsp[s2]), reads=[Bg1[s2]], writes=[Bls[s2]])
                tr.op("dve", lambda: nc.vector.tensor_tensor(out=lsl[s2][:, 0:256], in0=lsp[s2], in1=lsh[s2][:, 0:256], op=ALU.subtract),
                      reads=[Bg1[s2], Bls[s2]], writes=[Bls[s2]], conc=True)

            def G1b(idx, t, bwd):
                pass

            def G2(idx, t, bwd):
                sl, s2, s3 = idx % NL, idx % 2, idx % 3
                Ltri = LtB if bwd else LtF
                Lrest = LlB if bwd else LuF
                bTps = ps[1][0:96, 0:384].rearrange("p (j t) -> p j t", j=3)
                b3ps = ps[7][:, 0:256]
                for j in range(3):
                    c = 96 * j
                    o_ap = ps[1][0:96, j * 128:(j + 1) * 128]
                    tr.op("pe", lambda c=c, o_ap=o_ap: nc.tensor.matmul(o_ap, lhsT=lsh[s2][:, c:c + 96], rhs=Ltri, start=True, stop=False),
                          reads=[Bls[s2], Bc], writes=[BpbT], inc=False)
                    tr.op("pe", lambda c=c, o_ap=o_ap: nc.tensor.matmul(o_ap, lhsT=lsl[s2][:, c:c + 96], rhs=Ltri, start=False, stop=True),
                          reads=[Bls[s2], Bc], writes=[BpbT], inc=(j == 2))
                tr.op("pe", lambda: nc.tensor.matmul(b3ps, lhsT=Lrest, rhs=lsh[s2][:, 0:256], start=True, stop=False), reads=[Bls[s2], Bc], writes=[Bpb3], inc=False)
                tr.op("pe", lambda: nc.tensor.matmul(b3ps, lhsT=Lrest, rhs=lsl[s2][:, 0:256], start=False, stop=True), reads=[Bls[s2], Bc], writes=[Bpb3])
                tr.op("act", lambda: nc.scalar.activation(out=E1T[s3], in_=bTps, func=AF.Exp), reads=[BpbT], writes=[Bgo[s3]])
                tr.op("act", lambda: nc.scalar.activation(out=E2T[s2], in_=bTps, func=AF.Exp, scale=-1.0), reads=[BpbT], writes=[Bg2[s2]])
                tr.op("act", lambda: nc.scalar.activation(out=E3[s2], in_=b3ps, func=AF.Exp), reads=[Bpb3], writes=[Bg2[s2]], conc=True)
                tr.op("dve", lambda: nc.vector.scalar_tensor_tensor(out=qin[s3], in0=qT3[sl], scalar=32.0 ** -0.5, in1=E1T[s3], op0=ALU.mult, op1=ALU.mult),
                      reads=[Bld[sl], Bgo[s3]], writes=[Bgo[s3]], conc=True)
                tr.op("pool", lambda: nc.gpsimd.tensor_tensor(out=kin[s3], in0=kT3[sl], in1=E2T[s2], op=ALU.mult), reads=[Bld[sl], Bg2[s2]], writes=[Bgo[s3]], conc=True)
                tr.op("dve", lambda: nc.vector.tensor_tensor(out=kst[s3], in0=ktk[sl], in1=E3[s2], op=ALU.mult), reads=[Bld[sl], Bg2[s2]], writes=[Bgo[s3]], conc=True)

            def H1(idx, t, bwd):
                s3, a = idx % 3, idx % 2
                M = MB if bwd else MF
                abank = (2, 3, 6)
                for h in range(8):
                    j, rt = h // 3, h % 3
                    p0 = 32 * rt
                    out = ps[abank[rt]][:, j * 128:(j + 1) * 128]
                    tr.op("pe", lambda out=out, j=j, p0=p0: nc.tensor.matmul(out, lhsT=kin[s3][p0:p0 + 32, j, :], rhs=qin[s3][p0:p0 + 32, j, :], start=True, stop=True),
                          reads=[Bgo[s3]], writes=[BpA], inc=(h == 7))
                for rt in range(3):
                    nh = 3 if rt < 2 else 2
                    src = ps[abank[rt]][:, 0:nh * 128].rearrange("p (h q) -> p h q", h=nh)
                    mk = M.unsqueeze(1).to_broadcast([128, nh, 128])
                    dst = attT[a][:, sl_(rt, nh, 3), :]
                    tr.op("dve", lambda src=src, mk=mk, dst=dst: nc.vector.tensor_tensor(out=dst, in0=src, in1=mk, op=ALU.mult),
                          reads=[BpA, Bc], writes=[Batt[a]], conc=True)

            def H2(idx, t, bwd):
                sl, s3, a = idx % NL, idx % 3, idx % 2
                sb_r, sb_w = Sbf[idx % 2], Sbf[(idx + 1) % 2]
                Bsb_r, Bsb_w = BSb[idx % 2], BSb[(idx + 1) % 2]
                seg = t // TPS
                if bwd:
                    entering = (t % TPS == TPS - 1) and t != NT - 1
                    fl = self.flg[0:96, seg + 1:seg + 2]
                else:
                    entering = (t % TPS == 0) and t != 0
                    fl = self.flg[0:96, seg:seg + 1]
                def cast_state(dst, Bdst):
                    for h3 in range(3):
                        p0 = 32 * h3
                        tr.op("pool", lambda p0=p0, h3=h3: nc.gpsimd.tensor_copy(out=dst[p0:p0 + 32, :, 64 * h3:64 * h3 + 64], in_=Sst[p0:p0 + 32, :, :]),
                              reads=[BS], writes=[Bdst], conc=(h3 > 0))

                if entering:
                    tr.op("dve", lambda: nc.vector.tensor_scalar_mul(out=Sst, in0=Sst, scalar1=fl), reads=[BS, self.B_const], writes=[BS])
                    cast_state(sb_r, Bsb_r)
                for h in range(8):
                    j, p0 = h // 3, 32 * (h % 3)
                    out = ps[5][p0:p0 + 32, j * 64:(j + 1) * 64]
                    tr.op("pe", lambda out=out, h=h: nc.tensor.matmul(out, lhsT=kst[s3][:, h * 32:(h + 1) * 32], rhs=vtk[sl][:, h * 64:(h + 1) * 64], start=True, stop=True),
                          reads=[Bgo[s3], Bld[sl]], writes=[BpU], inc=(h == 7))
                for j in range(3):
                    nh = 3 if j < 2 else 2
                    outj = ps[4][:, j * 192:j * 192 + nh * 64]
                    tr.op("pe", lambda j=j, nh=nh, outj=outj: nc.tensor.matmul(outj, lhsT=qin[s3][0:GT[j], j, :], rhs=sb_r[0:GT[j], j, 0:nh * 64], start=True, stop=False),
                          reads=[Bgo[s3], Bsb_r], writes=[BpO], inc=False)
                    for hh in range(nh):
                        h = 3 * j + hh
                        out = ps[4][:, h * 64:(h + 1) * 64]
                        tr.op("pe", lambda out=out, h=h: nc.tensor.matmul(out, lhsT=attT[a][:, h, :], rhs=vtk[sl][:, h * 64:(h + 1) * 64], start=False, stop=(hh == nh - 1)),
                              reads=[Batt[a], Bld[sl]], writes=[BpO], inc=(h == 7))
                col = 0 if bwd else 127
                for j in range(3):
                    Mj = GT[j]
                    tr.op("dve", lambda j=j, Mj=Mj: nc.vector.scalar_tensor_tensor(out=Sst[0:Mj, j, :], in0=Sst[0:Mj, j, :], scalar=E1T[s3][0:Mj, j, col:col + 1],
                                                                                   in1=ps[5][0:Mj, j * 64:(j + 1) * 64], op0=ALU.mult, op1=ALU.add),
                          reads=[BS, Bgo[s3], BpU], writes=[BS])
                cast_state(sb_w, Bsb_w)
                tk = slice(t * 128, (t + 1) * 128)
                oa = idx % 3
                if not bwd:
                    tr.op("act", lambda: nc.scalar.copy(out=osb[oa], in_=ps[4]), reads=[BpO], writes=[Bosb[oa]])
                    tr.dma("sp", S["of"][tk, :], osb[oa], reads=[Bosb[oa]])
                else:
                    o = osb[oa]
                    o3 = o.rearrange("p (h d) -> p h d", h=8)
                    Bf = Bfin[a]
                    tr.op("dve", lambda: nc.vector.tensor_tensor(out=o, in0=oft[sl], in1=ps[4], op=ALU.add), reads=[BpO, Bld2[sl]], writes=[Bosb[oa]])
                    tr.op("pool", lambda: nc.gpsimd.tensor_tensor(out=sq[a], in0=o, in1=o, op=ALU.mult), reads=[Bosb[oa]], writes=[Bf])
                    tr.op("dve", lambda: nc.vector.tensor_reduce(out=ssg[a], in_=sq[a].rearrange("p (h d) -> p h d", h=8), axis=AX.X, op=ALU.add), reads=[Bf], writes=[Bf])
                    tr.op("act", lambda: nc.scalar.activation(out=rsg[a], in_=ssg[a], func=AF.Ln, scale=1.0 / 64, bias=eps), reads=[Bf, Bc], writes=[Bf])
                    tr.op("act", lambda: nc.scalar.activation(out=rsg[a], in_=rsg[a], func=AF.Exp, scale=-0.5), reads=[Bf], writes=[Bf])
                    tr.op("act", lambda: nc.scalar.activation(out=sil[a], in_=rtk[sl], func=AF.Exp, scale=-1.0), reads=[Bld2[sl], Bf], writes=[Bf])
                    tr.op("act", lambda: nc.scalar.activation(out=sil[a], in_=sil[a], func=AF.Ln, bias=1.0), reads=[Bf], writes=[Bf])
                    tr.op("act", lambda: nc.scalar.activation(out=sil[a], in_=sil[a], func=AF.Exp, scale=-1.0), reads=[Bf], writes=[Bf])
                    tr.op("pool", lambda: nc.gpsimd.tensor_tensor(out=sil[a], in0=sil[a], in1=rtk[sl], op=ALU.mult), reads=[Bf, Bld2[sl]], writes=[Bf])
                    tr.op("pool", lambda: nc.gpsimd.tensor_tensor(out=sil[a], in0=sil[a], in1=gn, op=ALU.mult), reads=[Bf, Bc], writes=[Bf])
                    tr.op("dve", lambda: nc.vector.tensor_tensor(out=o3, in0=o3, in1=rsg[a].unsqueeze(2).to_broadcast([128, 8, 64]), op=ALU.mult), reads=[Bf, Bosb[oa]], writes=[Bosb[oa]])
                    tr.op("pool", lambda: nc.gpsimd.tensor_tensor(out=ogb[a], in0=o, in1=sil[a], op=ALU.mult), reads=[Bosb[oa], Bf], writes=[Bf])
                    pt = psb[5][:, 512:1024]
                    for c in range(4):
                        tr.op("pe", lambda c=c: nc.tensor.transpose(out=pt[:, c * 128:(c + 1) * 128], in_=ogb[a][:, c * 128:(c + 1) * 128], identity=self.ident),
                              reads=[Bf, self.B_const], writes=[BpT], inc=(c == 3))
                    tr.op("act", lambda: nc.scalar.copy(out=ogT[a], in_=pt.rearrange("p (c t) -> p c t", c=4)), reads=[BpT], writes=[BogT[a]])
                    tr.dma("sp", S["mixT"][512:1024, tk].rearrange("(c p) t -> p c t", p=128), ogT[a], reads=[BogT[a]])

            for bwd in (False, True):
                order = list(range(NT))
                if bwd:
                    order = order[::-1]
                    tr.barrier()
                tr.op("dve", lambda: nc.vector.memset(Sst, 0.0), reads=[BS], writes=[BS])
                tr.op("pool", lambda: nc.gpsimd.memset(Sbf[0], 0.0), reads=[BSb[0]], writes=[BSb[0]])
                tr.op("pool", lambda: nc.gpsimd.memset(Sbf[1], 0.0), reads=[BSb[1]], writes=[BSb[1]])
                for i in range(min(NL, NT)):
                    loads(i, order[i], bwd)
                for it in range(-3, NT):
                    if bg is not None:
                        next(bg, None)
                    if 0 <= it + 3 < NT:
                        G1(it + 3, order[it + 3], bwd)
                    if 0 <= it + 2 < NT:
                        G2(it + 2, order[it + 2], bwd)
                    if 0 <= it < NT:
                        H2(it, order[it], bwd)
                    if 0 <= it + 1 < NT:
                        H1(it + 1, order[it + 1], bwd)
                    if 0 <= it + 3 < NT:
                        G1b(it + 3, order[it + 3], bwd)
                    if it >= 0 and it + NL < NT:
                        loads(it + NL, order[it + NL], bwd)

    def phase_outproj(self, l, x_src):
        nc, tr, S, I = self.nc, self.tr, self.S, self.I
        T_ = self.T
        NG = T_ // 512
        NTL = T_ // 128
        ps, psb = self.ps, self.psb
        with ExitStack() as es:
            wbf, Bw = self.load_weight(es, I["w_out"][l], DM, DM, None, "wout")
            npb = self.sbt(es, "npb", [128, DM], F32)
            Bc = Buf("c")
            tr.dma("sp", npb, I["n_post"][l], writes=[Bc])
            eps = self.sbt(es, "eps", [128, 1], F32)
            tr.op("pool", lambda: nc.gpsimd.memset(eps, EPS), writes=[Bc], conc=True)
            mix = [self.sbt(es, "mix", [128, 8, 512], BF16) for _ in range(3)]
            Bmix = [Buf("mix") for _ in range(3)]
            NX = 6
            xt = [self.sbt(es, "xt", [128, DM], F32) for _ in range(NX)]
            Bxt = [Buf("xt") for _ in range(NX)]
            t1 = [self.sbt(es, "t1", [128, DM], F32) for _ in range(2)]
            Bt1 = [Buf("t1") for _ in range(2)]
            junk2 = [self.sbt(es, "junk", [128, DM], BF16) for _ in range(2)]
            Bj2 = [Buf("junk") for _ in range(2)]
            ss = [self.sbt(es, "ss", [128, 4], F32) for _ in range(4)]
            rs = [self.sbt(es, "rs", [128, 2], F32) for _ in range(4)]
            Bss = [Buf("ss") for _ in range(4)]
            Brs = [Buf("rs") for _ in range(4)]
            xn = [self.sbt(es, "xn", [128, DM], BF16) for _ in range(2)]
            Bxn = [Buf("xn") for _ in range(2)]
            xT = [self.sbt(es, "xT", [128, 8, 128], BF16) for _ in range(2)]
            BxT = [Buf("xT") for _ in range(2)]
            Bpy = [Buf("py") for _ in range(3)]
            Bpt = [Buf("pt") for _ in range(2)]

            def load_g(g):
                tr.dma("sp", mix[g % 3], S["mixT"][:, g * 512:(g + 1) * 512].rearrange("(kc p) t -> p kc t", p=128), writes=[Bmix[g % 3]])

            def load_x(i):
                tr.dma("sp", xt[i % NX], x_src[i * 128:(i + 1) * 128, :], writes=[Bxt[i % NX]])

            def stA(i):
                g, j = i // 4, i % 4
                if j == 0 and g + 1 < NG:
                    load_g(g + 1)
                p = i % 3
                py = [ps[2 * p], ps[2 * p + 1]]
                for n in range(2):
                    for kc in range(8):
                        tr.op("pe", lambda n=n, kc=kc: nc.tensor.matmul(py[n], lhsT=mix[g % 3][:, kc, j * 128:(j + 1) * 128], rhs=wbf[:, kc, n * 512:(n + 1) * 512],
                                                                        start=(kc == 0), stop=(kc == 7)), reads=[Bmix[g % 3], Bw], writes=[Bpy[p]],
                              inc=(kc == 7 and n == 1))

            def stB(i):
                p, q, a = i % 3, i % 4, i % 2
                py = [ps[2 * p], ps[2 * p + 1]]
                for n in range(2):
                    tr.op("act", lambda n=n: nc.scalar.activation(out=junk2[a][:, n * 512:(n + 1) * 512], in_=py[n], func=AF.Square, accum_out=ss[q][:, n:n + 1]),
                          reads=[Bpy[p]], writes=[Bj2[a], Bss[q]], conc=True)
                tr.op("dve", lambda: nc.vector.tensor_tensor(out=ss[q][:, 3:4], in0=ss[q][:, 0:1], in1=ss[q][:, 1:2], op=ALU.add), reads=[Bss[q]], writes=[Brs[q]])
                tr.op("act", lambda: nc.scalar.activation(out=rs[q][:, 0:1], in_=ss[q][:, 3:4], func=AF.Ln, scale=1.0 / DM, bias=eps), reads=[Brs[q], Bc], writes=[Brs[q]])
                tr.op("act", lambda: nc.scalar.activation(out=rs[q][:, 0:1], in_=rs[q][:, 0:1], func=AF.Exp, scale=-0.5), reads=[Brs[q]], writes=[Brs[q]])

            def stC(i):
                p, q, a, xs_ = i % 3, i % 4, i % 2, i % NX
                py = [ps[2 * p], ps[2 * p + 1]]
                for n in range(2):
                    tr.op("act", lambda n=n: nc.scalar.activation(out=t1[a][:, n * 512:(n + 1) * 512], in_=py[n], func=AF.Copy, scale=rs[q][:, 0:1]),
                          reads=[Bpy[p], Brs[q]], writes=[Bt1[a]], conc=True)
                tr.op("dve", lambda: nc.vector.tensor_tensor(out=t1[a], in0=t1[a], in1=npb, op=ALU.mult), reads=[Bt1[a], Bc], writes=[Bt1[a]])
                tr.op("pool", lambda: nc.gpsimd.tensor_tensor(out=xt[xs_], in0=xt[xs_], in1=t1[a], op=ALU.add), reads=[Bt1[a], Bxt[xs_]], writes=[Bxt[xs_]])
                tr.dma("sp", S["x1"][i * 128:(i + 1) * 128, :], xt[xs_], reads=[Bxt[xs_]])

            def stD(i):
                q, a, xs_ = i % 4, i % 2, i % NX
                tr.op("act", lambda: nc.scalar.activation(out=junk2[a], in_=xt[xs_], func=AF.Square, accum_out=ss[q][:, 2:3]), reads=[Bxt[xs_]], writes=[Bj2[a], Bss[q]])
                tr.op("act", lambda: nc.scalar.activation(out=rs[q][:, 1:2], in_=ss[q][:, 2:3], func=AF.Ln, scale=1.0 / DM, bias=eps), reads=[Bss[q], Bc], writes=[Brs[q]])
                tr.op("act", lambda: nc.scalar.activation(out=rs[q][:, 1:2], in_=rs[q][:, 1:2], func=AF.Exp, scale=-0.5), reads=[Brs[q]], writes=[Brs[q]])
                tr.op("dve", lambda: nc.vector.tensor_scalar_mul(out=xn[a], in0=xt[xs_], scalar1=rs[q][:, 1:2]), reads=[Bxt[xs_], Brs[q]], writes=[Bxn[a]])

            def stE(i):
                a = i % 2
                pt = psb[6 + a]
                for kc in range(8):
                    tr.op("pe", lambda kc=kc: nc.tensor.transpose(out=pt[:, kc * 128:(kc + 1) * 128], in_=xn[a][:, kc * 128:(kc + 1) * 128], identity=self.ident),
                          reads=[Bxn[a], self.B_const], writes=[Bpt[a]], inc=(kc == 7))
                tr.op("dve", lambda: nc.vector.tensor_copy(out=xT[a], in_=pt.rearrange("p (kc t) -> p kc t", kc=8)), reads=[Bpt[a]], writes=[BxT[a]])
                tr.dma("sp", S["xT2"][:, 1 + i * 128:1 + (i + 1) * 128].rearrange("(kc p) t -> p kc t", p=128), xT[a], reads=[BxT[a]])

            load_g(0)
            load_x(0)
            for it in range(-4, NTL):
                if 0 <= it < NTL:
                    stE(it)
                if 0 <= it + 1 < NTL:
                    stD(it + 1)
                if 0 <= it + 2 < NTL:
                    stC(it + 2)
                if 0 <= it + 3 < NTL:
                    stB(it + 3)
                if 0 <= it + 4 < NTL:
                    stA(it + 4)
                if 1 <= it + 5 < NTL:
                    load_x(it + 5)

    def phase_ffn(self, l, x_dst, wup_pre=None):
        nc, tr, S, I = self.nc, self.tr, self.S, self.I
        T_ = self.T
        NU = T_ // 256
        UPS = SEG // 256
        ps = self.ps
        with ExitStack() as es:
            if wup_pre is not None:
                wup = wup_pre
            else:
                wup, Bwu = self.load_weight(es, I["w_up"][l], DM, 2 * DFF, I["g_fpre"][l], "wup")
            wdn, Bwd = self.load_weight(es, I["w_down"][l], DFF, DM, None, "wdn")
            Bw = Buf("w")
            npb = self.sbt(es, "npb", [128, DM], F32)
            Bc = Buf("c")
            tr.dma("sp", npb, I["n_fpost"][l], writes=[Bc])
            cw = self.sbt(es, "cw", [128, NFT, 4], F32)
            tr.dma("sp", cw, I["cw"][l].rearrange("p (f c) -> p f c", c=4), writes=[Bc])
            eps = self.sbt(es, "eps", [128, 1], F32)
            tr.op("pool", lambda: nc.gpsimd.memset(eps, EPS), writes=[Bc])
            xu = [self.sbt(es, "xu", [128, 8, 258], BF16) for _ in range(2)]
            Bxu = [Buf("xu") for _ in range(2)]
            fT = self.sbt(es, "fT", [128, NFT, 256], BF16)
            BfT = Buf("fT")
            NC3 = 5
            c1 = [self.sbt(es, "c1", [128, 256], F32) for _ in range(NC3)]
            ge = [self.sbt(es, "ge", [128, 256], F32) for _ in range(NC3)]
            Bc1 = [Buf("c1") for _ in range(NC3)]
            Bge = [Buf("ge") for _ in range(NC3)]
            asb = [self.sbt(es, "asb", [128, 258], F32) for _ in range(NC3)]
            gsb = [self.sbt(es, "gsb", [128, 256], F32) for _ in range(NC3)]
            Basb = [Buf("asb") for _ in range(NC3)]
            Bgsb = [Buf("gsb") for _ in range(NC3)]
            t1 = [self.sbt(es, "t1", [128, DM], F32) for _ in range(2)]
            Bt1 = [Buf("t1") for _ in range(2)]
            x1t = [self.sbt(es, "x1t", [128, DM], F32) for _ in range(2)]
            Bx1 = [Buf("x1t") for _ in range(2)]
            junk2 = [self.sbt(es, "junk", [128, DM], BF16) for _ in range(2)]
            Bj2 = [Buf("junk") for _ in range(2)]
            ss = [self.sbt(es, "ss", [128, 2], F32) for _ in range(2)]
            rs = [self.sbt(es, "rs", [128, 1], F32) for _ in range(2)]
            Bss = [Buf("ss") for _ in range(2)]
            BpA = [Buf("pA") for _ in range(2)]
            BpG = [Buf("pG") for _ in range(2)]
            Bpy = [Buf("py") for _ in range(2)]
            cnt = {"f": 0, "tt": 0}

            def load_u(u):
                b = u % 2
                tr.dma("sp", xu[b], S["xT2"][:, u * 256:u * 256 + 258].rearrange("(kc p) t -> p kc t", p=128), writes=[Bxu[b]])
                seg = u // UPS
                if u % UPS == 0:
                    fl = self.flg[:, seg:seg + 1]
                    tr.op("pool", lambda: nc.gpsimd.tensor_scalar_mul(out=xu[b][:, :, 0:1], in0=xu[b][:, :, 0:1], scalar1=fl), reads=[Bxu[b], self.B_const], writes=[Bxu[b]])
                if u % UPS == UPS - 1:
                    fl = self.flg[:, seg + 1:seg + 2]
                    tr.op("pool", lambda: nc.gpsimd.tensor_scalar_mul(out=xu[b][:, :, 257:258], in0=xu[b][:, :, 257:258], scalar1=fl), reads=[Bxu[b], self.B_const], writes=[Bxu[b]])

            def up_mm(u, f):
                b = u % 2
                k = cnt["f"] % 2
                pa = ps[k][:, 0:258]
                pg = ps[2 if k == 0 else 7][:, 0:256]
                for kc in range(8):
                    tr.op("pe", lambda kc=kc: nc.tensor.matmul(pa, lhsT=wup[:, kc, f * 128:(f + 1) * 128], rhs=xu[b][:, kc, :], start=(kc == 0), stop=(kc == 7)),
                          reads=[Bxu[b]], writes=[BpA[k]], inc=(kc == 7))
                for kc in range(8):
                    tr.op("pe", lambda kc=kc: nc.tensor.matmul(pg, lhsT=wup[:, kc, DFF + f * 128:DFF + (f + 1) * 128], rhs=xu[b][:, kc, 1:257], start=(kc == 0), stop=(kc == 7)),
                          reads=[Bxu[b]], writes=[BpG[k]], inc=(kc == 7))
                cnt["f"] += 1
                return k

            def ew1(u, f, k):
                pa = ps[k][:, 0:258]
                pg = ps[2 if k == 0 else 7][:, 0:256]
                c = (u * NFT + f) % NC3
                tr.op("act", lambda: nc.scalar.copy(out=asb[c], in_=pa), reads=[BpA[k]], writes=[Basb[c]])
                tr.op("act", lambda: nc.scalar.copy(out=gsb[c], in_=pg), reads=[BpG[k]], writes=[Bgsb[c]])
                tr.op("pool", lambda: nc.gpsimd.tensor_scalar(out=c1[c], in0=asb[c][:, 1:257], scalar1=cw[:, f, 1:2], scalar2=cw[:, f, 3:4], op0=ALU.mult, op1=ALU.add),
                      reads=[Basb[c], Bc], writes=[Bc1[c]])

            def ew2(u, f):
                c = (u * NFT + f) % NC3
                tr.op("dve", lambda: nc.vector.scalar_tensor_tensor(out=c1[c], in0=asb[c][:, 0:256], scalar=cw[:, f, 0:1], in1=c1[c], op0=ALU.mult, op1=ALU.add),
                      reads=[Basb[c], Bc, Bc1[c]], writes=[Bc1[c]])
                tr.op("dve", lambda: nc.vector.scalar_tensor_tensor(out=c1[c], in0=asb[c][:, 2:258], scalar=cw[:, f, 2:3], in1=c1[c], op0=ALU.mult, op1=ALU.add),
                      reads=[Basb[c], Bc, Bc1[c]], writes=[Bc1[c]])

            def ew3(u, f):
                c = (u * NFT + f) % NC3
                tr.op("act", lambda: nc.scalar.activation(out=ge[c], in_=c1[c], func=AF.Gelu_apprx_tanh), reads=[Bc1[c]], writes=[Bge[c]])
                tr.op("pool", lambda: nc.gpsimd.tensor_tensor(out=fT[:, f, :], in0=ge[c], in1=gsb[c], op=ALU.mult), reads=[Bge[c], Bgsb[c]], writes=[BfT], conc=True)

            def down(u):
                for j in range(2):
                    i = u * 2 + j
                    a = i % 2
                    py = [ps[3 + 2 * a], ps[4 + 2 * a]]
                    tr.dma("sp", x1t[a], S["x1"][i * 128:(i + 1) * 128, :], writes=[Bx1[a]])
                    for n in range(2):
                        for f in range(NFT):
                            tr.op("pe", lambda n=n, f=f: nc.tensor.matmul(py[n], lhsT=fT[:, f, j * 128:(j + 1) * 128], rhs=wdn[:, f, n * 512:(n + 1) * 512],
                                                                          start=(f == 0), stop=(f == NFT - 1)), reads=[BfT], writes=[Bpy[a]],
                                  inc=(f == NFT - 1 and n == 1))
                    for n in range(2):
                        tr.op("act", lambda n=n: nc.scalar.activation(out=junk2[a][:, n * 512:(n + 1) * 512], in_=py[n], func=AF.Square, accum_out=ss[a][:, n:n + 1]),
                              reads=[Bpy[a]], writes=[Bj2[a], Bss[a]])
                    tr.op("dve", lambda: nc.vector.tensor_tensor(out=rs[a], in0=ss[a][:, 0:1], in1=ss[a][:, 1:2], op=ALU.add), reads=[Bss[a]], writes=[Bss[a]])
                    tr.op("act", lambda: nc.scalar.activation(out=rs[a], in_=rs[a], func=AF.Ln, scale=1.0 / DM, bias=eps), reads=[Bss[a], Bc], writes=[Bss[a]])
                    tr.op("act", lambda: nc.scalar.activation(out=rs[a], in_=rs[a], func=AF.Exp, scale=-0.5), reads=[Bss[a]], writes=[Bss[a]])
                    for n in range(2):
                        tr.op("act", lambda n=n: nc.scalar.activation(out=t1[a][:, n * 512:(n + 1) * 512], in_=py[n], func=AF.Copy, scale=rs[a][:, 0:1]),
                              reads=[Bpy[a], Bss[a]], writes=[Bt1[a]])
                    tr.op("pool", lambda: nc.gpsimd.tensor_tensor(out=t1[a], in0=t1[a], in1=npb, op=ALU.mult), reads=[Bt1[a], Bc], writes=[Bt1[a]])
                    tr.op("pool", lambda: nc.gpsimd.tensor_tensor(out=x1t[a], in0=x1t[a], in1=t1[a], op=ALU.add), reads=[Bt1[a], Bx1[a]], writes=[Bx1[a]])
                    tr.dma("sp", x_dst[i * 128:(i + 1) * 128, :], x1t[a], reads=[Bx1[a]])

            load_u(0)
            if NU > 1:
                load_u(1)
            for u in range(NU):
                for f in range(NFT):
                    k = up_mm(u, f)
                    ew1(u, f, k)
                    if f >= 1:
                        ew2(u, f - 1)
                    if f >= 2:
                        ew3(u, f - 2)
                ew2(u, NFT - 1)
                ew3(u, NFT - 2)
                ew3(u, NFT - 1)
                down(u)
                if u + 2 < NU:
                    load_u(u + 2)


def make_consts():
    c = np.zeros((128, C_END), np.float32)
    k = np.arange(128)[:, None]
    q = np.arange(128)[None, :]
    dA = np.abs(k - 64 - q).astype(np.float32)
    dB = np.abs(k + 64 - q).astype(np.float32)
    c[:, C_DA:C_DA + 128] = np.where(dA <= 64, dA, BIG)
    c[:, C_DB:C_DB + 128] = np.where(dB <= 64, dB, BIG)
    j, i = k, q
    g = -1.0 / 16.0
    c[:, C_LTF:C_LTF + 128] = np.where(j <= i, g, 0.0)
    c[:, C_LUF:C_LUF + 128] = np.where(j > i, g, 0.0)
    c[:, C_LTB:C_LTB + 128] = np.where(j >= i, g, 0.0)
    c[:, C_LLB:C_LLB + 128] = np.where(j < i, g, 0.0)
    c[:, C_MF:C_MF + 128] = np.where(j <= i, 1.0, 0.0)
    c[:, C_MB:C_MB + 128] = np.where(j >= i, 1.0, 0.0)
    c[:, C_ID:C_ID + 128] = np.eye(128, dtype=np.float32)
    return c


def core_layout(nseg=3):
    segs, links = [], []
    for c in range(NCORES):
        if c < 4:
            segs.append([("s", c, 0), ("s", c, 1), ("p", c, 0)])
            links.append([0.0, 1.0, 0.0, 0.0])
        else:
            b0 = 4 + 3 * (c - 4)
            segs.append([("p", b0, 0), ("p", b0 + 1, 0), ("p", b0 + 2, 0)])
            links.append([0.0, 0.0, 0.0, 0.0])
    return segs, links


def host_params(inp, L=2):
    f = lambda a: np.ascontiguousarray(a, dtype=np.float32)
    P = {}
    P["consts"] = make_consts()
    P["w_in"] = f(inp["w_in"])
    P["w_out"] = f(inp["w_out"])
    P["w_up"] = f(inp["w_up"])
    P["w_down"] = f(inp["w_down"])
    P["g_pre"] = f(inp["norm_mix_pre"].reshape(L, 8, 128).transpose(0, 2, 1))
    P["g_fpre"] = f(inp["norm_ffn_pre"].reshape(L, 8, 128).transpose(0, 2, 1))
    P["n_post"] = f(np.broadcast_to(inp["norm_mix_post"][:, None, :], (L, 128, DM)))
    P["n_fpost"] = f(np.broadcast_to(inp["norm_ffn_post"][:, None, :], (L, 128, DM)))
    P["gnorm"] = f(np.broadcast_to(np.tile(inp["gla_norm"], (1, 8))[:, None, :], (L, 128, 512)))
    P["wg_f"] = f(np.concatenate([inp["b_gate_fwd"][:, None, :], inp["w_gate_fwd"]], axis=1))
    P["wg_b"] = f(np.concatenate([inp["b_gate_bwd"][:, None, :], inp["w_gate_bwd"]], axis=1))
    cw = np.concatenate([inp["conv_w"], inp["conv_b"][:, None, :]], axis=1)
    P["cw"] = f(cw.reshape(L, 4, NFT, 128).transpose(0, 3, 2, 1).reshape(L, 128, NFT * 4))
    return P


def flag_arrays(link):
    n = len(link)
    ns = n - 1
    flg = np.broadcast_to(np.asarray(link, np.float32)[None, :], (128, n)).copy()
    c = make_consts()
    DA = c[:, C_DA:C_DA + 128]
    DB = c[:, C_DB:C_DB + 128]
    kk = np.arange(128)[:, None]
    qq = np.arange(128)[None, :]
    DA_first = np.where((kk < 64) | ((qq < 64) & (kk - 64 - qq < 0)), BIG, DA).astype(np.float32)
    DB_last = np.where(kk >= 64, BIG, DB).astype(np.float32)
    tabs = np.zeros((128, 1 + 3 * ns, 256), np.float32)
    tabs[:, 0, :128], tabs[:, 0, 128:] = DA, DB
    for s in range(ns):
        A = DA if link[s] else DA_first
        B = DB if link[s + 1] else DB_last
        tabs[:, 1 + s, :128], tabs[:, 1 + s, 128:] = A, DB
        tabs[:, 1 + ns + s, :128], tabs[:, 1 + ns + s, 128:] = DA, B
        tabs[:, 1 + 2 * ns + s, :128], tabs[:, 1 + 2 * ns + s, 128:] = A, B
    X = np.zeros((8, 1024), np.float64)
    for h in range(8):
        for (d, nres, ntile) in BRANCHES:
            for j in range(1024):
                qi = (1024 + j) // d - SEG // d + 64
                if qi > 0:
                    X[h, j] += np.exp(-SLOPES[h] * d * np.arange(64 - qi, 64, dtype=np.float64)).sum()
    xden = np.zeros((ns, 8, 1024), np.float32)
    for s in range(ns):
        xden[s] = X * (1.0 - link[s + 1])
    return flg, tabs.reshape(128, -1), xden


_CACHE = {}


def kernel(**inputs):
    inp = {k: np.asarray(v) for k, v in inputs.items()}
    xp, xs = inp["x_prompt"], inp["x_sample"]
    segs, links = core_layout()
    P = host_params(inp)
    in_maps = []
    for c in range(NCORES):
        parts = []
        for (grp, b, half) in segs[c]:
            parts.append(xs[b, half * SEG:(half + 1) * SEG] if grp == "s" else xp[b])
        m = dict(P)
        m["x"] = np.ascontiguousarray(np.concatenate(parts, axis=0), dtype=np.float32)
        m["flg"], m["atab"], m["xden"] = flag_arrays(links[c])
        in_maps.append(m)
    if "nc" not in _CACHE:
        _CACHE["nc"] = Builder(nseg=3, nlayers=2).build()
    res = run_bass_kernel_spmd(_CACHE["nc"], in_maps, core_ids=list(range(NCORES)))
    yp = np.empty_like(xp, dtype=np.float32)
    ys = np.empty_like(xs, dtype=np.float32)
    for c in range(NCORES):
        y = res.results[c]["y"]
        for i, (grp, b, half) in enumerate(segs[c]):
            blk = y[i * SEG:(i + 1) * SEG]
            if grp == "s":
                ys[b, half * SEG:(half + 1) * SEG] = blk
            else:
                yp[b] = blk
    return (yp, ys)
```

```python
import os
import numpy as np
from contextlib import ExitStack
import concourse.bass as bass
import concourse.mybir as mybir
from concourse.bass_utils import run_bass_kernel_spmd

F32 = mybir.dt.float32
BF16 = mybir.dt.bfloat16
AF = mybir.ActivationFunctionType
ALU = mybir.AluOpType
AX = mybir.AxisListType
F_INC = os.environ.get("F_INC", "1") == "1"
F_CONC = os.environ.get("F_CONC", "1") == "1"
F_ACC = os.environ.get("F_ACC", "1") == "1"
F_INCT = os.environ.get("F_INCT", "1") == "1"
F_INCA = os.environ.get("F_INCA", "1") == "1"

DM = 1024
SEG = 2048
PAD = 1024
INW = 3088
DFF = 2816
NFT = DFF // 128
NCORES = 8
EPS = 1e-6
BIG = 1.0e30
GT = (96, 96, 64)
BRANCHES = ((1, 1, 16), (4, 4, 4), (16, 16, 1))
SLOPES = [2.0 ** (-(h + 1)) for h in range(8)]
C_DA, C_DB, C_LTF, C_LUF, C_LTB, C_LLB, C_MF, C_MB, C_ID, C_END = 0, 128, 256, 384, 512, 640, 768, 896, 1024, 1152


def sl_(start, n, step=1):
    return slice(start, start + step * (n - 1) + 1, step)


class Buf:
    __slots__ = ("name", "w", "r", "pr", "pw", "wc")

    def __init__(self, name=""):
        self.name = name
        self.w = {}
        self.r = {}
        self.pr = {}
        self.pw = {}
        self.wc = True


class Tracker:
    ROLL = 30000

    def __init__(self, nc):
        self.nc = nc
        self.engs = {"pe": nc.tensor, "act": nc.scalar, "dve": nc.vector, "pool": nc.gpsimd, "sp": nc.sync}
        self.sems = []
        self.cur = {}
        self.cnt = {}
        self.seen = {e: {} for e in self.engs}
        for e in ("pe", "act", "dve", "pool"):
            self._new_sem(e)
        self.dq = {}
        for q, n in (("sp", 8), ("pool", 2), ("act", 2)):
            ks = []
            for i in range(n):
                k = len(self.sems)
                self.sems.append(nc.alloc_semaphore(name=f"d_{q}{i}"))
                self.cnt[k] = 0
                ks.append(k)
            self.dq[q] = [ks, 0]
        self.n_inst = 0
        self.n_wait = 0
        self.pend = {}

    def _new_sem(self, e):
        k = len(self.sems)
        self.sems.append(self.nc.alloc_semaphore(name=f"c_{e}{k}"))
        self.cnt[k] = 0
        self.cur[e] = k

    def _wait(self, eng, deps):
        seen = self.seen[eng]
        best = {}
        for (k, v) in deps:
            if best.get(k, 0) < v:
                best[k] = v
        for k, v in best.items():
            if seen.get(k, 0) < v:
                self.engs[eng].wait_ge(self.sems[k], v)
                seen[k] = v
                self.n_wait += 1

    @staticmethod
    def _deps(reads, writes, conc):
        deps = []
        for b in reads:
            deps.extend(b.w.items())
        for b in writes:
            if b.r:
                deps.extend(b.r.items())
                deps.extend(b.w.items())
            else:
                deps.extend(b.pr.items())
                deps.extend(b.pw.items())
                if not (conc and b.wc):
                    deps.extend(b.w.items())
        return deps

    @staticmethod
    def _commit(ev, reads, writes, conc):
        k, v = ev
        for b in reads:
            if b.r.get(k, 0) < v:
                b.r[k] = v
        for b in writes:
            if b.r:
                b.pr, b.pw = b.r, {}
                b.r = {}
                b.w = {k: v}
                b.wc = conc
            elif conc and b.wc:
                if b.w.get(k, 0) < v:
                    b.w[k] = v
            elif conc:
                b.pw = b.w
                b.w = {k: v}
                b.wc = True
            else:
                b.w = {k: v}
                b.wc = False
                b.pr, b.pw = {}, {}

    def op(self, eng, fn, reads=(), writes=(), inc=True, conc=False):
        if not F_INC:
            inc = True
        if not F_CONC:
            conc = False
        pend = self.pend.setdefault(eng, [[], []])
        if inc and not pend[0] and not pend[1] and self.cnt[self.cur[eng]] >= self.ROLL:
            self._new_sem(eng)
        self._wait(eng, self._deps(reads, writes, conc))
        inst = fn()
        self.n_inst += 1
        if not inc:
            pend[0].extend(reads)
            pend[1].extend(writes)
            return inst
        k = self.cur[eng]
        inst.then_inc(self.sems[k], 1)
        self.cnt[k] += 1
        self._commit((k, self.cnt[k]), list(reads) + pend[0], list(writes) + pend[1], conc)
        pend[0].clear()
        pend[1].clear()
        return inst

    def dma(self, q, out, in_, reads=(), writes=(), conc=False, **kw):
        if not F_CONC:
            conc = False
        ks, i = self.dq[q]
        k = ks[i % len(ks)]
        self.dq[q][1] = i + 1
        deps = self._deps(reads, writes, conc)
        if self.cnt[k] > 0:
            deps.append((k, self.cnt[k]))
        self._wait(q, deps)
        inst = self.engs[q].dma_start(out=out, in_=in_, **kw)
        inst.then_inc(self.sems[k], 16)
        self.cnt[k] += 16
        self.n_inst += 1
        self._commit((k, self.cnt[k]), reads, writes, conc)
        return inst

    def barrier(self):
        allev = [(k, v) for k, v in self.cnt.items() if v > 0]
        for e in self.engs:
            self._wait(e, allev)


class Builder:
    def __init__(self, nseg=3, nlayers=2, debug=(), phases=None):
        self.nseg = nseg
        self.T = nseg * SEG
        self.nlayers = nlayers
        self.debug = set(debug)
        self.phases = phases
        self.nc = bass.Bass("TRN2", target_bir_lowering=False)
        self.uid = 0

    def din(self, name, shape, dt=F32):
        return self.nc.dram_tensor(name, list(shape), dt, kind="ExternalInput").ap()

    def dscr(self, name, shape, dt):
        kind = "ExternalOutput" if name in self.debug else "Internal"
        return self.nc.dram_tensor(name, list(shape), dt, kind=kind).ap()

    def sbt(self, es, name, shape, dt):
        self.uid += 1
        t = es.enter_context(self.nc.sbuf_tensor(f"{name}_{self.uid}", list(shape), dt))
        return t.ap() if hasattr(t, "ap") else t

    def build(self):
        nc = self.nc
        T_ = self.T
        L = self.nlayers
        ns = self.nseg
        self.tr = Tracker(nc)
        tr = self.tr
        I = {}
        I["x"] = self.din("x", [T_, DM])
        I["flg"] = self.din("flg", [128, ns + 1])
        I["atab"] = self.din("atab", [128, (1 + 3 * ns) * 256])
        I["xden"] = self.din("xden", [ns, 8, 1024])
        I["consts"] = self.din("consts", [128, C_END])
        I["w_in"] = self.din("w_in", [L, DM, INW])
        I["w_out"] = self.din("w_out", [L, DM, DM])
        I["w_up"] = self.din("w_up", [L, DM, 2 * DFF])
        I["w_down"] = self.din("w_down", [L, DFF, DM])
        I["g_pre"] = self.din("g_pre", [L, 128, 8])
        I["g_fpre"] = self.din("g_fpre", [L, 128, 8])
        I["n_post"] = self.din("n_post", [L, 128, DM])
        I["n_fpost"] = self.din("n_fpost", [L, 128, DM])
        I["gnorm"] = self.din("gnorm", [L, 128, 512])
        I["wg_f"] = self.din("wg_f", [L, 17, 256])
        I["wg_b"] = self.din("wg_b", [L, 17, 256])
        I["cw"] = self.din("cw", [L, 128, NFT * 4])
        self.I = I
        self.y = nc.dram_tensor("y", [T_, DM], F32, kind="ExternalOutput").ap()
        S = {}
        S["qT"] = self.dscr("qT_s", [512, T_], BF16)
        S["kT"] = self.dscr("kT_s", [512, T_ + 2 * PAD], BF16)
        S["v"] = self.dscr("v_s", [4, T_ + 2 * PAD, 128], BF16)
        S["gqT"] = self.dscr("gqT_s", [3, 96, T_], F32)
        S["gkT"] = self.dscr("gkT_s", [3, 96, T_], F32)
        S["gk"] = self.dscr("gk_s", [T_, 256], F32)
        S["gv"] = self.dscr("gv_s", [T_, 512], BF16)
        S["gr"] = self.dscr("gr_s", [T_, 512], F32)
        S["lrh"] = self.dscr("lrh_s", [16, T_], BF16)
        S["lrl"] = self.dscr("lrl_s", [16, T_], BF16)
        S["of"] = self.dscr("of_s", [T_, 512], F32)
        S["mixT"] = self.dscr("mixT_s", [DM, T_], BF16)
        S["x1"] = self.dscr("x1_s", [T_, DM], F32)
        S["xT2"] = self.dscr("xT2_s", [DM, T_ + 2], BF16)
        S["xs"] = self.dscr("xs_s", [T_, DM], F32)
        self.S = S
        self.ps = [nc.alloc_psum_tensor(f"psb{i}", [128, 512], F32).ap() for i in range(8)]
        self.psb = [p.bitcast(BF16) for p in self.ps]

        with ExitStack() as es:
            self.flg = self.sbt(es, "flg", [128, ns + 1], F32)
            self.ident = self.sbt(es, "ident", [128, 128], BF16)
            self.B_const = Buf("const")
            zero = self.sbt(es, "zero", [128, 1024], BF16)
            idf = self.sbt(es, "idf", [128, 128], F32)
            tr.dma("sp", self.flg, I["flg"], writes=[self.B_const])
            tr.dma("sp", idf, I["consts"][:, C_ID:C_ID + 128], writes=[self.B_const])
            tr.op("dve", lambda: nc.vector.tensor_copy(out=self.ident, in_=idf), reads=[self.B_const], writes=[self.B_const])
            Bz = Buf("zero")
            tr.op("pool", lambda: nc.gpsimd.memset(zero, 0.0), writes=[Bz])
            for i in range(4):
                tr.dma("sp", S["kT"][i * 128:(i + 1) * 128, 0:PAD], zero, reads=[Bz])
                tr.dma("sp", S["kT"][i * 128:(i + 1) * 128, PAD + T_:PAD + T_ + PAD], zero, reads=[Bz])
                for j in range(PAD // 128):
                    tr.dma("sp", S["v"][i, j * 128:(j + 1) * 128, :], zero[:, 0:128], reads=[Bz])
                    tr.dma("sp", S["v"][i, PAD + T_ + j * 128:PAD + T_ + (j + 1) * 128, :], zero[:, 0:128], reads=[Bz])
            for kc in range(8):
                tr.dma("sp", S["xT2"][kc * 128:(kc + 1) * 128, 0:1], zero[:, 0:1], reads=[Bz], allow_slow_non_contiguous=True)
                tr.dma("sp", S["xT2"][kc * 128:(kc + 1) * 128, T_ + 1:T_ + 2], zero[:, 0:1], reads=[Bz], allow_slow_non_contiguous=True)
            zf = zero.bitcast(F32)
            for c0 in range(0, T_, 512):
                tr.dma("sp", S["gqT"][2, 64:96, c0:c0 + 512], zf[0:32, :], reads=[Bz])
                tr.dma("sp", S["gkT"][2, 64:96, c0:c0 + 512], zf[0:32, :], reads=[Bz])
            tr.barrier()

            want = self.phases
            for l in range(L):
                x_src = I["x"] if l == 0 else S["xs"]
                x_dst = self.y if l == L - 1 else S["xs"]
                if want is None or "p1" in want:
                    self.phase_inproj(l, x_src)
                    tr.barrier()
                if want is None or "p2" in want:
                    self.phase_attn(l)
                    tr.barrier()
                with ExitStack() as esl:
                    full = want is None
                    wup = bg = None
                    with ExitStack() as es_stg:
                        if full:
                            wup = self.sbt(esl, "wup", [128, 8, 2 * DFF], BF16)
                            CHB = 1408
                            stg = [self.sbt(es_stg, "bstg", [128, CHB], F32) for _ in range(2)]
                            Bs = [Buf("bstg") for _ in range(2)]
                            gsb = self.sbt(es_stg, "bgsb", [128, 8], F32)
                            Bg = Buf("bgsb")
                            tr.dma("sp", gsb, I["g_fpre"][l], writes=[Bg])
                            bg = self.weight_steps(wup, I["w_up"][l], DM, 2 * DFF, gsb, Bg, stg, Bs, CHB)
                        if want is None or "p3" in want:
                            self.phase_gla(l, bg)
                        if bg is not None:
                            for _ in bg:
                                pass
                        tr.barrier()
                    if want is None or "p4" in want:
                        self.phase_outproj(l, x_src)
                        tr.barrier()
                        self.phase_ffn(l, x_dst, wup)
                        tr.barrier()
            tr.barrier()
        return nc

    def weight_steps(self, wbf, wdram, K, N, gsb, Bg, stg, Bs, CH):
        nc, tr = self.nc, self.tr
        nk = K // 128
        ns_ = len(stg)
        i = 0
        for kc in range(nk):
            for c0 in range(0, N, CH):
                cw = min(CH, N - c0)
                s = i % ns_
                tr.dma("sp", stg[s][:, 0:cw], wdram[kc * 128:(kc + 1) * 128, c0:c0 + cw], writes=[Bs[s]])
                o_ap = wbf[:, kc, c0:c0 + cw]
                i_ap = stg[s][:, 0:cw]
                if gsb is None:
                    fn = lambda o_ap=o_ap, i_ap=i_ap: nc.scalar.copy(out=o_ap, in_=i_ap)
                else:
                    g_ap = gsb[:, kc:kc + 1]
                    fn = lambda o_ap=o_ap, i_ap=i_ap, g_ap=g_ap: nc.scalar.activation(out=o_ap, in_=i_ap, func=AF.Copy, scale=g_ap)
                tr.op("act", fn, reads=[Bs[s], Bg], writes=[])
                i += 1
                yield

    def load_weight(self, es_outer, wdram, K, N, gsrc, name):
        nc, tr = self.nc, self.tr
        nk = K // 128
        wbf = self.sbt(es_outer, name, [128, nk, N], BF16)
        Bw = Buf(name)
        CH = 2816 if N > 2816 else N
        with ExitStack() as es:
            stg = [self.sbt(es, "wstg", [128, CH], F32) for _ in range(3)]
            Bs = [Buf("wstg") for _ in range(3)]
            gsb = None
            if gsrc is not None:
                gsb = self.sbt(es, "gsb", [128, nk], F32)
                tr.dma("sp", gsb, gsrc, writes=[Bw])
            for _ in self.weight_steps(wbf, wdram, K, N, gsb, Bw, stg, Bs, CH):
                pass
            tr.barrier()
        return wbf, Bw

    def phase_inproj(self, l, x_src):
        nc, tr, S, I = self.nc, self.tr, self.S, self.I
        T_ = self.T
        NG = T_ // 512
        ps, psb = self.ps, self.psb
        with ExitStack() as es:
            wbf, Bw = self.load_weight(es, I["w_in"][l], DM, INW, I["g_pre"][l], "win")
            self.eps_ap = self.sbt(es, "eps", [128, 1], F32)
            Beps = Buf("eps")
            tr.op("pool", lambda: nc.gpsimd.memset(self.eps_ap, EPS), writes=[Beps])
            xg = [self.sbt(es, "xg", [128, 4, DM], F32) for _ in range(2)]
            Bxg = [Buf("xg") for _ in range(2)]
            xn = [self.sbt(es, "xn", [128, 4, DM], BF16) for _ in range(2)]
            Bxn = [Buf("xn") for _ in range(2)]
            xT = [self.sbt(es, "xT", [128, 8, 512], BF16) for _ in range(2)]
            BxT = [Buf("xT") for _ in range(2)]
            junk = [self.sbt(es, "junk", [128, DM], BF16) for _ in range(4)]
            Bjunk = [Buf("junk") for _ in range(4)]
            ss = [self.sbt(es, "ss", [128, 4], F32) for _ in range(2)]
            rs = [self.sbt(es, "rs", [128, 4], F32) for _ in range(2)]
            Bss = [Buf("ss") for _ in range(2)]
            Brs = [Buf("rs") for _ in range(2)]
            NST = 6
            stb = [self.sbt(es, "stb", [128, 512], BF16) for _ in range(NST)]
            stf = [self.sbt(es, "stf", [128, 512], F32) for _ in range(NST)]
            Bstb = [Buf("stb") for _ in range(NST)]
            Bstf = [Buf("stf") for _ in range(NST)]
            Bps = [Buf(f"ps{i}") for i in range(8)]
            BpsT = [Buf(f"psT{i}") for i in range(4)]
            cnt = {"stb": 0, "stf": 0, "ps": 0, "ev": 0, "pt": 0}

            def load(g):
                tr.dma("sp", xg[g % 2], x_src[g * 512:(g + 1) * 512, :].rearrange("(j p) d -> p j d", p=128), writes=[Bxg[g % 2]])

            def norm(g):
                s = g % 2
                for j in range(4):
                    tr.op("act", lambda j=j: nc.scalar.activation(out=junk[j], in_=xg[s][:, j, :], func=AF.Square, accum_out=ss[s][:, j:j + 1]),
                          reads=[Bxg[s]], writes=[Bjunk[j], Bss[s]])
                tr.op("act", lambda: nc.scalar.activation(out=rs[s], in_=ss[s], func=AF.Ln, scale=1.0 / DM, bias=self.eps_ap),
                      reads=[Bss[s], Beps], writes=[Brs[s]])
                tr.op("act", lambda: nc.scalar.activation(out=rs[s], in_=rs[s], func=AF.Exp, scale=-0.5), reads=[Brs[s]], writes=[Brs[s]])
                for j in range(4):
                    tr.op("act", lambda j=j: nc.scalar.activation(out=xn[s][:, j, :], in_=xg[s][:, j, :], func=AF.Copy, scale=rs[s][:, j:j + 1]),
                          reads=[Bxg[s], Brs[s]], writes=[Bxn[s]], conc=(j > 0))

            def transposes(g):
                s = g % 2
                for kp in range(4):
                    pt = cnt["pt"] % 2
                    cnt["pt"] += 1
                    pv = psb[pt]
                    for k2 in range(2):
                        kc = 2 * kp + k2
                        for j in range(4):
                            o_ap = pv[:, k2 * 512 + j * 128:k2 * 512 + (j + 1) * 128]
                            tr.op("pe", lambda j=j, kc=kc, o_ap=o_ap: nc.tensor.transpose(out=o_ap, in_=xn[s][:, j, kc * 128:(kc + 1) * 128], identity=self.ident),
                                  reads=[Bxn[s], self.B_const], writes=[BpsT[pt]], inc=(not F_INCT) or (k2 == 1 and j == 3))
                    src = pv.rearrange("p (k t) -> p k t", k=2)
                    if kp % 2 == 0:
                        tr.op("act", lambda kp=kp, src=src: nc.scalar.copy(out=xT[s][:, 2 * kp:2 * kp + 2, :], in_=src), reads=[BpsT[pt]], writes=[BxT[s]])
                    else:
                        tr.op("dve", lambda kp=kp, src=src: nc.vector.tensor_copy(out=xT[s][:, 2 * kp:2 * kp + 2, :], in_=src), reads=[BpsT[pt]], writes=[BxT[s]])

            def proj_tile(g, kind, lhs_fn, rhs_fn, M, N, dst, dt, scale=None):
                s = g % 2
                pi = 2 + cnt["ps"] % 6
                cnt["ps"] += 1
                pv = ps[pi][0:M, 0:N]
                for kc in range(8):
                    tr.op("pe", lambda kc=kc: nc.tensor.matmul(pv, lhsT=lhs_fn(kc), rhs=rhs_fn(kc), start=(kc == 0), stop=(kc == 7)),
                          reads=[BxT[s], Bw], writes=[Bps[pi]], inc=(kc == 7))
                if dt == BF16:
                    si = cnt["stb"] % NST
                    cnt["stb"] += 1
                    st, Bst = stb[si], Bstb[si]
                else:
                    si = cnt["stf"] % NST
                    cnt["stf"] += 1
                    st, Bst = stf[si], Bstf[si]
                sv = st[0:M, 0:N]
                ev = cnt["ev"] % 2
                cnt["ev"] += 1
                if ev == 0:
                    if scale is None:
                        tr.op("act", lambda: nc.scalar.copy(out=sv, in_=pv), reads=[Bps[pi]], writes=[Bst])
                    else:
                        tr.op("act", lambda: nc.scalar.mul(out=sv, in_=pv, mul=scale), reads=[Bps[pi]], writes=[Bst])
                else:
                    if scale is None:
                        tr.op("dve", lambda: nc.vector.tensor_copy(out=sv, in_=pv), reads=[Bps[pi]], writes=[Bst])
                    else:
                        tr.op("dve", lambda: nc.vector.tensor_scalar_mul(out=sv, in0=pv, scalar1=scale), reads=[Bps[pi]], writes=[Bst])
                return sv, Bst

            def feat_major(g):
                s = g % 2
                t0 = g * 512
                tiles = []
                for i in range(4):
                    tiles.append((i * 128, 128, S["qT"][i * 128:(i + 1) * 128, t0:t0 + 512], BF16, 0.125))
                for i in range(4):
                    tiles.append((512 + i * 128, 128, S["kT"][i * 128:(i + 1) * 128, PAD + t0:PAD + t0 + 512], BF16, None))
                c = 0
                for j in range(3):
                    tiles.append((1536 + c, GT[j], S["gqT"][j, 0:GT[j], t0:t0 + 512], F32, None))
                    tiles.append((1792 + c, GT[j], S["gkT"][j, 0:GT[j], t0:t0 + 512], F32, None))
                    c += GT[j]
                for (c0, M, dst, dt, sc) in tiles:
                    sv, Bst = proj_tile(g, "f", lambda kc, c0=c0, M=M: wbf[:, kc, c0:c0 + M], lambda kc: xT[s][:, kc, :], M, 512, None, dt, sc)
                    tr.dma("sp", dst, sv, reads=[Bst])
                pi = 2 + cnt["ps"] % 6
                cnt["ps"] += 1
                pv = ps[pi][0:16, 0:512]
                for kc in range(8):
                    tr.op("pe", lambda kc=kc: nc.tensor.matmul(pv, lhsT=wbf[:, kc, 3072:3088], rhs=xT[s][:, kc, :], start=(kc == 0), stop=(kc == 7)),
                          reads=[BxT[s], Bw], writes=[Bps[pi]], inc=(kc == 7))
                s1 = cnt["stb"] % NST
                s2_ = (cnt["stb"] + 1) % NST
                cnt["stb"] += 2
                tr.op("act", lambda: nc.scalar.copy(out=stb[s1][0:16, :], in_=pv), reads=[Bps[pi]], writes=[Bstb[s1]])
                tr.op("dve", lambda: nc.vector.tensor_tensor(out=stb[s2_][0:16, :], in0=pv, in1=stb[s1][0:16, :], op=ALU.subtract),
                      reads=[Bps[pi], Bstb[s1]], writes=[Bstb[s2_]])
                tr.dma("sp", S["lrh"][:, t0:t0 + 512], stb[s1][0:16, :], reads=[Bstb[s1]])
                tr.dma("sp", S["lrl"][:, t0:t0 + 512], stb[s2_][0:16, :], reads=[Bstb[s2_]])

            def tok_major(g):
                s = g % 2
                for j in range(4):
                    r0 = g * 512 + j * 128
                    specs = [
                        (1024, 512, S["v"][:, PAD + r0:PAD + r0 + 128, :].rearrange("h t f -> t h f"), BF16, "v"),
                        (1792, 256, S["gk"][r0:r0 + 128, :], F32, ""),
                        (2048, 512, S["gv"][r0:r0 + 128, :], BF16, ""),
                        (2560, 512, S["gr"][r0:r0 + 128, :], F32, ""),
                    ]
                    for (c0, N, dst, dt, kind) in specs:
                        sv, Bst = proj_tile(g, "t", lambda kc, j=j: xT[s][:, kc, j * 128:(j + 1) * 128],
                                            lambda kc, c0=c0, N=N: wbf[:, kc, c0:c0 + N], 128, N, None, dt, None)
                        src = sv.rearrange("t (h f) -> t h f", h=4) if kind == "v" else sv
                        tr.dma("sp", dst, src, reads=[Bst])

            load(0)
            if NG > 1:
                load(1)
            norm(0)
            transposes(0)
            for g in range(NG):
                feat_major(g)
                if g + 1 < NG:
                    norm(g + 1)
                    transposes(g + 1)
                tok_major(g)
                if g + 2 < NG:
                    load(g + 2)

    def phase_attn(self, l):
        nc, tr, S, I = self.nc, self.tr, self.S, self.I
        ns = self.nseg
        ps = self.ps
        with ExitStack() as es:
            Bc = Buf("attc")
            ntab = 1 + 3 * ns
            tabs = self.sbt(es, "tabs", [128, ntab, 256], F32)
            tr.dma("sp", tabs, I["atab"].rearrange("p (n c) -> p n c", c=256), writes=[Bc])
            xd = [self.sbt(es, "xd", [128, 2, 1024], F32) for _ in range(3)]
            Bxd = [Buf("xd") for _ in range(3)]
            for i in range(3):
                tr.op("pool", lambda i=i: nc.gpsimd.memset(xd[i][0:64], 0.0), writes=[Bxd[i]])
            QT = [self.sbt(es, "QT", [128, SEG], BF16) for _ in range(2)]
            KT = [self.sbt(es, "KT", [128, SEG + 2 * PAD], BF16) for _ in range(2)]
            BQK = [Buf("QK") for _ in range(2)]
            NVBIG, NVS, LA = 2, 8, 5
            Vbig = [self.sbt(es, "Vbig", [128, 17, 2, 128], BF16) for _ in range(NVBIG)]
            Vsml = [self.sbt(es, "Vsml", [128, 5, 2, 128], BF16) for _ in range(NVS)]
            BVbig = [Buf("Vbig") for _ in range(NVBIG)]
            BVsml = [Buf("Vsml") for _ in range(NVS)]
            for i in range(NVBIG):
                tr.op("pool", lambda i=i: nc.gpsimd.memset(Vbig[i], 1.0), writes=[BVbig[i]])
            for i in range(NVS):
                tr.op("pool", lambda i=i: nc.gpsimd.memset(Vsml[i], 1.0), writes=[BVsml[i]])
            acc = [self.sbt(es, "acc", [128, 2, SEG], F32) for _ in range(2)]
            Bacc = [Buf("acc") for _ in range(2)]
            rb = self.sbt(es, "rb", [64, 2, SEG], F32)
            Brb = [Buf("rb0"), Buf("rb1")]
            ob = [self.sbt(es, "ob", [64, 2, SEG], BF16) for _ in range(2)]
            Bob = [Buf("ob") for _ in range(2)]
            NS3 = 3
            tmp = [self.sbt(es, "tmp", [128, 2, 256], F32) for _ in range(NS3)]
            PT = [self.sbt(es, "PT", [128, 2, 256], BF16) for _ in range(NS3)]
            Btmp = [Buf("tmp") for _ in range(NS3)]
            BPT = [Buf("PT") for _ in range(NS3)]
            BpsS = [Buf("psS") for _ in range(2)]
            BpsO = [Buf("psO") for _ in range(4)]
            osb = [self.sbt(es, "osb", [128, 2, 128], F32) for _ in range(4)]
            Bosb = [Buf("osb") for _ in range(4)]

            units = [(s, hp) for s in range(ns) for hp in range(4)]
            groups = []
            for ui, (s, hp) in enumerate(units):
                for (d, nres, ntile) in BRANCHES:
                    for r in range(nres):
                        groups.append((ui, d, r, ntile))
            gslot = []
            small_ids = []
            for gi, (ui, d, r, ntile) in enumerate(groups):
                if d == 1:
                    gslot.append((Vbig[ui % NVBIG], BVbig[ui % NVBIG]))
                else:
                    si = len(small_ids)
                    small_ids.append(gi)
                    gslot.append((Vsml[si % NVS], BVsml[si % NVS]))
            small_pos = {gi: si for si, gi in enumerate(small_ids)}
            items = []
            for gi, (ui, d, r, ntile) in enumerate(groups):
                for t in range(ntile):
                    items.append((gi, t))

            def load_unit(ui):
                s, hp = units[ui]
                b = ui % 2
                tr.dma("sp", QT[b], S["qT"][hp * 128:(hp + 1) * 128, s * SEG:(s + 1) * SEG], writes=[BQK[b]], conc=True)
                tr.dma("sp", KT[b], S["kT"][hp * 128:(hp + 1) * 128, s * SEG:s * SEG + SEG + 2 * PAD], writes=[BQK[b]], conc=True)
                tr.dma("sp", xd[ui % 3][64:128], I["xden"][s, 2 * hp:2 * hp + 2, :].partition_broadcast(64), writes=[Bxd[ui % 3]], conc=True)

            def load_group(gi):
                ui, d, r, ntile = groups[gi]
                s, hp = units[ui]
                Vt, BVt = gslot[gi]
                nch = ntile + 1
                R0 = PAD + s * SEG + r - 64 * d
                c0 = 0
                while c0 < nch:
                    n = min(6, nch - c0)
                    base = R0 + d * 128 * c0
                    for e in range(2):
                        src = S["v"][hp, sl_(base, 128 * n, d), e * 64:(e + 1) * 64].rearrange("(c i) f -> i c f", i=128)
                        tr.dma("sp", Vt[:, c0:c0 + n, e, 0:64], src, writes=[BVt], conc=True)
                    c0 += n

            def tab_index(s, d, t, ntile):
                if ntile == 1:
                    return 1 + 2 * ns + s
                if t == 0:
                    return 1 + s
                if t == ntile - 1:
                    return 1 + ns + s
                return 0

            def stageA(ii):
                gi, t = items[ii]
                ui, d, r, ntile = groups[gi]
                b = ui % 2
                sl = ii % 3
                sp_ = ii % 2
                q0 = r + d * 128 * t
                for e in range(2):
                    qs = QT[b][e * 64:(e + 1) * 64, sl_(q0, 128, d)]
                    for c in range(2):
                        k0 = PAD + r + d * (-64 + 128 * (t + c))
                        ks = KT[b][e * 64:(e + 1) * 64, sl_(k0, 128, d)]
                        out = ps[2 * sp_ + e][:, c * 128:(c + 1) * 128]
                        tr.op("pe", lambda out=out, ks=ks, qs=qs: nc.tensor.matmul(out, lhsT=ks, rhs=qs, start=True, stop=True),
                              reads=[BQK[b]], writes=[BpsS[sp_]], inc=(not F_INCA) or (e == 1 and c == 1))

            def stageB(ii):
                gi, t = items[ii]
                ui, d, r, ntile = groups[gi]
                s, hp = units[ui]
                sl = ii % 3
                sp_ = ii % 2
                ti = tab_index(s, d, t, ntile)
                for e in range(2):
                    h = hp * 2 + e
                    cneg = -SLOPES[h] * d
                    tr.op("dve", lambda e=e, cneg=cneg: nc.vector.scalar_tensor_tensor(out=tmp[sl][:, e, :], in0=tabs[:, ti, :], scalar=cneg,
                                                                                       in1=ps[2 * sp_ + e][:, 0:256], op0=ALU.mult, op1=ALU.add),
                          reads=[Bc, BpsS[sp_]], writes=[Btmp[sl]], conc=(e == 1))
                tr.op("act", lambda: nc.scalar.activation(out=PT[sl], in_=tmp[sl], func=AF.Exp), reads=[Btmp[sl]], writes=[BPT[sl]])

            def stageC(ii):
                gi, t = items[ii]
                ui, d, r, ntile = groups[gi]
                b = ui % 2
                sl = ii % 3
                so = ii % 4
                Vt, BVt = gslot[gi]
                po = ps[4 + so][:, 0:256]
                for e in range(2):
                    for c in range(2):
                        tr.op("pe", lambda e=e, c=c: nc.tensor.matmul(po[:, e * 128:(e + 1) * 128], lhsT=Vt[:, t + c, e, :],
                                                                       rhs=PT[sl][:, e, c * 128:(c + 1) * 128], start=(c == 0), stop=(c == 1)),
                              reads=[BVt, BPT[sl]], writes=[BpsO[so]], inc=(not F_INCA) or (e == 1 and c == 1))
                q0 = r + d * 128 * t
                dst = acc[b][:, :, sl_(q0, 128, d)]
                src = po.rearrange("p (e q) -> p e q", e=2)
                if not F_ACC:
                    if d == 1:
                        tr.op("dve", lambda: nc.vector.tensor_copy(out=dst, in_=src), reads=[BpsO[so]], writes=[Bacc[b]])
                    else:
                        tr.op("dve", lambda: nc.vector.tensor_tensor(out=dst, in0=dst, in1=src, op=ALU.add), reads=[BpsO[so], Bacc[b]], writes=[Bacc[b]])
                elif d == 1 and q0 >= 1024:
                    xs = xd[ui % 3][:, :, q0 - 1024:q0 - 1024 + 128]
                    tr.op("dve", lambda: nc.vector.tensor_tensor(out=dst, in0=src, in1=xs, op=ALU.add), reads=[BpsO[so], Bxd[ui % 3]], writes=[Bacc[b]])
                elif d == 1:
                    tr.op("act", lambda: nc.scalar.copy(out=dst, in_=src), reads=[BpsO[so]], writes=[Bacc[b]])
                else:
                    ot = osb[so]
                    tr.op("act", lambda: nc.scalar.copy(out=ot, in_=src), reads=[BpsO[so]], writes=[Bosb[so]])
                    tr.op("pool", lambda: nc.gpsimd.tensor_tensor(out=dst, in0=dst, in1=ot, op=ALU.add), reads=[Bosb[so], Bacc[b]], writes=[Bacc[b]])

            def finalize(ui):
                s, hp = units[ui]
                b = ui % 2
                for e in range(2):
                    tr.op("act", lambda e=e: nc.scalar.activation(out=rb[:, e, :], in_=acc[b][64:128, e, :], func=AF.Ln), reads=[Bacc[b]], writes=[Brb[e]])
                    tr.op("act", lambda e=e: nc.scalar.activation(out=rb[:, e, :], in_=rb[:, e, :], func=AF.Exp, scale=-1.0), reads=[Brb[e]], writes=[Brb[e]])
                for e in range(2):
                    tr.op("pool", lambda e=e: nc.gpsimd.tensor_tensor(out=ob[b][:, e, :], in0=acc[b][0:64, e, :], in1=rb[:, e, :], op=ALU.mult),
                          reads=[Bacc[b], Brb[e]], writes=[Bob[b]], conc=(e == 1))
                    h = hp * 2 + e
                    tr.dma("sp", S["mixT"][h * 64:(h + 1) * 64, s * SEG:(s + 1) * SEG], ob[b][:, e, :], reads=[Bob[b]])

            NI = len(items)
            first_item_of_group = {}
            for ii, (gi, t) in enumerate(items):
                first_item_of_group.setdefault(gi, ii)
            load_unit(0)
            load_group(0)
            for si in range(min(LA, len(small_ids))):
                load_group(small_ids[si])
            last_item_of_unit = {}
            for ii, (gi, t) in enumerate(items):
                last_item_of_unit[groups[gi][0]] = ii

            def pre(ii):
                gi, t = items[ii]
                if first_item_of_group[gi] != ii:
                    return
                ui, d = groups[gi][0], groups[gi][1]
                if d == 1:
                    if ui + 1 < len(units):
                        load_unit(ui + 1)
                else:
                    si = small_pos[gi]
                    if si + LA < len(small_ids):
                        load_group(small_ids[si + LA])
                    if groups[gi - 1][1] == 1 and ui + 1 < len(units):
                        nxt = gi - 1 + sum(n for (_, n, _) in BRANCHES)
                        load_group(nxt)

            for step in range(NI + 2):
                if step < NI:
                    pre(step)
                    stageA(step)
                if 0 <= step - 1 < NI:
                    stageB(step - 1)
                if 0 <= step - 2 < NI:
                    stageC(step - 2)
                    ui = groups[items[step - 2][0]][0]
                    if last_item_of_unit[ui] == step - 2:
                        finalize(ui)

    def phase_gla(self, l, bg=None):
        nc, tr, S, I = self.nc, self.tr, self.S, self.I
        T_ = self.T
        NT = T_ // 128
        ps, psb = self.ps, self.psb
        TPS = SEG // 128
        with ExitStack() as es:
            cst = self.sbt(es, "gcst", [128, 768], F32)
            Bc = Buf("gcst")
            tr.dma("sp", cst, I["consts"][:, C_LTF:C_LTF + 768], writes=[Bc])
            Lb = self.sbt(es, "Lb", [128, 512], BF16)
            tr.op("dve", lambda: nc.vector.tensor_copy(out=Lb, in_=cst[:, 0:512]), reads=[Bc], writes=[Bc])
            LtF, LuF, LtB, LlB = Lb[:, 0:128], Lb[:, 128:256], Lb[:, 256:384], Lb[:, 384:512]
            MF, MB = cst[:, 512:640], cst[:, 640:768]
            wg = [self.sbt(es, "wg", [17, 256], F32) for _ in range(2)]
            wgh = [self.sbt(es, "wgh", [17, 256], BF16) for _ in range(2)]
            wgl = [self.sbt(es, "wgl", [17, 256], BF16) for _ in range(2)]
            tr.dma("sp", wg[0], I["wg_f"][l], writes=[Bc], conc=True)
            tr.dma("sp", wg[1], I["wg_b"][l], writes=[Bc], conc=True)
            for d_ in range(2):
                tr.op("dve", lambda d_=d_: nc.vector.tensor_copy(out=wgh[d_], in_=wg[d_]), reads=[Bc], writes=[Bc])
                tr.op("dve", lambda d_=d_: nc.vector.tensor_tensor(out=wgl[d_], in0=wg[d_], in1=wgh[d_], op=ALU.subtract), reads=[Bc], writes=[Bc])
            gn = self.sbt(es, "gn", [128, 512], F32)
            tr.dma("sp", gn, I["gnorm"][l], writes=[Bc], conc=True)
            eps = self.sbt(es, "eps", [128, 1], F32)
            tr.op("pool", lambda: nc.gpsimd.memset(eps, EPS), writes=[Bc], conc=True)

            NL = 5
            lrh = [self.sbt(es, "lrh", [17, 128], BF16) for _ in range(NL)]
            lrl = [self.sbt(es, "lrl", [17, 128], BF16) for _ in range(NL)]
            qT3 = [self.sbt(es, "qT3", [96, 3, 128], F32) for _ in range(NL)]
            kT3 = [self.sbt(es, "kT3", [96, 3, 128], F32) for _ in range(NL)]
            ktk = [self.sbt(es, "ktk", [128, 256], F32) for _ in range(NL)]
            vtk = [self.sbt(es, "vtk", [128, 512], BF16) for _ in range(NL)]
            rtk = [self.sbt(es, "rtk", [128, 512], F32) for _ in range(NL)]
            oft = [self.sbt(es, "oft", [128, 512], F32) for _ in range(NL)]
            Bld = [Buf("gld") for _ in range(NL)]
            Bld2 = [Buf("gld2") for _ in range(NL)]
            for i in range(NL):
                tr.op("pool", lambda i=i: nc.gpsimd.memset(lrh[i][0:1, :], 1.0), writes=[Bld[i]], conc=True)
                tr.op("pool", lambda i=i: nc.gpsimd.memset(lrl[i][0:1, :], 0.0), writes=[Bld[i]], conc=True)
            e1 = [self.sbt(es, "e1", [128, 256], F32) for _ in range(2)]
            lsp = [self.sbt(es, "lsp", [128, 256], F32) for _ in range(2)]
            Bg1 = [Buf("g1") for _ in range(2)]
            lsh = [self.sbt(es, "lsh", [128, 288], BF16) for _ in range(2)]
            lsl = [self.sbt(es, "lsl", [128, 288], BF16) for _ in range(2)]
            Bls = [Buf("ls") for _ in range(2)]
            for i in range(2):
                tr.op("pool", lambda i=i: nc.gpsimd.memset(lsh[i][:, 256:288], 0.0), writes=[Bls[i]], conc=True)
                tr.op("pool", lambda i=i: nc.gpsimd.memset(lsl[i][:, 256:288], 0.0), writes=[Bls[i]], conc=True)
            E2T = [self.sbt(es, "E2T", [96, 3, 128], F32) for _ in range(2)]
            E3 = [self.sbt(es, "E3", [128, 256], F32) for _ in range(2)]
            Bg2 = [Buf("g2") for _ in range(2)]
            E1T = [self.sbt(es, "E1T", [96, 3, 128], F32) for _ in range(3)]
            qin = [self.sbt(es, "qin", [96, 3, 128], BF16) for _ in range(3)]
            kin = [self.sbt(es, "kin", [96, 3, 128], BF16) for _ in range(3)]
            kst = [self.sbt(es, "kst", [128, 256], BF16) for _ in range(3)]
            Bgo = [Buf("gGo") for _ in range(3)]
            attT = [self.sbt(es, "attT", [128, 8, 128], BF16) for _ in range(2)]
            Batt = [Buf("attT") for _ in range(2)]
            Sst = self.sbt(es, "Sst", [96, 3, 64], F32)
            Sbf = [self.sbt(es, "Sbd", [96, 3, 192], BF16) for _ in range(2)]
            BS = Buf("S")
            BSb = [Buf("Sbf") for _ in range(2)]
            osb = [self.sbt(es, "osb", [128, 512], F32) for _ in range(2)]
            Bosb = [Buf("osb") for _ in range(2)]
            sq = [self.sbt(es, "sq", [128, 512], F32) for _ in range(2)]
            ssg = [self.sbt(es, "ssg", [128, 8], F32) for _ in range(2)]
            rsg = [self.sbt(es, "rsg", [128, 8], F32) for _ in range(2)]
            sil = [self.sbt(es, "sil", [128, 512], F32) for _ in range(2)]
            ogb = [self.sbt(es, "ogb", [128, 512], BF16) for _ in range(2)]
            ogT = [self.sbt(es, "ogT", [128, 4, 128], BF16) for _ in range(2)]
            BogT = [Buf("ogT") for _ in range(2)]
            Bfin = [Buf("gfin") for _ in range(2)]
            Bpz, BpbT, Bpb3, BpA, BpO, BpU = (Buf("pz"), Buf("pbT"), Buf("pb3"), Buf("pA"), Buf("pO"), Buf("pU"))
            BpT = BpU

            def loads(idx, t, bwd):
                sl = idx % NL
                tk = slice(t * 128, (t + 1) * 128)
                tr.dma("sp", lrh[sl][1:17, :], S["lrh"][:, tk], writes=[Bld[sl]], conc=True)
                tr.dma("sp", lrl[sl][1:17, :], S["lrl"][:, tk], writes=[Bld[sl]], conc=True)
                tr.dma("sp", qT3[sl], S["gqT"][:, :, tk].rearrange("j p t -> p j t"), writes=[Bld[sl]], conc=True)
                tr.dma("sp", kT3[sl], S["gkT"][:, :, tk].rearrange("j p t -> p j t"), writes=[Bld[sl]], conc=True)
                tr.dma("sp", ktk[sl], S["gk"][tk, :], writes=[Bld[sl]], conc=True)
                tr.dma("sp", vtk[sl], S["gv"][tk, :], writes=[Bld[sl]], conc=True)
                if bwd:
                    tr.dma("sp", rtk[sl], S["gr"][tk, :], writes=[Bld2[sl]], conc=True)
                    tr.dma("sp", oft[sl], S["of"][tk, :], writes=[Bld2[sl]], conc=True)

            def G1(idx, t, bwd):
                sl, s2 = idx % NL, idx % 2
                d_ = 1 if bwd else 0
                zps = ps[0][:, 0:256]
                tr.op("pe", lambda: nc.tensor.matmul(zps, lhsT=lrh[sl], rhs=wgh[d_], start=True, stop=False), reads=[Bld[sl], Bc], writes=[Bpz], inc=False)
                tr.op("pe", lambda: nc.tensor.matmul(zps, lhsT=lrh[sl], rhs=wgl[d_], start=False, stop=False), reads=[Bld[sl], Bc], writes=[Bpz], inc=False)
                tr.op("pe", lambda: nc.tensor.matmul(zps, lhsT=lrl[sl], rhs=wgh[d_], start=False, stop=True), reads=[Bld[sl], Bc], writes=[Bpz])
                tr.op("act", lambda: nc.scalar.activation(out=e1[s2], in_=zps, func=AF.Exp, scale=-1.0), reads=[Bpz], writes=[Bg1[s2]])
                tr.op("act", lambda: nc.scalar.activation(out=lsp[s2], in_=e1[s2], func=AF.Ln, bias=1.0), reads=[Bg1[s2]], writes=[Bg1[s2]])
                tr.op("act", lambda: nc.scalar.copy(out=lsh[s2][:, 0:256], in_=lsp[s2]), reads=[Bg1[s2]], writes=[Bls[s2]])
                tr.op("dve", lambda: nc.vector.tensor_tensor(out=lsl[s2][:, 0:256], in0=lsp[s2], in1=lsh[s2][:, 0:256], op=ALU.subtract),
                      reads=[Bg1[s2], Bls[s2]], writes=[Bls[s2]], conc=True)

            def G1b(idx, t, bwd):
                pass

            def G2(idx, t, bwd):
                sl, s2, s3 = idx % NL, idx % 2, idx % 3
                Ltri = LtB if bwd else LtF
                Lrest = LlB if bwd else LuF
                bTps = ps[1][0:96, 0:384].rearrange("p (j t) -> p j t", j=3)
                b3ps = ps[7][:, 0:256]
                for j in range(3):
                    c = 96 * j
                    o_ap = ps[1][0:96, j * 128:(j + 1) * 128]
                    tr.op("pe", lambda c=c, o_ap=o_ap: nc.tensor.matmul(o_ap, lhsT=lsh[s2][:, c:c + 96], rhs=Ltri, start=True, stop=False),
                          reads=[Bls[s2], Bc], writes=[BpbT], inc=False)
                    tr.op("pe", lambda c=c, o_ap=o_ap: nc.tensor.matmul(o_ap, lhsT=lsl[s2][:, c:c + 96], rhs=Ltri, start=False, stop=True),
                          reads=[Bls[s2], Bc], writes=[BpbT], inc=(j == 2))
                tr.op("pe", lambda: nc.tensor.matmul(b3ps, lhsT=Lrest, rhs=lsh[s2][:, 0:256], start=True, stop=False), reads=[Bls[s2], Bc], writes=[Bpb3], inc=False)
                tr.op("pe", lambda: nc.tensor.matmul(b3ps, lhsT=Lrest, rhs=lsl[s2][:, 0:256], start=False, stop=True), reads=[Bls[s2], Bc], writes=[Bpb3])
                tr.op("act", lambda: nc.scalar.activation(out=E1T[s3], in_=bTps, func=AF.Exp), reads=[BpbT], writes=[Bgo[s3]])
                tr.op("act", lambda: nc.scalar.activation(out=E2T[s2], in_=bTps, func=AF.Exp, scale=-1.0), reads=[BpbT], writes=[Bg2[s2]])
                tr.op("act", lambda: nc.scalar.activation(out=E3[s2], in_=b3ps, func=AF.Exp), reads=[Bpb3], writes=[Bg2[s2]], conc=True)
                tr.op("dve", lambda: nc.vector.scalar_tensor_tensor(out=qin[s3], in0=qT3[sl], scalar=32.0 ** -0.5, in1=E1T[s3], op0=ALU.mult, op1=ALU.mult),
                      reads=[Bld[sl], Bgo[s3]], writes=[Bgo[s3]], conc=True)
                tr.op("pool", lambda: nc.gpsimd.tensor_tensor(out=kin[s3], in0=kT3[sl], in1=E2T[s2], op=ALU.mult), reads=[Bld[sl], Bg2[s2]], writes=[Bgo[s3]], conc=True)
                tr.op("dve", lambda: nc.vector.tensor_tensor(out=kst[s3], in0=ktk[sl], in1=E3[s2], op=ALU.mult), reads=[Bld[sl], Bg2[s2]], writes=[Bgo[s3]], conc=True)

            def H1(idx, t, bwd):
                s3, a = idx % 3, idx % 2
                M = MB if bwd else MF
                abank = (2, 3, 6)
                for h in range(8):
                    j, rt = h // 3, h % 3
                    p0 = 32 * rt
                    out = ps[abank[rt]][:, j * 128:(j + 1) * 128]
                    tr.op("pe", lambda out=out, j=j, p0=p0: nc.tensor.matmul(out, lhsT=kin[s3][p0:p0 + 32, j, :], rhs=qin[s3][p0:p0 + 32, j, :], start=True, stop=True),
                          reads=[Bgo[s3]], writes=[BpA], inc=(h == 7))
                for rt in range(3):
                    nh = 3 if rt < 2 else 2
                    src = ps[abank[rt]][:, 0:nh * 128].rearrange("p (h q) -> p h q", h=nh)
                    mk = M.unsqueeze(1).to_broadcast([128, nh, 128])
                    dst = attT[a][:, sl_(rt, nh, 3), :]
                    tr.op("dve", lambda src=src, mk=mk, dst=dst: nc.vector.tensor_tensor(out=dst, in0=src, in1=mk, op=ALU.mult),
                          reads=[BpA, Bc], writes=[Batt[a]], conc=True)

            def H2(idx, t, bwd):
                sl, s3, a = idx % NL, idx % 3, idx % 2
                sb_r, sb_w = Sbf[idx % 2], Sbf[(idx + 1) % 2]
                Bsb_r, Bsb_w = BSb[idx % 2], BSb[(idx + 1) % 2]
                seg = t // TPS
                if bwd:
                    entering = (t % TPS == TPS - 1) and t != NT - 1
                    fl = self.flg[0:96, seg + 1:seg + 2]
                else:
                    entering = (t % TPS == 0) and t != 0
                    fl = self.flg[0:96, seg:seg + 1]
                def cast_state(dst, Bdst):
                    for h3 in range(3):
                        p0 = 32 * h3
                        tr.op("pool", lambda p0=p0, h3=h3: nc.gpsimd.tensor_copy(out=dst[p0:p0 + 32, :, 64 * h3:64 * h3 + 64], in_=Sst[p0:p0 + 32, :, :]),
                              reads=[BS], writes=[Bdst], conc=(h3 > 0))

                if entering:
                    tr.op("dve", lambda: nc.vector.tensor_scalar_mul(out=Sst, in0=Sst, scalar1=fl), reads=[BS, self.B_const], writes=[BS])
                    cast_state(sb_r, Bsb_r)
                for h in range(8):
                    j, p0 = h // 3, 32 * (h % 3)
                    out = ps[5][p0:p0 + 32, j * 64:(j + 1) * 64]
                    tr.op("pe", lambda out=out, h=h: nc.tensor.matmul(out, lhsT=kst[s3][:, h * 32:(h + 1) * 32], rhs=vtk[sl][:, h * 64:(h + 1) * 64], start=True, stop=True),
                          reads=[Bgo[s3], Bld[sl]], writes=[BpU], inc=(h == 7))
                for j in range(3):
                    nh = 3 if j < 2 else 2
                    outj = ps[4][:, j * 192:j * 192 + nh * 64]
                    tr.op("pe", lambda j=j, nh=nh, outj=outj: nc.tensor.matmul(outj, lhsT=qin[s3][0:GT[j], j, :], rhs=sb_r[0:GT[j], j, 0:nh * 64], start=True, stop=False),
                          reads=[Bgo[s3], Bsb_r], writes=[BpO], inc=False)
                    for hh in range(nh):
                        h = 3 * j + hh
                        out = ps[4][:, h * 64:(h + 1) * 64]
                        tr.op("pe", lambda out=out, h=h: nc.tensor.matmul(out, lhsT=attT[a][:, h, :], rhs=vtk[sl][:, h * 64:(h + 1) * 64], start=False, stop=(hh == nh - 1)),
                              reads=[Batt[a], Bld[sl]], writes=[BpO], inc=(h == 7))
                col = 0 if bwd else 127
                for j in range(3):
                    Mj = GT[j]
                    tr.op("dve", lambda j=j, Mj=Mj: nc.vector.scalar_tensor_tensor(out=Sst[0:Mj, j, :], in0=Sst[0:Mj, j, :], scalar=E1T[s3][0:Mj, j, col:col + 1],
                                                                                   in1=ps[5][0:Mj, j * 64:(j + 1) * 64], op0=ALU.mult, op1=ALU.add),
                          reads=[BS, Bgo[s3], BpU], writes=[BS])
                cast_state(sb_w, Bsb_w)
                tk = slice(t * 128, (t + 1) * 128)
                if not bwd:
                    tr.op("act", lambda: nc.scalar.copy(out=osb[a], in_=ps[4]), reads=[BpO], writes=[Bosb[a]])
                    tr.dma("sp", S["of"][tk, :], osb[a], reads=[Bosb[a]])
                else:
                    o = osb[a]
                    o3 = o.rearrange("p (h d) -> p h d", h=8)
                    Bf = Bfin[a]
                    tr.op("dve", lambda: nc.vector.tensor_tensor(out=o, in0=oft[sl], in1=ps[4], op=ALU.add), reads=[BpO, Bld2[sl]], writes=[Bosb[a]])
                    tr.op("pool", lambda: nc.gpsimd.tensor_tensor(out=sq[a], in0=o, in1=o, op=ALU.mult), reads=[Bosb[a]], writes=[Bf])
                    tr.op("dve", lambda: nc.vector.tensor_reduce(out=ssg[a], in_=sq[a].rearrange("p (h d) -> p h d", h=8), axis=AX.X, op=ALU.add), reads=[Bf], writes=[Bf])
                    tr.op("act", lambda: nc.scalar.activation(out=rsg[a], in_=ssg[a], func=AF.Ln, scale=1.0 / 64, bias=eps), reads=[Bf, Bc], writes=[Bf])
                    tr.op("act", lambda: nc.scalar.activation(out=rsg[a], in_=rsg[a], func=AF.Exp, scale=-0.5), reads=[Bf], writes=[Bf])
                    tr.op("act", lambda: nc.scalar.activation(out=sil[a], in_=rtk[sl], func=AF.Exp, scale=-1.0), reads=[Bld2[sl], Bf], writes=[Bf])
                    tr.op("act", lambda: nc.scalar.activation(out=sil[a], in_=sil[a], func=AF.Ln, bias=1.0), reads=[Bf], writes=[Bf])
                    tr.op("act", lambda: nc.scalar.activation(out=sil[a], in_=sil[a], func=AF.Exp, scale=-1.0), reads=[Bf], writes=[Bf])
                    tr.op("pool", lambda: nc.gpsimd.tensor_tensor(out=sil[a], in0=sil[a], in1=rtk[sl], op=ALU.mult), reads=[Bf, Bld2[sl]], writes=[Bf])
                    tr.op("pool", lambda: nc.gpsimd.tensor_tensor(out=sil[a], in0=sil[a], in1=gn, op=ALU.mult), reads=[Bf, Bc], writes=[Bf])
                    tr.op("dve", lambda: nc.vector.tensor_tensor(out=o3, in0=o3, in1=rsg[a].unsqueeze(2).to_broadcast([128, 8, 64]), op=ALU.mult), reads=[Bf, Bosb[a]], writes=[Bosb[a]])
                    tr.op("pool", lambda: nc.gpsimd.tensor_tensor(out=ogb[a], in0=o, in1=sil[a], op=ALU.mult), reads=[Bosb[a], Bf], writes=[Bf])
                    pt = psb[5][:, 512:1024]
                    for c in range(4):
                        tr.op("pe", lambda c=c: nc.tensor.transpose(out=pt[:, c * 128:(c + 1) * 128], in_=ogb[a][:, c * 128:(c + 1) * 128], identity=self.ident),
                              reads=[Bf, self.B_const], writes=[BpT], inc=(c == 3))
                    tr.op("act", lambda: nc.scalar.copy(out=ogT[a], in_=pt.rearrange("p (c t) -> p c t", c=4)), reads=[BpT], writes=[BogT[a]])
                    tr.dma("sp", S["mixT"][512:1024, tk].rearrange("(c p) t -> p c t", p=128), ogT[a], reads=[BogT[a]])

            for bwd in (False, True):
                order = list(range(NT))
                if bwd:
                    order = order[::-1]
                    tr.barrier()
                tr.op("dve", lambda: nc.vector.memset(Sst, 0.0), reads=[BS], writes=[BS])
                tr.op("pool", lambda: nc.gpsimd.memset(Sbf[0], 0.0), reads=[BSb[0]], writes=[BSb[0]])
                tr.op("pool", lambda: nc.gpsimd.memset(Sbf[1], 0.0), reads=[BSb[1]], writes=[BSb[1]])
                for i in range(min(NL, NT)):
                    loads(i, order[i], bwd)
                for it in range(-3, NT):
                    if bg is not None:
                        next(bg, None)
                    if 0 <= it + 3 < NT:
                        G1(it + 3, order[it + 3], bwd)
                    if 0 <= it + 2 < NT:
                        G2(it + 2, order[it + 2], bwd)
                    if 0 <= it < NT:
                        H2(it, order[it], bwd)
                    if 0 <= it + 1 < NT:
                        H1(it + 1, order[it + 1], bwd)
                    if 0 <= it + 3 < NT:
                        G1b(it + 3, order[it + 3], bwd)
                    if it >= 0 and it + NL < NT:
                        loads(it + NL, order[it + NL], bwd)

    def phase_outproj(self, l, x_src):
        nc, tr, S, I = self.nc, self.tr, self.S, self.I
        T_ = self.T
        NG = T_ // 512
        NTL = T_ // 128
        ps, psb = self.ps, self.psb
        with ExitStack() as es:
            wbf, Bw = self.load_weight(es, I["w_out"][l], DM, DM, None, "wout")
            npb = self.sbt(es, "npb", [128, DM], F32)
            Bc = Buf("c")
            tr.dma("sp", npb, I["n_post"][l], writes=[Bc])
            eps = self.sbt(es, "eps", [128, 1], F32)
            tr.op("pool", lambda: nc.gpsimd.memset(eps, EPS), writes=[Bc], conc=True)
            mix = [self.sbt(es, "mix", [128, 8, 512], BF16) for _ in range(3)]
            Bmix = [Buf("mix") for _ in range(3)]
            NX = 6
            xt = [self.sbt(es, "xt", [128, DM], F32) for _ in range(NX)]
            Bxt = [Buf("xt") for _ in range(NX)]
            t1 = [self.sbt(es, "t1", [128, DM], F32) for _ in range(2)]
            Bt1 = [Buf("t1") for _ in range(2)]
            junk2 = [self.sbt(es, "junk", [128, DM], BF16) for _ in range(2)]
            Bj2 = [Buf("junk") for _ in range(2)]
            ss = [self.sbt(es, "ss", [128, 4], F32) for _ in range(4)]
            rs = [self.sbt(es, "rs", [128, 2], F32) for _ in range(4)]
            Bss = [Buf("ss") for _ in range(4)]
            Brs = [Buf("rs") for _ in range(4)]
            xn = [self.sbt(es, "xn", [128, DM], BF16) for _ in range(2)]
            Bxn = [Buf("xn") for _ in range(2)]
            xT = [self.sbt(es, "xT", [128, 8, 128], BF16) for _ in range(2)]
            BxT = [Buf("xT") for _ in range(2)]
            Bpy = [Buf("py") for _ in range(3)]
            Bpt = [Buf("pt") for _ in range(2)]

            def load_g(g):
                tr.dma("sp", mix[g % 3], S["mixT"][:, g * 512:(g + 1) * 512].rearrange("(kc p) t -> p kc t", p=128), writes=[Bmix[g % 3]])

            def load_x(i):
                tr.dma("sp", xt[i % NX], x_src[i * 128:(i + 1) * 128, :], writes=[Bxt[i % NX]])

            def stA(i):
                g, j = i // 4, i % 4
                if j == 0 and g + 1 < NG:
                    load_g(g + 1)
                p = i % 3
                py = [ps[2 * p], ps[2 * p + 1]]
                for n in range(2):
                    for kc in range(8):
                        tr.op("pe", lambda n=n, kc=kc: nc.tensor.matmul(py[n], lhsT=mix[g % 3][:, kc, j * 128:(j + 1) * 128], rhs=wbf[:, kc, n * 512:(n + 1) * 512],
                                                                        start=(kc == 0), stop=(kc == 7)), reads=[Bmix[g % 3], Bw], writes=[Bpy[p]],
                              inc=(kc == 7 and n == 1))

            def stB(i):
                p, q, a = i % 3, i % 4, i % 2
                py = [ps[2 * p], ps[2 * p + 1]]
                for n in range(2):
                    tr.op("act", lambda n=n: nc.scalar.activation(out=junk2[a][:, n * 512:(n + 1) * 512], in_=py[n], func=AF.Square, accum_out=ss[q][:, n:n + 1]),
                          reads=[Bpy[p]], writes=[Bj2[a], Bss[q]], conc=True)
                tr.op("dve", lambda: nc.vector.tensor_tensor(out=ss[q][:, 3:4], in0=ss[q][:, 0:1], in1=ss[q][:, 1:2], op=ALU.add), reads=[Bss[q]], writes=[Brs[q]])
                tr.op("act", lambda: nc.scalar.activation(out=rs[q][:, 0:1], in_=ss[q][:, 3:4], func=AF.Ln, scale=1.0 / DM, bias=eps), reads=[Brs[q], Bc], writes=[Brs[q]])
                tr.op("act", lambda: nc.scalar.activation(out=rs[q][:, 0:1], in_=rs[q][:, 0:1], func=AF.Exp, scale=-0.5), reads=[Brs[q]], writes=[Brs[q]])

            def stC(i):
                p, q, a, xs_ = i % 3, i % 4, i % 2, i % NX
                py = [ps[2 * p], ps[2 * p + 1]]
                for n in range(2):
                    tr.op("act", lambda n=n: nc.scalar.activation(out=t1[a][:, n * 512:(n + 1) * 512], in_=py[n], func=AF.Copy, scale=rs[q][:, 0:1]),
                          reads=[Bpy[p], Brs[q]], writes=[Bt1[a]], conc=True)
                tr.op("dve", lambda: nc.vector.tensor_tensor(out=t1[a], in0=t1[a], in1=npb, op=ALU.mult), reads=[Bt1[a], Bc], writes=[Bt1[a]])
                tr.op("pool", lambda: nc.gpsimd.tensor_tensor(out=xt[xs_], in0=xt[xs_], in1=t1[a], op=ALU.add), reads=[Bt1[a], Bxt[xs_]], writes=[Bxt[xs_]])
                tr.dma("sp", S["x1"][i * 128:(i + 1) * 128, :], xt[xs_], reads=[Bxt[xs_]])

            def stD(i):
                q, a, xs_ = i % 4, i % 2, i % NX
                tr.op("act", lambda: nc.scalar.activation(out=junk2[a], in_=xt[xs_], func=AF.Square, accum_out=ss[q][:, 2:3]), reads=[Bxt[xs_]], writes=[Bj2[a], Bss[q]])
                tr.op("act", lambda: nc.scalar.activation(out=rs[q][:, 1:2], in_=ss[q][:, 2:3], func=AF.Ln, scale=1.0 / DM, bias=eps), reads=[Bss[q], Bc], writes=[Brs[q]])
                tr.op("act", lambda: nc.scalar.activation(out=rs[q][:, 1:2], in_=rs[q][:, 1:2], func=AF.Exp, scale=-0.5), reads=[Brs[q]], writes=[Brs[q]])
                tr.op("dve", lambda: nc.vector.tensor_scalar_mul(out=xn[a], in0=xt[xs_], scalar1=rs[q][:, 1:2]), reads=[Bxt[xs_], Brs[q]], writes=[Bxn[a]])

            def stE(i):
                a = i % 2
                pt = psb[6 + a]
                for kc in range(8):
                    tr.op("pe", lambda kc=kc: nc.tensor.transpose(out=pt[:, kc * 128:(kc + 1) * 128], in_=xn[a][:, kc * 128:(kc + 1) * 128], identity=self.ident),
                          reads=[Bxn[a], self.B_const], writes=[Bpt[a]], inc=(kc == 7))
                tr.op("dve", lambda: nc.vector.tensor_copy(out=xT[a], in_=pt.rearrange("p (kc t) -> p kc t", kc=8)), reads=[Bpt[a]], writes=[BxT[a]])
                tr.dma("sp", S["xT2"][:, 1 + i * 128:1 + (i + 1) * 128].rearrange("(kc p) t -> p kc t", p=128), xT[a], reads=[BxT[a]])

            load_g(0)
            load_x(0)
            for it in range(-4, NTL):
                if 0 <= it < NTL:
                    stE(it)
                if 0 <= it + 1 < NTL:
                    stD(it + 1)
                if 0 <= it + 2 < NTL:
                    stC(it + 2)
                if 0 <= it + 3 < NTL:
                    stB(it + 3)
                if 0 <= it + 4 < NTL:
                    stA(it + 4)
                if 1 <= it + 5 < NTL:
                    load_x(it + 5)

    def phase_ffn(self, l, x_dst, wup_pre=None):
        nc, tr, S, I = self.nc, self.tr, self.S, self.I
        T_ = self.T
        NU = T_ // 256
        UPS = SEG // 256
        ps = self.ps
        with ExitStack() as es:
            if wup_pre is not None:
                wup = wup_pre
            else:
                wup, Bwu = self.load_weight(es, I["w_up"][l], DM, 2 * DFF, I["g_fpre"][l], "wup")
            wdn, Bwd = self.load_weight(es, I["w_down"][l], DFF, DM, None, "wdn")
            Bw = Buf("w")
            npb = self.sbt(es, "npb", [128, DM], F32)
            Bc = Buf("c")
            tr.dma("sp", npb, I["n_fpost"][l], writes=[Bc])
            cw = self.sbt(es, "cw", [128, NFT, 4], F32)
            tr.dma("sp", cw, I["cw"][l].rearrange("p (f c) -> p f c", c=4), writes=[Bc])
            eps = self.sbt(es, "eps", [128, 1], F32)
            tr.op("pool", lambda: nc.gpsimd.memset(eps, EPS), writes=[Bc])
            xu = [self.sbt(es, "xu", [128, 8, 258], BF16) for _ in range(2)]
            Bxu = [Buf("xu") for _ in range(2)]
            fT = self.sbt(es, "fT", [128, NFT, 256], BF16)
            BfT = Buf("fT")
            NC3 = 5
            c1 = [self.sbt(es, "c1", [128, 256], F32) for _ in range(NC3)]
            ge = [self.sbt(es, "ge", [128, 256], F32) for _ in range(NC3)]
            Bc1 = [Buf("c1") for _ in range(NC3)]
            Bge = [Buf("ge") for _ in range(NC3)]
            asb = [self.sbt(es, "asb", [128, 258], F32) for _ in range(NC3)]
            gsb = [self.sbt(es, "gsb", [128, 256], F32) for _ in range(NC3)]
            Basb = [Buf("asb") for _ in range(NC3)]
            Bgsb = [Buf("gsb") for _ in range(NC3)]
            t1 = [self.sbt(es, "t1", [128, DM], F32) for _ in range(2)]
            Bt1 = [Buf("t1") for _ in range(2)]
            x1t = [self.sbt(es, "x1t", [128, DM], F32) for _ in range(2)]
            Bx1 = [Buf("x1t") for _ in range(2)]
            junk2 = [self.sbt(es, "junk", [128, DM], BF16) for _ in range(2)]
            Bj2 = [Buf("junk") for _ in range(2)]
            ss = [self.sbt(es, "ss", [128, 2], F32) for _ in range(2)]
            rs = [self.sbt(es, "rs", [128, 1], F32) for _ in range(2)]
            Bss = [Buf("ss") for _ in range(2)]
            BpA = [Buf("pA") for _ in range(2)]
            BpG = [Buf("pG") for _ in range(2)]
            Bpy = [Buf("py") for _ in range(2)]
            cnt = {"f": 0, "tt": 0}

            def load_u(u):
                b = u % 2
                tr.dma("sp", xu[b], S["xT2"][:, u * 256:u * 256 + 258].rearrange("(kc p) t -> p kc t", p=128), writes=[Bxu[b]])
                seg = u // UPS
                if u % UPS == 0:
                    fl = self.flg[:, seg:seg + 1]
                    tr.op("pool", lambda: nc.gpsimd.tensor_scalar_mul(out=xu[b][:, :, 0:1], in0=xu[b][:, :, 0:1], scalar1=fl), reads=[Bxu[b], self.B_const], writes=[Bxu[b]])
                if u % UPS == UPS - 1:
                    fl = self.flg[:, seg + 1:seg + 2]
                    tr.op("pool", lambda: nc.gpsimd.tensor_scalar_mul(out=xu[b][:, :, 257:258], in0=xu[b][:, :, 257:258], scalar1=fl), reads=[Bxu[b], self.B_const], writes=[Bxu[b]])

            def up_mm(u, f):
                b = u % 2
                k = cnt["f"] % 2
                pa = ps[k][:, 0:258]
                pg = ps[2 if k == 0 else 7][:, 0:256]
                for kc in range(8):
                    tr.op("pe", lambda kc=kc: nc.tensor.matmul(pa, lhsT=wup[:, kc, f * 128:(f + 1) * 128], rhs=xu[b][:, kc, :], start=(kc == 0), stop=(kc == 7)),
                          reads=[Bxu[b]], writes=[BpA[k]], inc=(kc == 7))
                for kc in range(8):
                    tr.op("pe", lambda kc=kc: nc.tensor.matmul(pg, lhsT=wup[:, kc, DFF + f * 128:DFF + (f + 1) * 128], rhs=xu[b][:, kc, 1:257], start=(kc == 0), stop=(kc == 7)),
                          reads=[Bxu[b]], writes=[BpG[k]], inc=(kc == 7))
                cnt["f"] += 1
                return k

            def ew1(u, f, k):
                pa = ps[k][:, 0:258]
                pg = ps[2 if k == 0 else 7][:, 0:256]
                c = (u * NFT + f) % NC3
                tr.op("act", lambda: nc.scalar.copy(out=asb[c], in_=pa), reads=[BpA[k]], writes=[Basb[c]])
                tr.op("act", lambda: nc.scalar.copy(out=gsb[c], in_=pg), reads=[BpG[k]], writes=[Bgsb[c]])
                tr.op("pool", lambda: nc.gpsimd.tensor_scalar(out=c1[c], in0=asb[c][:, 1:257], scalar1=cw[:, f, 1:2], scalar2=cw[:, f, 3:4], op0=ALU.mult, op1=ALU.add),
                      reads=[Basb[c], Bc], writes=[Bc1[c]])

            def ew2(u, f):
                c = (u * NFT + f) % NC3
                tr.op("dve", lambda: nc.vector.scalar_tensor_tensor(out=c1[c], in0=asb[c][:, 0:256], scalar=cw[:, f, 0:1], in1=c1[c], op0=ALU.mult, op1=ALU.add),
                      reads=[Basb[c], Bc, Bc1[c]], writes=[Bc1[c]])
                tr.op("dve", lambda: nc.vector.scalar_tensor_tensor(out=c1[c], in0=asb[c][:, 2:258], scalar=cw[:, f, 2:3], in1=c1[c], op0=ALU.mult, op1=ALU.add),
                      reads=[Basb[c], Bc, Bc1[c]], writes=[Bc1[c]])

            def ew3(u, f):
                c = (u * NFT + f) % NC3
                tr.op("act", lambda: nc.scalar.activation(out=ge[c], in_=c1[c], func=AF.Gelu_apprx_tanh), reads=[Bc1[c]], writes=[Bge[c]])
                tr.op("pool", lambda: nc.gpsimd.tensor_tensor(out=fT[:, f, :], in0=ge[c], in1=gsb[c], op=ALU.mult), reads=[Bge[c], Bgsb[c]], writes=[BfT], conc=True)

            def down(u):
                for j in range(2):
                    i = u * 2 + j
                    a = i % 2
                    py = [ps[3 + 2 * a], ps[4 + 2 * a]]
                    tr.dma("sp", x1t[a], S["x1"][i * 128:(i + 1) * 128, :], writes=[Bx1[a]])
                    for n in range(2):
                        for f in range(NFT):
                            tr.op("pe", lambda n=n, f=f: nc.tensor.matmul(py[n], lhsT=fT[:, f, j * 128:(j + 1) * 128], rhs=wdn[:, f, n * 512:(n + 1) * 512],
                                                                          start=(f == 0), stop=(f == NFT - 1)), reads=[BfT], writes=[Bpy[a]],
                                  inc=(f == NFT - 1 and n == 1))
                    for n in range(2):
                        tr.op("act", lambda n=n: nc.scalar.activation(out=junk2[a][:, n * 512:(n + 1) * 512], in_=py[n], func=AF.Square, accum_out=ss[a][:, n:n + 1]),
                              reads=[Bpy[a]], writes=[Bj2[a], Bss[a]])
                    tr.op("dve", lambda: nc.vector.tensor_tensor(out=rs[a], in0=ss[a][:, 0:1], in1=ss[a][:, 1:2], op=ALU.add), reads=[Bss[a]], writes=[Bss[a]])
                    tr.op("act", lambda: nc.scalar.activation(out=rs[a], in_=rs[a], func=AF.Ln, scale=1.0 / DM, bias=eps), reads=[Bss[a], Bc], writes=[Bss[a]])
                    tr.op("act", lambda: nc.scalar.activation(out=rs[a], in_=rs[a], func=AF.Exp, scale=-0.5), reads=[Bss[a]], writes=[Bss[a]])
                    for n in range(2):
                        tr.op("act", lambda n=n: nc.scalar.activation(out=t1[a][:, n * 512:(n + 1) * 512], in_=py[n], func=AF.Copy, scale=rs[a][:, 0:1]),
                              reads=[Bpy[a], Bss[a]], writes=[Bt1[a]])
                    tr.op("pool", lambda: nc.gpsimd.tensor_tensor(out=t1[a], in0=t1[a], in1=npb, op=ALU.mult), reads=[Bt1[a], Bc], writes=[Bt1[a]])
                    tr.op("pool", lambda: nc.gpsimd.tensor_tensor(out=x1t[a], in0=x1t[a], in1=t1[a], op=ALU.add), reads=[Bt1[a], Bx1[a]], writes=[Bx1[a]])
                    tr.dma("sp", x_dst[i * 128:(i + 1) * 128, :], x1t[a], reads=[Bx1[a]])

            load_u(0)
            if NU > 1:
                load_u(1)
            k = up_mm(0, 0)
            ew1(0, 0, k)
            k = up_mm(0, 1)
            ew1(0, 1, k)
            ew2(0, 0)
            for u in range(NU):
                for f in range(2, NFT):
                    k = up_mm(u, f)
                    ew1(u, f, k)
                    ew2(u, f - 1)
                    ew3(u, f - 2)
                nxt = u + 1 < NU
                if nxt:
                    k = up_mm(u + 1, 0)
                    ew1(u + 1, 0, k)
                ew2(u, NFT - 1)
                ew3(u, NFT - 2)
                if nxt:
                    k = up_mm(u + 1, 1)
                    ew1(u + 1, 1, k)
                ew3(u, NFT - 1)
                if nxt:
                    ew2(u + 1, 0)
                down(u)
                if u + 2 < NU:
                    load_u(u + 2)


def make_consts():
    c = np.zeros((128, C_END), np.float32)
    k = np.arange(128)[:, None]
    q = np.arange(128)[None, :]
    dA = np.abs(k - 64 - q).astype(np.float32)
    dB = np.abs(k + 64 - q).astype(np.float32)
    c[:, C_DA:C_DA + 128] = np.where(dA <= 64, dA, BIG)
    c[:, C_DB:C_DB + 128] = np.where(dB <= 64, dB, BIG)
    j, i = k, q
    g = -1.0 / 16.0
    c[:, C_LTF:C_LTF + 128] = np.where(j <= i, g, 0.0)
    c[:, C_LUF:C_LUF + 128] = np.where(j > i, g, 0.0)
    c[:, C_LTB:C_LTB + 128] = np.where(j >= i, g, 0.0)
    c[:, C_LLB:C_LLB + 128] = np.where(j < i, g, 0.0)
    c[:, C_MF:C_MF + 128] = np.where(j <= i, 1.0, 0.0)
    c[:, C_MB:C_MB + 128] = np.where(j >= i, 1.0, 0.0)
    c[:, C_ID:C_ID + 128] = np.eye(128, dtype=np.float32)
    return c


def core_layout(nseg=3):
    segs, links = [], []
    for c in range(NCORES):
        if c < 4:
            segs.append([("s", c, 0), ("s", c, 1), ("p", c, 0)])
            links.append([0.0, 1.0, 0.0, 0.0])
        else:
            b0 = 4 + 3 * (c - 4)
            segs.append([("p", b0, 0), ("p", b0 + 1, 0), ("p", b0 + 2, 0)])
            links.append([0.0, 0.0, 0.0, 0.0])
    return segs, links


def host_params(inp, L=2):
    f = lambda a: np.ascontiguousarray(a, dtype=np.float32)
    P = {}
    P["consts"] = make_consts()
    P["w_in"] = f(inp["w_in"])
    P["w_out"] = f(inp["w_out"])
    P["w_up"] = f(inp["w_up"])
    P["w_down"] = f(inp["w_down"])
    P["g_pre"] = f(inp["norm_mix_pre"].reshape(L, 8, 128).transpose(0, 2, 1))
    P["g_fpre"] = f(inp["norm_ffn_pre"].reshape(L, 8, 128).transpose(0, 2, 1))
    P["n_post"] = f(np.broadcast_to(inp["norm_mix_post"][:, None, :], (L, 128, DM)))
    P["n_fpost"] = f(np.broadcast_to(inp["norm_ffn_post"][:, None, :], (L, 128, DM)))
    P["gnorm"] = f(np.broadcast_to(np.tile(inp["gla_norm"], (1, 8))[:, None, :], (L, 128, 512)))
    P["wg_f"] = f(np.concatenate([inp["b_gate_fwd"][:, None, :], inp["w_gate_fwd"]], axis=1))
    P["wg_b"] = f(np.concatenate([inp["b_gate_bwd"][:, None, :], inp["w_gate_bwd"]], axis=1))
    cw = np.concatenate([inp["conv_w"], inp["conv_b"][:, None, :]], axis=1)
    P["cw"] = f(cw.reshape(L, 4, NFT, 128).transpose(0, 3, 2, 1).reshape(L, 128, NFT * 4))
    return P


def flag_arrays(link):
    n = len(link)
    ns = n - 1
    flg = np.broadcast_to(np.asarray(link, np.float32)[None, :], (128, n)).copy()
    c = make_consts()
    DA = c[:, C_DA:C_DA + 128]
    DB = c[:, C_DB:C_DB + 128]
    kk = np.arange(128)[:, None]
    qq = np.arange(128)[None, :]
    DA_first = np.where((kk < 64) | ((qq < 64) & (kk - 64 - qq < 0)), BIG, DA).astype(np.float32)
    DB_last = np.where(kk >= 64, BIG, DB).astype(np.float32)
    tabs = np.zeros((128, 1 + 3 * ns, 256), np.float32)
    tabs[:, 0, :128], tabs[:, 0, 128:] = DA, DB
    for s in range(ns):
        A = DA if link[s] else DA_first
        B = DB if link[s + 1] else DB_last
        tabs[:, 1 + s, :128], tabs[:, 1 + s, 128:] = A, DB
        tabs[:, 1 + ns + s, :128], tabs[:, 1 + ns + s, 128:] = DA, B
        tabs[:, 1 + 2 * ns + s, :128], tabs[:, 1 + 2 * ns + s, 128:] = A, B
    X = np.zeros((8, 1024), np.float64)
    for h in range(8):
        for (d, nres, ntile) in BRANCHES:
            for j in range(1024):
                qi = (1024 + j) // d - SEG // d + 64
                if qi > 0:
                    X[h, j] += np.exp(-SLOPES[h] * d * np.arange(64 - qi, 64, dtype=np.float64)).sum()
    xden = np.zeros((ns, 8, 1024), np.float32)
    for s in range(ns):
        xden[s] = X * (1.0 - link[s + 1])
    return flg, tabs.reshape(128, -1), xden


_CACHE = {}


def kernel(**inputs):
    inp = {k: np.asarray(v) for k, v in inputs.items()}
    xp, xs = inp["x_prompt"], inp["x_sample"]
    segs, links = core_layout()
    P = host_params(inp)
    in_maps = []
    for c in range(NCORES):
        parts = []
        for (grp, b, half) in segs[c]:
            parts.append(xs[b, half * SEG:(half + 1) * SEG] if grp == "s" else xp[b])
        m = dict(P)
        m["x"] = np.ascontiguousarray(np.concatenate(parts, axis=0), dtype=np.float32)
        m["flg"], m["atab"], m["xden"] = flag_arrays(links[c])
        in_maps.append(m)
    if "nc" not in _CACHE:
        _CACHE["nc"] = Builder(nseg=3, nlayers=2).build()
    res = run_bass_kernel_spmd(_CACHE["nc"], in_maps, core_ids=list(range(NCORES)))
    yp = np.empty_like(xp, dtype=np.float32)
    ys = np.empty_like(xs, dtype=np.float32)
    for c in range(NCORES):
        y = res.results[c]["y"]
        for i, (grp, b, half) in enumerate(segs[c]):
            blk = y[i * SEG:(i + 1) * SEG]
            if grp == "s":
                ys[b, half * SEG:(half + 1) * SEG] = blk
            else:
                yp[b] = blk
    return (yp, ys)
```

```python
import os
import numpy as np
from contextlib import ExitStack
import concourse.bass as bass
import concourse.mybir as mybir
from concourse.bass_utils import run_bass_kernel_spmd

F32 = mybir.dt.float32
BF16 = mybir.dt.bfloat16
AF = mybir.ActivationFunctionType
ALU = mybir.AluOpType
AX = mybir.AxisListType
F_INC = os.environ.get("F_INC", "1") == "1"
F_CONC = os.environ.get("F_CONC", "1") == "1"
F_ACC = os.environ.get("F_ACC", "1") == "1"
F_INCT = os.environ.get("F_INCT", "1") == "1"
F_INCA = os.environ.get("F_INCA", "1") == "1"

DM = 1024
SEG = 2048
PAD = 1024
INW = 3088
DFF = 2816
NFT = DFF // 128
NCORES = 8
EPS = 1e-6
BIG = 1.0e30
GT = (96, 96, 64)
BRANCHES = ((1, 1, 16), (4, 4, 4), (16, 16, 1))
SLOPES = [2.0 ** (-(h + 1)) for h in range(8)]
C_DA, C_DB, C_LTF, C_LUF, C_LTB, C_LLB, C_MF, C_MB, C_ID, C_END = 0, 128, 256, 384, 512, 640, 768, 896, 1024, 1152


def sl_(start, n, step=1):
    return slice(start, start + step * (n - 1) + 1, step)


class Buf:
    __slots__ = ("name", "w", "r", "pr", "pw", "wc")

    def __init__(self, name=""):
        self.name = name
        self.w = {}
        self.r = {}
        self.pr = {}
        self.pw = {}
        self.wc = True


class Tracker:
    ROLL = 30000

    def __init__(self, nc):
        self.nc = nc
        self.engs = {"pe": nc.tensor, "act": nc.scalar, "dve": nc.vector, "pool": nc.gpsimd, "sp": nc.sync}
        self.sems = []
        self.cur = {}
        self.cnt = {}
        self.seen = {e: {} for e in self.engs}
        for e in ("pe", "act", "dve", "pool"):
            self._new_sem(e)
        self.dq = {}
        for q, n in (("sp", 8), ("pool", 2), ("act", 2)):
            ks = []
            for i in range(n):
                k = len(self.sems)
                self.sems.append(nc.alloc_semaphore(name=f"d_{q}{i}"))
                self.cnt[k] = 0
                ks.append(k)
            self.dq[q] = [ks, 0]
        self.n_inst = 0
        self.n_wait = 0
        self.pend = {}

    def _new_sem(self, e):
        k = len(self.sems)
        self.sems.append(self.nc.alloc_semaphore(name=f"c_{e}{k}"))
        self.cnt[k] = 0
        self.cur[e] = k

    def _wait(self, eng, deps):
        seen = self.seen[eng]
        best = {}
        for (k, v) in deps:
            if best.get(k, 0) < v:
                best[k] = v
        for k, v in best.items():
            if seen.get(k, 0) < v:
                self.engs[eng].wait_ge(self.sems[k], v)
                seen[k] = v
                self.n_wait += 1

    @staticmethod
    def _deps(reads, writes, conc):
        deps = []
        for b in reads:
            deps.extend(b.w.items())
        for b in writes:
            if b.r:
                deps.extend(b.r.items())
                deps.extend(b.w.items())
            else:
                deps.extend(b.pr.items())
                deps.extend(b.pw.items())
                if not (conc and b.wc):
                    deps.extend(b.w.items())
        return deps

    @staticmethod
    def _commit(ev, reads, writes, conc):
        k, v = ev
        for b in reads:
            if b.r.get(k, 0) < v:
                b.r[k] = v
        for b in writes:
            if b.r:
                b.pr, b.pw = b.r, {}
                b.r = {}
                b.w = {k: v}
                b.wc = conc
            elif conc and b.wc:
                if b.w.get(k, 0) < v:
                    b.w[k] = v
            elif conc:
                b.pw = b.w
                b.w = {k: v}
                b.wc = True
            else:
                b.w = {k: v}
                b.wc = False
                b.pr, b.pw = {}, {}

    def op(self, eng, fn, reads=(), writes=(), inc=True, conc=False):
        if not F_INC:
            inc = True
        if not F_CONC:
            conc = False
        pend = self.pend.setdefault(eng, [[], []])
        if inc and not pend[0] and not pend[1] and self.cnt[self.cur[eng]] >= self.ROLL:
            self._new_sem(eng)
        self._wait(eng, self._deps(reads, writes, conc))
        inst = fn()
        self.n_inst += 1
        if not inc:
            pend[0].extend(reads)
            pend[1].extend(writes)
            return inst
        k = self.cur[eng]
        inst.then_inc(self.sems[k], 1)
        self.cnt[k] += 1
        self._commit((k, self.cnt[k]), list(reads) + pend[0], list(writes) + pend[1], conc)
        pend[0].clear()
        pend[1].clear()
        return inst

    def dma(self, q, out, in_, reads=(), writes=(), conc=False, **kw):
        if not F_CONC:
            conc = False
        ks, i = self.dq[q]
        k = ks[i % len(ks)]
        self.dq[q][1] = i + 1
        deps = self._deps(reads, writes, conc)
        if self.cnt[k] > 0:
            deps.append((k, self.cnt[k]))
        self._wait(q, deps)
        inst = self.engs[q].dma_start(out=out, in_=in_, **kw)
        inst.then_inc(self.sems[k], 16)
        self.cnt[k] += 16
        self.n_inst += 1
        self._commit((k, self.cnt[k]), reads, writes, conc)
        return inst

    def barrier(self):
        allev = [(k, v) for k, v in self.cnt.items() if v > 0]
        for e in self.engs:
            self._wait(e, allev)


class Builder:
    def __init__(self, nseg=3, nlayers=2, debug=(), phases=None):
        self.nseg = nseg
        self.T = nseg * SEG
        self.nlayers = nlayers
        self.debug = set(debug)
        self.phases = phases
        self.nc = bass.Bass("TRN2", target_bir_lowering=False)
        self.uid = 0

    def din(self, name, shape, dt=F32):
        return self.nc.dram_tensor(name, list(shape), dt, kind="ExternalInput").ap()

    def dscr(self, name, shape, dt):
        kind = "ExternalOutput" if name in self.debug else "Internal"
        return self.nc.dram_tensor(name, list(shape), dt, kind=kind).ap()

    def sbt(self, es, name, shape, dt):
        self.uid += 1
        t = es.enter_context(self.nc.sbuf_tensor(f"{name}_{self.uid}", list(shape), dt))
        return t.ap() if hasattr(t, "ap") else t

    def build(self):
        nc = self.nc
        T_ = self.T
        L = self.nlayers
        ns = self.nseg
        self.tr = Tracker(nc)
        tr = self.tr
        I = {}
        I["x"] = self.din("x", [T_, DM])
        I["flg"] = self.din("flg", [128, ns + 1])
        I["atab"] = self.din("atab", [128, (1 + 3 * ns) * 256])
        I["xden"] = self.din("xden", [ns, 8, 1024])
        I["consts"] = self.din("consts", [128, C_END])
        I["w_in"] = self.din("w_in", [L, DM, INW])
        I["w_out"] = self.din("w_out", [L, DM, DM])
        I["w_up"] = self.din("w_up", [L, DM, 2 * DFF])
        I["w_down"] = self.din("w_down", [L, DFF, DM])
        I["g_pre"] = self.din("g_pre", [L, 128, 8])
        I["g_fpre"] = self.din("g_fpre", [L, 128, 8])
        I["n_post"] = self.din("n_post", [L, 128, DM])
        I["n_fpost"] = self.din("n_fpost", [L, 128, DM])
        I["gnorm"] = self.din("gnorm", [L, 128, 512])
        I["wg_f"] = self.din("wg_f", [L, 17, 256])
        I["wg_b"] = self.din("wg_b", [L, 17, 256])
        I["cw"] = self.din("cw", [L, 128, NFT * 4])
        self.I = I
        self.y = nc.dram_tensor("y", [T_, DM], F32, kind="ExternalOutput").ap()
        S = {}
        S["qT"] = self.dscr("qT_s", [512, T_], BF16)
        S["kT"] = self.dscr("kT_s", [512, T_ + 2 * PAD], BF16)
        S["v"] = self.dscr("v_s", [4, T_ + 2 * PAD, 128], BF16)
        S["gqT"] = self.dscr("gqT_s", [3, 96, T_], F32)
        S["gkT"] = self.dscr("gkT_s", [3, 96, T_], F32)
        S["gk"] = self.dscr("gk_s", [T_, 256], F32)
        S["gv"] = self.dscr("gv_s", [T_, 512], BF16)
        S["gr"] = self.dscr("gr_s", [T_, 512], F32)
        S["lrh"] = self.dscr("lrh_s", [16, T_], BF16)
        S["lrl"] = self.dscr("lrl_s", [16, T_], BF16)
        S["of"] = self.dscr("of_s", [T_, 512], F32)
        S["mixT"] = self.dscr("mixT_s", [DM, T_], BF16)
        S["x1"] = self.dscr("x1_s", [T_, DM], F32)
        S["xT2"] = self.dscr("xT2_s", [DM, T_ + 2], BF16)
        S["xs"] = self.dscr("xs_s", [T_, DM], F32)
        self.S = S
        self.ps = [nc.alloc_psum_tensor(f"psb{i}", [128, 512], F32).ap() for i in range(8)]
        self.psb = [p.bitcast(BF16) for p in self.ps]

        with ExitStack() as es:
            self.flg = self.sbt(es, "flg", [128, ns + 1], F32)
            self.ident = self.sbt(es, "ident", [128, 128], BF16)
            self.B_const = Buf("const")
            zero = self.sbt(es, "zero", [128, 1024], BF16)
            idf = self.sbt(es, "idf", [128, 128], F32)
            tr.dma("sp", self.flg, I["flg"], writes=[self.B_const])
            tr.dma("sp", idf, I["consts"][:, C_ID:C_ID + 128], writes=[self.B_const])
            tr.op("dve", lambda: nc.vector.tensor_copy(out=self.ident, in_=idf), reads=[self.B_const], writes=[self.B_const])
            Bz = Buf("zero")
            tr.op("pool", lambda: nc.gpsimd.memset(zero, 0.0), writes=[Bz])
            for i in range(4):
                tr.dma("sp", S["kT"][i * 128:(i + 1) * 128, 0:PAD], zero, reads=[Bz])
                tr.dma("sp", S["kT"][i * 128:(i + 1) * 128, PAD + T_:PAD + T_ + PAD], zero, reads=[Bz])
                for j in range(PAD // 128):
                    tr.dma("sp", S["v"][i, j * 128:(j + 1) * 128, :], zero[:, 0:128], reads=[Bz])
                    tr.dma("sp", S["v"][i, PAD + T_ + j * 128:PAD + T_ + (j + 1) * 128, :], zero[:, 0:128], reads=[Bz])
            for kc in range(8):
                tr.dma("sp", S["xT2"][kc * 128:(kc + 1) * 128, 0:1], zero[:, 0:1], reads=[Bz], allow_slow_non_contiguous=True)
                tr.dma("sp", S["xT2"][kc * 128:(kc + 1) * 128, T_ + 1:T_ + 2], zero[:, 0:1], reads=[Bz], allow_slow_non_contiguous=True)
            zf = zero.bitcast(F32)
            for c0 in range(0, T_, 512):
                tr.dma("sp", S["gqT"][2, 64:96, c0:c0 + 512], zf[0:32, :], reads=[Bz])
                tr.dma("sp", S["gkT"][2, 64:96, c0:c0 + 512], zf[0:32, :], reads=[Bz])
            tr.barrier()

            want = self.phases
            for l in range(L):
                x_src = I["x"] if l == 0 else S["xs"]
                x_dst = self.y if l == L - 1 else S["xs"]
                if want is None or "p1" in want:
                    self.phase_inproj(l, x_src)
                    tr.barrier()
                if want is None or "p2" in want:
                    self.phase_attn(l)
                    tr.barrier()
                with ExitStack() as esl:
                    full = want is None
                    wup = bg = None
                    with ExitStack() as es_stg:
                        if full:
                            wup = self.sbt(esl, "wup", [128, 8, 2 * DFF], BF16)
                            CHB = 1408
                            stg = [self.sbt(es_stg, "bstg", [128, CHB], F32) for _ in range(2)]
                            Bs = [Buf("bstg") for _ in range(2)]
                            gsb = self.sbt(es_stg, "bgsb", [128, 8], F32)
                            Bg = Buf("bgsb")
                            tr.dma("sp", gsb, I["g_fpre"][l], writes=[Bg])
                            bg = self.weight_steps(wup, I["w_up"][l], DM, 2 * DFF, gsb, Bg, stg, Bs, CHB)
                        if want is None or "p3" in want:
                            self.phase_gla(l, bg)
                        if bg is not None:
                            for _ in bg:
                                pass
                        tr.barrier()
                    if want is None or "p4" in want:
                        self.phase_outproj(l, x_src)
                        tr.barrier()
                        self.phase_ffn(l, x_dst, wup)
                        tr.barrier()
            tr.barrier()
        return nc

    def weight_steps(self, wbf, wdram, K, N, gsb, Bg, stg, Bs, CH):
        nc, tr = self.nc, self.tr
        nk = K // 128
        ns_ = len(stg)
        i = 0
        for kc in range(nk):
            for c0 in range(0, N, CH):
                cw = min(CH, N - c0)
                s = i % ns_
                tr.dma("sp", stg[s][:, 0:cw], wdram[kc * 128:(kc + 1) * 128, c0:c0 + cw], writes=[Bs[s]])
                o_ap = wbf[:, kc, c0:c0 + cw]
                i_ap = stg[s][:, 0:cw]
                if gsb is None:
                    fn = lambda o_ap=o_ap, i_ap=i_ap: nc.scalar.copy(out=o_ap, in_=i_ap)
                else:
                    g_ap = gsb[:, kc:kc + 1]
                    fn = lambda o_ap=o_ap, i_ap=i_ap, g_ap=g_ap: nc.scalar.activation(out=o_ap, in_=i_ap, func=AF.Copy, scale=g_ap)
                tr.op("act", fn, reads=[Bs[s], Bg], writes=[])
                i += 1
                yield

    def load_weight(self, es_outer, wdram, K, N, gsrc, name):
        nc, tr = self.nc, self.tr
        nk = K // 128
        wbf = self.sbt(es_outer, name, [128, nk, N], BF16)
        Bw = Buf(name)
        CH = 2816 if N > 2816 else N
        with ExitStack() as es:
            stg = [self.sbt(es, "wstg", [128, CH], F32) for _ in range(5)]
            Bs = [Buf("wstg") for _ in range(5)]
            gsb = None
            if gsrc is not None:
                gsb = self.sbt(es, "gsb", [128, nk], F32)
                tr.dma("sp", gsb, gsrc, writes=[Bw])
            for _ in self.weight_steps(wbf, wdram, K, N, gsb, Bw, stg, Bs, CH):
                pass
            tr.barrier()
        return wbf, Bw

    def phase_inproj(self, l, x_src):
        nc, tr, S, I = self.nc, self.tr, self.S, self.I
        T_ = self.T
        NG = T_ // 512
        ps, psb = self.ps, self.psb
        with ExitStack() as es:
            wbf, Bw = self.load_weight(es, I["w_in"][l], DM, INW, I["g_pre"][l], "win")
            self.eps_ap = self.sbt(es, "eps", [128, 1], F32)
            Beps = Buf("eps")
            tr.op("pool", lambda: nc.gpsimd.memset(self.eps_ap, EPS), writes=[Beps])
            xg = [self.sbt(es, "xg", [128, 4, DM], F32) for _ in range(2)]
            Bxg = [Buf("xg") for _ in range(2)]
            xn = [self.sbt(es, "xn", [128, 4, DM], BF16) for _ in range(2)]
            Bxn = [Buf("xn") for _ in range(2)]
            xT = [self.sbt(es, "xT", [128, 8, 512], BF16) for _ in range(2)]
            BxT = [Buf("xT") for _ in range(2)]
            junk = [self.sbt(es, "junk", [128, DM], BF16) for _ in range(4)]
            Bjunk = [Buf("junk") for _ in range(4)]
            ss = [self.sbt(es, "ss", [128, 4], F32) for _ in range(2)]
            rs = [self.sbt(es, "rs", [128, 4], F32) for _ in range(2)]
            Bss = [Buf("ss") for _ in range(2)]
            Brs = [Buf("rs") for _ in range(2)]
            NST = 6
            stb = [self.sbt(es, "stb", [128, 512], BF16) for _ in range(NST)]
            stf = [self.sbt(es, "stf", [128, 512], F32) for _ in range(NST)]
            Bstb = [Buf("stb") for _ in range(NST)]
            Bstf = [Buf("stf") for _ in range(NST)]
            Bps = [Buf(f"ps{i}") for i in range(8)]
            BpsT = [Buf(f"psT{i}") for i in range(4)]
            cnt = {"stb": 0, "stf": 0, "ps": 0, "ev": 0, "pt": 0}

            def load(g):
                tr.dma("sp", xg[g % 2], x_src[g * 512:(g + 1) * 512, :].rearrange("(j p) d -> p j d", p=128), writes=[Bxg[g % 2]])

            def norm(g):
                s = g % 2
                for j in range(4):
                    tr.op("act", lambda j=j: nc.scalar.activation(out=junk[j], in_=xg[s][:, j, :], func=AF.Square, accum_out=ss[s][:, j:j + 1]),
                          reads=[Bxg[s]], writes=[Bjunk[j], Bss[s]])
                tr.op("act", lambda: nc.scalar.activation(out=rs[s], in_=ss[s], func=AF.Ln, scale=1.0 / DM, bias=self.eps_ap),
                      reads=[Bss[s], Beps], writes=[Brs[s]])
                tr.op("act", lambda: nc.scalar.activation(out=rs[s], in_=rs[s], func=AF.Exp, scale=-0.5), reads=[Brs[s]], writes=[Brs[s]])
                for j in range(4):
                    tr.op("act", lambda j=j: nc.scalar.activation(out=xn[s][:, j, :], in_=xg[s][:, j, :], func=AF.Copy, scale=rs[s][:, j:j + 1]),
                          reads=[Bxg[s], Brs[s]], writes=[Bxn[s]], conc=(j > 0))

            def transposes(g):
                s = g % 2
                for kp in range(4):
                    pt = cnt["pt"] % 2
                    cnt["pt"] += 1
                    pv = psb[pt]
                    for k2 in range(2):
                        kc = 2 * kp + k2
                        for j in range(4):
                            o_ap = pv[:, k2 * 512 + j * 128:k2 * 512 + (j + 1) * 128]
                            tr.op("pe", lambda j=j, kc=kc, o_ap=o_ap: nc.tensor.transpose(out=o_ap, in_=xn[s][:, j, kc * 128:(kc + 1) * 128], identity=self.ident),
                                  reads=[Bxn[s], self.B_const], writes=[BpsT[pt]], inc=(not F_INCT) or (k2 == 1 and j == 3))
                    src = pv.rearrange("p (k t) -> p k t", k=2)
                    if kp % 2 == 0:
                        tr.op("act", lambda kp=kp, src=src: nc.scalar.copy(out=xT[s][:, 2 * kp:2 * kp + 2, :], in_=src), reads=[BpsT[pt]], writes=[BxT[s]])
                    else:
                        tr.op("dve", lambda kp=kp, src=src: nc.vector.tensor_copy(out=xT[s][:, 2 * kp:2 * kp + 2, :], in_=src), reads=[BpsT[pt]], writes=[BxT[s]])

            def proj_tile(g, kind, lhs_fn, rhs_fn, M, N, dst, dt, scale=None):
                s = g % 2
                pi = 2 + cnt["ps"] % 6
                cnt["ps"] += 1
                pv = ps[pi][0:M, 0:N]
                for kc in range(8):
                    tr.op("pe", lambda kc=kc: nc.tensor.matmul(pv, lhsT=lhs_fn(kc), rhs=rhs_fn(kc), start=(kc == 0), stop=(kc == 7)),
                          reads=[BxT[s], Bw], writes=[Bps[pi]], inc=(kc == 7))
                if dt == BF16:
                    si = cnt["stb"] % NST
                    cnt["stb"] += 1
                    st, Bst = stb[si], Bstb[si]
                else:
                    si = cnt["stf"] % NST
                    cnt["stf"] += 1
                    st, Bst = stf[si], Bstf[si]
                sv = st[0:M, 0:N]
                ev = cnt["ev"] % 2
                cnt["ev"] += 1
                if ev == 0:
                    if scale is None:
                        tr.op("act", lambda: nc.scalar.copy(out=sv, in_=pv), reads=[Bps[pi]], writes=[Bst])
                    else:
                        tr.op("act", lambda: nc.scalar.mul(out=sv, in_=pv, mul=scale), reads=[Bps[pi]], writes=[Bst])
                else:
                    if scale is None:
                        tr.op("dve", lambda: nc.vector.tensor_copy(out=sv, in_=pv), reads=[Bps[pi]], writes=[Bst])
                    else:
                        tr.op("dve", lambda: nc.vector.tensor_scalar_mul(out=sv, in0=pv, scalar1=scale), reads=[Bps[pi]], writes=[Bst])
                return sv, Bst

            def feat_major(g):
                s = g % 2
                t0 = g * 512
                tiles = []
                for i in range(4):
                    tiles.append((i * 128, 128, S["qT"][i * 128:(i + 1) * 128, t0:t0 + 512], BF16, 0.125))
                for i in range(4):
                    tiles.append((512 + i * 128, 128, S["kT"][i * 128:(i + 1) * 128, PAD + t0:PAD + t0 + 512], BF16, None))
                c = 0
                for j in range(3):
                    tiles.append((1536 + c, GT[j], S["gqT"][j, 0:GT[j], t0:t0 + 512], F32, None))
                    tiles.append((1792 + c, GT[j], S["gkT"][j, 0:GT[j], t0:t0 + 512], F32, None))
                    c += GT[j]
                for (c0, M, dst, dt, sc) in tiles:
                    sv, Bst = proj_tile(g, "f", lambda kc, c0=c0, M=M: wbf[:, kc, c0:c0 + M], lambda kc: xT[s][:, kc, :], M, 512, None, dt, sc)
                    tr.dma("sp", dst, sv, reads=[Bst])
                pi = 2 + cnt["ps"] % 6
                cnt["ps"] += 1
                pv = ps[pi][0:16, 0:512]
                for kc in range(8):
                    tr.op("pe", lambda kc=kc: nc.tensor.matmul(pv, lhsT=wbf[:, kc, 3072:3088], rhs=xT[s][:, kc, :], start=(kc == 0), stop=(kc == 7)),
                          reads=[BxT[s], Bw], writes=[Bps[pi]], inc=(kc == 7))
                s1 = cnt["stb"] % NST
                s2_ = (cnt["stb"] + 1) % NST
                cnt["stb"] += 2
                tr.op("act", lambda: nc.scalar.copy(out=stb[s1][0:16, :], in_=pv), reads=[Bps[pi]], writes=[Bstb[s1]])
                tr.op("dve", lambda: nc.vector.tensor_tensor(out=stb[s2_][0:16, :], in0=pv, in1=stb[s1][0:16, :], op=ALU.subtract),
                      reads=[Bps[pi], Bstb[s1]], writes=[Bstb[s2_]])
                tr.dma("sp", S["lrh"][:, t0:t0 + 512], stb[s1][0:16, :], reads=[Bstb[s1]])
                tr.dma("sp", S["lrl"][:, t0:t0 + 512], stb[s2_][0:16, :], reads=[Bstb[s2_]])

            def tok_major(g):
                s = g % 2
                for j in range(4):
                    r0 = g * 512 + j * 128
                    specs = [
                        (1024, 512, S["v"][:, PAD + r0:PAD + r0 + 128, :].rearrange("h t f -> t h f"), BF16, "v"),
                        (1792, 256, S["gk"][r0:r0 + 128, :], F32, ""),
                        (2048, 512, S["gv"][r0:r0 + 128, :], BF16, ""),
                        (2560, 512, S["gr"][r0:r0 + 128, :], F32, ""),
                    ]
                    for (c0, N, dst, dt, kind) in specs:
                        sv, Bst = proj_tile(g, "t", lambda kc, j=j: xT[s][:, kc, j * 128:(j + 1) * 128],
                                            lambda kc, c0=c0, N=N: wbf[:, kc, c0:c0 + N], 128, N, None, dt, None)
                        src = sv.rearrange("t (h f) -> t h f", h=4) if kind == "v" else sv
                        tr.dma("sp", dst, src, reads=[Bst])

            load(0)
            if NG > 1:
                load(1)
            norm(0)
            transposes(0)
            for g in range(NG):
                feat_major(g)
                if g + 1 < NG:
                    norm(g + 1)
                    transposes(g + 1)
                tok_major(g)
                if g + 2 < NG:
                    load(g + 2)

    def phase_attn(self, l):
        nc, tr, S, I = self.nc, self.tr, self.S, self.I
        ns = self.nseg
        ps = self.ps
        with ExitStack() as es:
            Bc = Buf("attc")
            ntab = 1 + 3 * ns
            tabs = self.sbt(es, "tabs", [128, ntab, 256], F32)
            tr.dma("sp", tabs, I["atab"].rearrange("p (n c) -> p n c", c=256), writes=[Bc])
            xd = [self.sbt(es, "xd", [128, 2, 1024], F32) for _ in range(3)]
            Bxd = [Buf("xd") for _ in range(3)]
            for i in range(3):
                tr.op("pool", lambda i=i: nc.gpsimd.memset(xd[i][0:64], 0.0), writes=[Bxd[i]])
            QT = [self.sbt(es, "QT", [128, SEG], BF16) for _ in range(2)]
            KT = [self.sbt(es, "KT", [128, SEG + 2 * PAD], BF16) for _ in range(2)]
            BQK = [Buf("QK") for _ in range(2)]
            NVBIG, NVS, LA = 2, 8, 5
            Vbig = [self.sbt(es, "Vbig", [128, 17, 2, 128], BF16) for _ in range(NVBIG)]
            Vsml = [self.sbt(es, "Vsml", [128, 5, 2, 128], BF16) for _ in range(NVS)]
            BVbig = [Buf("Vbig") for _ in range(NVBIG)]
            BVsml = [Buf("Vsml") for _ in range(NVS)]
            for i in range(NVBIG):
                tr.op("pool", lambda i=i: nc.gpsimd.memset(Vbig[i], 1.0), writes=[BVbig[i]])
            for i in range(NVS):
                tr.op("pool", lambda i=i: nc.gpsimd.memset(Vsml[i], 1.0), writes=[BVsml[i]])
            acc = [self.sbt(es, "acc", [128, 2, SEG], F32) for _ in range(2)]
            Bacc = [Buf("acc") for _ in range(2)]
            rb = self.sbt(es, "rb", [64, 2, SEG], F32)
            Brb = [Buf("rb0"), Buf("rb1")]
            ob = [self.sbt(es, "ob", [64, 2, SEG], BF16) for _ in range(2)]
            Bob = [Buf("ob") for _ in range(2)]
            NS3 = 3
            tmp = [self.sbt(es, "tmp", [128, 2, 256], F32) for _ in range(NS3)]
            PT = [self.sbt(es, "PT", [128, 2, 256], BF16) for _ in range(NS3)]
            Btmp = [Buf("tmp") for _ in range(NS3)]
            BPT = [Buf("PT") for _ in range(NS3)]
            BpsS = [Buf("psS") for _ in range(2)]
            BpsO = [Buf("psO") for _ in range(4)]
            osb = [self.sbt(es, "osb", [128, 2, 128], F32) for _ in range(4)]
            Bosb = [Buf("osb") for _ in range(4)]

            units = [(s, hp) for s in range(ns) for hp in range(4)]
            groups = []
            for ui, (s, hp) in enumerate(units):
                for (d, nres, ntile) in BRANCHES:
                    for r in range(nres):
                        groups.append((ui, d, r, ntile))
            gslot = []
            small_ids = []
            for gi, (ui, d, r, ntile) in enumerate(groups):
                if d == 1:
                    gslot.append((Vbig[ui % NVBIG], BVbig[ui % NVBIG]))
                else:
                    si = len(small_ids)
                    small_ids.append(gi)
                    gslot.append((Vsml[si % NVS], BVsml[si % NVS]))
            small_pos = {gi: si for si, gi in enumerate(small_ids)}
            items = []
            for gi, (ui, d, r, ntile) in enumerate(groups):
                for t in range(ntile):
                    items.append((gi, t))

            def load_unit(ui):
                s, hp = units[ui]
                b = ui % 2
                tr.dma("sp", QT[b], S["qT"][hp * 128:(hp + 1) * 128, s * SEG:(s + 1) * SEG], writes=[BQK[b]], conc=True)
                tr.dma("sp", KT[b], S["kT"][hp * 128:(hp + 1) * 128, s * SEG:s * SEG + SEG + 2 * PAD], writes=[BQK[b]], conc=True)
                tr.dma("sp", xd[ui % 3][64:128], I["xden"][s, 2 * hp:2 * hp + 2, :].partition_broadcast(64), writes=[Bxd[ui % 3]], conc=True)

            def load_group(gi):
                ui, d, r, ntile = groups[gi]
                s, hp = units[ui]
                Vt, BVt = gslot[gi]
                nch = ntile + 1
                R0 = PAD + s * SEG + r - 64 * d
                c0 = 0
                while c0 < nch:
                    n = min(6, nch - c0)
                    base = R0 + d * 128 * c0
                    for e in range(2):
                        src = S["v"][hp, sl_(base, 128 * n, d), e * 64:(e + 1) * 64].rearrange("(c i) f -> i c f", i=128)
                        tr.dma("sp", Vt[:, c0:c0 + n, e, 0:64], src, writes=[BVt], conc=True)
                    c0 += n

            def tab_index(s, d, t, ntile):
                if ntile == 1:
                    return 1 + 2 * ns + s
                if t == 0:
                    return 1 + s
                if t == ntile - 1:
                    return 1 + ns + s
                return 0

            def stageA(ii):
                gi, t = items[ii]
                ui, d, r, ntile = groups[gi]
                b = ui % 2
                sl = ii % 3
                sp_ = ii % 2
                q0 = r + d * 128 * t
                for e in range(2):
                    qs = QT[b][e * 64:(e + 1) * 64, sl_(q0, 128, d)]
                    for c in range(2):
                        k0 = PAD + r + d * (-64 + 128 * (t + c))
                        ks = KT[b][e * 64:(e + 1) * 64, sl_(k0, 128, d)]
                        out = ps[2 * sp_ + e][:, c * 128:(c + 1) * 128]
                        tr.op("pe", lambda out=out, ks=ks, qs=qs: nc.tensor.matmul(out, lhsT=ks, rhs=qs, start=True, stop=True),
                              reads=[BQK[b]], writes=[BpsS[sp_]], inc=(not F_INCA) or (e == 1 and c == 1))

            def stageB(ii):
                gi, t = items[ii]
                ui, d, r, ntile = groups[gi]
                s, hp = units[ui]
                sl = ii % 3
                sp_ = ii % 2
                ti = tab_index(s, d, t, ntile)
                for e in range(2):
                    h = hp * 2 + e
                    cneg = -SLOPES[h] * d
                    tr.op("dve", lambda e=e, cneg=cneg: nc.vector.scalar_tensor_tensor(out=tmp[sl][:, e, :], in0=tabs[:, ti, :], scalar=cneg,
                                                                                       in1=ps[2 * sp_ + e][:, 0:256], op0=ALU.mult, op1=ALU.add),
                          reads=[Bc, BpsS[sp_]], writes=[Btmp[sl]], conc=(e == 1))
                tr.op("act", lambda: nc.scalar.activation(out=PT[sl], in_=tmp[sl], func=AF.Exp), reads=[Btmp[sl]], writes=[BPT[sl]])

            def stageC(ii):
                gi, t = items[ii]
                ui, d, r, ntile = groups[gi]
                b = ui % 2
                sl = ii % 3
                so = ii % 4
                Vt, BVt = gslot[gi]
                po = ps[4 + so][:, 0:256]
                for e in range(2):
                    for c in range(2):
                        tr.op("pe", lambda e=e, c=c: nc.tensor.matmul(po[:, e * 128:(e + 1) * 128], lhsT=Vt[:, t + c, e, :],
                                                                       rhs=PT[sl][:, e, c * 128:(c + 1) * 128], start=(c == 0), stop=(c == 1)),
                              reads=[BVt, BPT[sl]], writes=[BpsO[so]], inc=(not F_INCA) or (e == 1 and c == 1))
                q0 = r + d * 128 * t
                dst = acc[b][:, :, sl_(q0, 128, d)]
                src = po.rearrange("p (e q) -> p e q", e=2)
                if not F_ACC:
                    if d == 1:
                        tr.op("dve", lambda: nc.vector.tensor_copy(out=dst, in_=src), reads=[BpsO[so]], writes=[Bacc[b]])
                    else:
                        tr.op("dve", lambda: nc.vector.tensor_tensor(out=dst, in0=dst, in1=src, op=ALU.add), reads=[BpsO[so], Bacc[b]], writes=[Bacc[b]])
                elif d == 1 and q0 >= 1024:
                    xs = xd[ui % 3][:, :, q0 - 1024:q0 - 1024 + 128]
                    tr.op("dve", lambda: nc.vector.tensor_tensor(out=dst, in0=src, in1=xs, op=ALU.add), reads=[BpsO[so], Bxd[ui % 3]], writes=[Bacc[b]])
                elif d == 1:
                    tr.op("act", lambda: nc.scalar.copy(out=dst, in_=src), reads=[BpsO[so]], writes=[Bacc[b]])
                else:
                    ot = osb[so]
                    tr.op("act", lambda: nc.scalar.copy(out=ot, in_=src), reads=[BpsO[so]], writes=[Bosb[so]])
                    tr.op("pool", lambda: nc.gpsimd.tensor_tensor(out=dst, in0=dst, in1=ot, op=ALU.add), reads=[Bosb[so], Bacc[b]], writes=[Bacc[b]])

            def finalize(ui):
                s, hp = units[ui]
                b = ui % 2
                for e in range(2):
                    tr.op("act", lambda e=e: nc.scalar.activation(out=rb[:, e, :], in_=acc[b][64:128, e, :], func=AF.Ln), reads=[Bacc[b]], writes=[Brb[e]])
                    tr.op("act", lambda e=e: nc.scalar.activation(out=rb[:, e, :], in_=rb[:, e, :], func=AF.Exp, scale=-1.0), reads=[Brb[e]], writes=[Brb[e]])
                for e in range(2):
                    tr.op("pool", lambda e=e: nc.gpsimd.tensor_tensor(out=ob[b][:, e, :], in0=acc[b][0:64, e, :], in1=rb[:, e, :], op=ALU.mult),
                          reads=[Bacc[b], Brb[e]], writes=[Bob[b]], conc=(e == 1))
                    h = hp * 2 + e
                    tr.dma("sp", S["mixT"][h * 64:(h + 1) * 64, s * SEG:(s + 1) * SEG], ob[b][:, e, :], reads=[Bob[b]])

            NI = len(items)
            first_item_of_group = {}
            for ii, (gi, t) in enumerate(items):
                first_item_of_group.setdefault(gi, ii)
            load_unit(0)
            load_group(0)
            for si in range(min(LA, len(small_ids))):
                load_group(small_ids[si])
            last_item_of_unit = {}
            for ii, (gi, t) in enumerate(items):
                last_item_of_unit[groups[gi][0]] = ii

            def pre(ii):
                gi, t = items[ii]
                if first_item_of_group[gi] != ii:
                    return
                ui, d = groups[gi][0], groups[gi][1]
                if d == 1:
                    if ui + 1 < len(units):
                        load_unit(ui + 1)
                else:
                    si = small_pos[gi]
                    if si + LA < len(small_ids):
                        load_group(small_ids[si + LA])
                    if groups[gi - 1][1] == 1 and ui + 1 < len(units):
                        nxt = gi - 1 + sum(n for (_, n, _) in BRANCHES)
                        load_group(nxt)

            for step in range(NI + 2):
                if step < NI:
                    pre(step)
                    stageA(step)
                if 0 <= step - 1 < NI:
                    stageB(step - 1)
                if 0 <= step - 2 < NI:
                    stageC(step - 2)
                    ui = groups[items[step - 2][0]][0]
                    if last_item_of_unit[ui] == step - 2:
                        finalize(ui)

    def phase_gla(self, l, bg=None):
        nc, tr, S, I = self.nc, self.tr, self.S, self.I
        T_ = self.T
        NT = T_ // 128
        ps, psb = self.ps, self.psb
        TPS = SEG // 128
        with ExitStack() as es:
            cst = self.sbt(es, "gcst", [128, 768], F32)
            Bc = Buf("gcst")
            tr.dma("sp", cst, I["consts"][:, C_LTF:C_LTF + 768], writes=[Bc])
            Lb = self.sbt(es, "Lb", [128, 512], BF16)
            tr.op("dve", lambda: nc.vector.tensor_copy(out=Lb, in_=cst[:, 0:512]), reads=[Bc], writes=[Bc])
            LtF, LuF, LtB, LlB = Lb[:, 0:128], Lb[:, 128:256], Lb[:, 256:384], Lb[:, 384:512]
            MF, MB = cst[:, 512:640], cst[:, 640:768]
            wg = [self.sbt(es, "wg", [17, 256], F32) for _ in range(2)]
            wgh = [self.sbt(es, "wgh", [17, 256], BF16) for _ in range(2)]
            wgl = [self.sbt(es, "wgl", [17, 256], BF16) for _ in range(2)]
            tr.dma("sp", wg[0], I["wg_f"][l], writes=[Bc], conc=True)
            tr.dma("sp", wg[1], I["wg_b"][l], writes=[Bc], conc=True)
            for d_ in range(2):
                tr.op("dve", lambda d_=d_: nc.vector.tensor_copy(out=wgh[d_], in_=wg[d_]), reads=[Bc], writes=[Bc])
                tr.op("dve", lambda d_=d_: nc.vector.tensor_tensor(out=wgl[d_], in0=wg[d_], in1=wgh[d_], op=ALU.subtract), reads=[Bc], writes=[Bc])
            gn = self.sbt(es, "gn", [128, 512], F32)
            tr.dma("sp", gn, I["gnorm"][l], writes=[Bc], conc=True)
            eps = self.sbt(es, "eps", [128, 1], F32)
            tr.op("pool", lambda: nc.gpsimd.memset(eps, EPS), writes=[Bc], conc=True)

            NL = 5
            lrh = [self.sbt(es, "lrh", [17, 128], BF16) for _ in range(NL)]
            lrl = [self.sbt(es, "lrl", [17, 128], BF16) for _ in range(NL)]
            qT3 = [self.sbt(es, "qT3", [96, 3, 128], F32) for _ in range(NL)]
            kT3 = [self.sbt(es, "kT3", [96, 3, 128], F32) for _ in range(NL)]
            ktk = [self.sbt(es, "ktk", [128, 256], F32) for _ in range(NL)]
            vtk = [self.sbt(es, "vtk", [128, 512], BF16) for _ in range(NL)]
            rtk = [self.sbt(es, "rtk", [128, 512], F32) for _ in range(NL)]
            oft = [self.sbt(es, "oft", [128, 512], F32) for _ in range(NL)]
            Bld = [Buf("gld") for _ in range(NL)]
            Bld2 = [Buf("gld2") for _ in range(NL)]
            for i in range(NL):
                tr.op("pool", lambda i=i: nc.gpsimd.memset(lrh[i][0:1, :], 1.0), writes=[Bld[i]], conc=True)
                tr.op("pool", lambda i=i: nc.gpsimd.memset(lrl[i][0:1, :], 0.0), writes=[Bld[i]], conc=True)
            e1 = [self.sbt(es, "e1", [128, 256], F32) for _ in range(2)]
            lsp = [self.sbt(es, "lsp", [128, 256], F32) for _ in range(2)]
            Bg1 = [Buf("g1") for _ in range(2)]
            lsh = [self.sbt(es, "lsh", [128, 288], BF16) for _ in range(2)]
            lsl = [self.sbt(es, "lsl", [128, 288], BF16) for _ in range(2)]
            Bls = [Buf("ls") for _ in range(2)]
            for i in range(2):
                tr.op("pool", lambda i=i: nc.gpsimd.memset(lsh[i][:, 256:288], 0.0), writes=[Bls[i]], conc=True)
                tr.op("pool", lambda i=i: nc.gpsimd.memset(lsl[i][:, 256:288], 0.0), writes=[Bls[i]], conc=True)
            E2T = [self.sbt(es, "E2T", [96, 3, 128], F32) for _ in range(2)]
            E3 = [self.sbt(es, "E3", [128, 256], F32) for _ in range(2)]
            Bg2 = [Buf("g2") for _ in range(2)]
            E1T = [self.sbt(es, "E1T", [96, 3, 128], F32) for _ in range(3)]
            qin = [self.sbt(es, "qin", [96, 3, 128], BF16) for _ in range(3)]
            kin = [self.sbt(es, "kin", [96, 3, 128], BF16) for _ in range(3)]
            kst = [self.sbt(es, "kst", [128, 256], BF16) for _ in range(3)]
            Bgo = [Buf("gGo") for _ in range(3)]
            attT = [self.sbt(es, "attT", [128, 8, 128], BF16) for _ in range(2)]
            Batt = [Buf("attT") for _ in range(2)]
            Sst = self.sbt(es, "Sst", [96, 3, 64], F32)
            Sbf = [self.sbt(es, "Sbd", [96, 3, 192], BF16) for _ in range(2)]
            BS = Buf("S")
            BSb = [Buf("Sbf") for _ in range(2)]
            osb = [self.sbt(es, "osb", [128, 512], F32) for _ in range(2)]
            Bosb = [Buf("osb") for _ in range(2)]
            sq = [self.sbt(es, "sq", [128, 512], F32) for _ in range(2)]
            ssg = [self.sbt(es, "ssg", [128, 8], F32) for _ in range(2)]
            rsg = [self.sbt(es, "rsg", [128, 8], F32) for _ in range(2)]
            sil = [self.sbt(es, "sil", [128, 512], F32) for _ in range(2)]
            ogb = [self.sbt(es, "ogb", [128, 512], BF16) for _ in range(2)]
            ogT = [self.sbt(es, "ogT", [128, 4, 128], BF16) for _ in range(2)]
            BogT = [Buf("ogT") for _ in range(2)]
            Bfin = [Buf("gfin") for _ in range(2)]
            Bpz, BpbT, Bpb3, BpA, BpO, BpU = (Buf("pz"), Buf("pbT"), Buf("pb3"), Buf("pA"), Buf("pO"), Buf("pU"))
            BpT = BpU

            def loads(idx, t, bwd):
                sl = idx % NL
                tk = slice(t * 128, (t + 1) * 128)
                tr.dma("sp", lrh[sl][1:17, :], S["lrh"][:, tk], writes=[Bld[sl]], conc=True)
                tr.dma("sp", lrl[sl][1:17, :], S["lrl"][:, tk], writes=[Bld[sl]], conc=True)
                tr.dma("sp", qT3[sl], S["gqT"][:, :, tk].rearrange("j p t -> p j t"), writes=[Bld[sl]], conc=True)
                tr.dma("sp", kT3[sl], S["gkT"][:, :, tk].rearrange("j p t -> p j t"), writes=[Bld[sl]], conc=True)
                tr.dma("sp", ktk[sl], S["gk"][tk, :], writes=[Bld[sl]], conc=True)
                tr.dma("sp", vtk[sl], S["gv"][tk, :], writes=[Bld[sl]], conc=True)
                if bwd:
                    tr.dma("sp", rtk[sl], S["gr"][tk, :], writes=[Bld2[sl]], conc=True)
                    tr.dma("sp", oft[sl], S["of"][tk, :], writes=[Bld2[sl]], conc=True)

            def G1(idx, t, bwd):
                sl, s2 = idx % NL, idx % 2
                d_ = 1 if bwd else 0
                zps = ps[0][:, 0:256]
                tr.op("pe", lambda: nc.tensor.matmul(zps, lhsT=lrh[sl], rhs=wgh[d_], start=True, stop=False), reads=[Bld[sl], Bc], writes=[Bpz], inc=False)
                tr.op("pe", lambda: nc.tensor.matmul(zps, lhsT=lrh[sl], rhs=wgl[d_], start=False, stop=False), reads=[Bld[sl], Bc], writes=[Bpz], inc=False)
                tr.op("pe", lambda: nc.tensor.matmul(zps, lhsT=lrl[sl], rhs=wgh[d_], start=False, stop=True), reads=[Bld[sl], Bc], writes=[Bpz])
                tr.op("act", lambda: nc.scalar.activation(out=e1[s2], in_=zps, func=AF.Exp, scale=-1.0), reads=[Bpz], writes=[Bg1[s2]])
                tr.op("act", lambda: nc.scalar.activation(out=lsp[s2], in_=e1[s2], func=AF.Ln, bias=1.0), reads=[Bg1[s2]], writes=[Bg1[s2]])
                tr.op("act", lambda: nc.scalar.copy(out=lsh[s2][:, 0:256], in_=lsp[s2]), reads=[Bg1[s2]], writes=[Bls[s2]])
                tr.op("dve", lambda: nc.vector.tensor_tensor(out=lsl[s2][:, 0:256], in0=lsp[s2], in1=lsh[s2][:, 0:256], op=ALU.subtract),
                      reads=[Bg1[s2], Bls[s2]], writes=[Bls[s2]], conc=True)

            def G1b(idx, t, bwd):
                pass

            def G2(idx, t, bwd):
                sl, s2, s3 = idx % NL, idx % 2, idx % 3
                Ltri = LtB if bwd else LtF
                Lrest = LlB if bwd else LuF
                bTps = ps[1][0:96, 0:384].rearrange("p (j t) -> p j t", j=3)
                b3ps = ps[7][:, 0:256]
                for j in range(3):
                    c = 96 * j
                    o_ap = ps[1][0:96, j * 128:(j + 1) * 128]
                    tr.op("pe", lambda c=c, o_ap=o_ap: nc.tensor.matmul(o_ap, lhsT=lsh[s2][:, c:c + 96], rhs=Ltri, start=True, stop=False),
                          reads=[Bls[s2], Bc], writes=[BpbT], inc=False)
                    tr.op("pe", lambda c=c, o_ap=o_ap: nc.tensor.matmul(o_ap, lhsT=lsl[s2][:, c:c + 96], rhs=Ltri, start=False, stop=True),
                          reads=[Bls[s2], Bc], writes=[BpbT], inc=(j == 2))
                tr.op("pe", lambda: nc.tensor.matmul(b3ps, lhsT=Lrest, rhs=lsh[s2][:, 0:256], start=True, stop=False), reads=[Bls[s2], Bc], writes=[Bpb3], inc=False)
                tr.op("pe", lambda: nc.tensor.matmul(b3ps, lhsT=Lrest, rhs=lsl[s2][:, 0:256], start=False, stop=True), reads=[Bls[s2], Bc], writes=[Bpb3])
                tr.op("act", lambda: nc.scalar.activation(out=E1T[s3], in_=bTps, func=AF.Exp), reads=[BpbT], writes=[Bgo[s3]])
                tr.op("act", lambda: nc.scalar.activation(out=E2T[s2], in_=bTps, func=AF.Exp, scale=-1.0), reads=[BpbT], writes=[Bg2[s2]])
                tr.op("act", lambda: nc.scalar.activation(out=E3[s2], in_=b3ps, func=AF.Exp), reads=[Bpb3], writes=[Bg2[s2]], conc=True)
                tr.op("dve", lambda: nc.vector.scalar_tensor_tensor(out=qin[s3], in0=qT3[sl], scalar=32.0 ** -0.5, in1=E1T[s3], op0=ALU.mult, op1=ALU.mult),
                      reads=[Bld[sl], Bgo[s3]], writes=[Bgo[s3]], conc=True)
                tr.op("pool", lambda: nc.gpsimd.tensor_tensor(out=kin[s3], in0=kT3[sl], in1=E2T[s2], op=ALU.mult), reads=[Bld[sl], Bg2[s2]], writes=[Bgo[s3]], conc=True)
                tr.op("dve", lambda: nc.vector.tensor_tensor(out=kst[s3], in0=ktk[sl], in1=E3[s2], op=ALU.mult), reads=[Bld[sl], Bg2[s2]], writes=[Bgo[s3]], conc=True)

            def H1(idx, t, bwd):
                s3, a = idx % 3, idx % 2
                M = MB if bwd else MF
                abank = (2, 3, 6)
                for h in range(8):
                    j, rt = h // 3, h % 3
                    p0 = 32 * rt
                    out = ps[abank[rt]][:, j * 128:(j + 1) * 128]
                    tr.op("pe", lambda out=out, j=j, p0=p0: nc.tensor.matmul(out, lhsT=kin[s3][p0:p0 + 32, j, :], rhs=qin[s3][p0:p0 + 32, j, :], start=True, stop=True),
                          reads=[Bgo[s3]], writes=[BpA], inc=(h == 7))
                for rt in range(3):
                    nh = 3 if rt < 2 else 2
                    src = ps[abank[rt]][:, 0:nh * 128].rearrange("p (h q) -> p h q", h=nh)
                    mk = M.unsqueeze(1).to_broadcast([128, nh, 128])
                    dst = attT[a][:, sl_(rt, nh, 3), :]
                    tr.op("dve", lambda src=src, mk=mk, dst=dst: nc.vector.tensor_tensor(out=dst, in0=src, in1=mk, op=ALU.mult),
                          reads=[BpA, Bc], writes=[Batt[a]], conc=True)

            def H2(idx, t, bwd):
                sl, s3, a = idx % NL, idx % 3, idx % 2
                sb_r, sb_w = Sbf[idx % 2], Sbf[(idx + 1) % 2]
                Bsb_r, Bsb_w = BSb[idx % 2], BSb[(idx + 1) % 2]
                seg = t // TPS
                if bwd:
                    entering = (t % TPS == TPS - 1) and t != NT - 1
                    fl = self.flg[0:96, seg + 1:seg + 2]
                else:
                    entering = (t % TPS == 0) and t != 0
                    fl = self.flg[0:96, seg:seg + 1]
                def cast_state(dst, Bdst):
                    for h3 in range(3):
                        p0 = 32 * h3
                        tr.op("pool", lambda p0=p0, h3=h3: nc.gpsimd.tensor_copy(out=dst[p0:p0 + 32, :, 64 * h3:64 * h3 + 64], in_=Sst[p0:p0 + 32, :, :]),
                              reads=[BS], writes=[Bdst], conc=(h3 > 0))

                if entering:
                    tr.op("dve", lambda: nc.vector.tensor_scalar_mul(out=Sst, in0=Sst, scalar1=fl), reads=[BS, self.B_const], writes=[BS])
                    cast_state(sb_r, Bsb_r)
                for h in range(8):
                    j, p0 = h // 3, 32 * (h % 3)
                    out = ps[5][p0:p0 + 32, j * 64:(j + 1) * 64]
                    tr.op("pe", lambda out=out, h=h: nc.tensor.matmul(out, lhsT=kst[s3][:, h * 32:(h + 1) * 32], rhs=vtk[sl][:, h * 64:(h + 1) * 64], start=True, stop=True),
                          reads=[Bgo[s3], Bld[sl]], writes=[BpU], inc=(h == 7))
                for j in range(3):
                    nh = 3 if j < 2 else 2
                    outj = ps[4][:, j * 192:j * 192 + nh * 64]
                    tr.op("pe", lambda j=j, nh=nh, outj=outj: nc.tensor.matmul(outj, lhsT=qin[s3][0:GT[j], j, :], rhs=sb_r[0:GT[j], j, 0:nh * 64], start=True, stop=False),
                          reads=[Bgo[s3], Bsb_r], writes=[BpO], inc=False)
                    for hh in range(nh):
                        h = 3 * j + hh
                        out = ps[4][:, h * 64:(h + 1) * 64]
                        tr.op("pe", lambda out=out, h=h: nc.tensor.matmul(out, lhsT=attT[a][:, h, :], rhs=vtk[sl][:, h * 64:(h + 1) * 64], start=False, stop=(hh == nh - 1)),
                              reads=[Batt[a], Bld[sl]], writes=[BpO], inc=(h == 7))
                col = 0 if bwd else 127
                for j in range(3):
                    Mj = GT[j]
                    tr.op("dve", lambda j=j, Mj=Mj: nc.vector.scalar_tensor_tensor(out=Sst[0:Mj, j, :], in0=Sst[0:Mj, j, :], scalar=E1T[s3][0:Mj, j, col:col + 1],
                                                                                   in1=ps[5][0:Mj, j * 64:(j + 1) * 64], op0=ALU.mult, op1=ALU.add),
                          reads=[BS, Bgo[s3], BpU], writes=[BS])
                cast_state(sb_w, Bsb_w)
                tk = slice(t * 128, (t + 1) * 128)
                if not bwd:
                    tr.op("act", lambda: nc.scalar.copy(out=osb[a], in_=ps[4]), reads=[BpO], writes=[Bosb[a]])
                    tr.dma("sp", S["of"][tk, :], osb[a], reads=[Bosb[a]])
                else:
                    o = osb[a]
                    o3 = o.rearrange("p (h d) -> p h d", h=8)
                    Bf = Bfin[a]
                    tr.op("dve", lambda: nc.vector.tensor_tensor(out=o, in0=oft[sl], in1=ps[4], op=ALU.add), reads=[BpO, Bld2[sl]], writes=[Bosb[a]])
                    tr.op("pool", lambda: nc.gpsimd.tensor_tensor(out=sq[a], in0=o, in1=o, op=ALU.mult), reads=[Bosb[a]], writes=[Bf])
                    tr.op("dve", lambda: nc.vector.tensor_reduce(out=ssg[a], in_=sq[a].rearrange("p (h d) -> p h d", h=8), axis=AX.X, op=ALU.add), reads=[Bf], writes=[Bf])
                    tr.op("act", lambda: nc.scalar.activation(out=rsg[a], in_=ssg[a], func=AF.Ln, scale=1.0 / 64, bias=eps), reads=[Bf, Bc], writes=[Bf])
                    tr.op("act", lambda: nc.scalar.activation(out=rsg[a], in_=rsg[a], func=AF.Exp, scale=-0.5), reads=[Bf], writes=[Bf])
                    tr.op("act", lambda: nc.scalar.activation(out=sil[a], in_=rtk[sl], func=AF.Exp, scale=-1.0), reads=[Bld2[sl], Bf], writes=[Bf])
                    tr.op("act", lambda: nc.scalar.activation(out=sil[a], in_=sil[a], func=AF.Ln, bias=1.0), reads=[Bf], writes=[Bf])
                    tr.op("act", lambda: nc.scalar.activation(out=sil[a], in_=sil[a], func=AF.Exp, scale=-1.0), reads=[Bf], writes=[Bf])
                    tr.op("pool", lambda: nc.gpsimd.tensor_tensor(out=sil[a], in0=sil[a], in1=rtk[sl], op=ALU.mult), reads=[Bf, Bld2[sl]], writes=[Bf])
                    tr.op("pool", lambda: nc.gpsimd.tensor_tensor(out=sil[a], in0=sil[a], in1=gn, op=ALU.mult), reads=[Bf, Bc], writes=[Bf])
                    tr.op("dve", lambda: nc.vector.tensor_tensor(out=o3, in0=o3, in1=rsg[a].unsqueeze(2).to_broadcast([128, 8, 64]), op=ALU.mult), reads=[Bf, Bosb[a]], writes=[Bosb[a]])
                    tr.op("pool", lambda: nc.gpsimd.tensor_tensor(out=ogb[a], in0=o, in1=sil[a], op=ALU.mult), reads=[Bosb[a], Bf], writes=[Bf])
                    pt = psb[5][:, 512:1024]
                    for c in range(4):
                        tr.op("pe", lambda c=c: nc.tensor.transpose(out=pt[:, c * 128:(c + 1) * 128], in_=ogb[a][:, c * 128:(c + 1) * 128], identity=self.ident),
                              reads=[Bf, self.B_const], writes=[BpT], inc=(c == 3))
                    tr.op("act", lambda: nc.scalar.copy(out=ogT[a], in_=pt.rearrange("p (c t) -> p c t", c=4)), reads=[BpT], writes=[BogT[a]])
                    tr.dma("sp", S["mixT"][512:1024, tk].rearrange("(c p) t -> p c t", p=128), ogT[a], reads=[BogT[a]])

            for bwd in (False, True):
                order = list(range(NT))
                if bwd:
                    order = order[::-1]
                    tr.barrier()
                tr.op("dve", lambda: nc.vector.memset(Sst, 0.0), reads=[BS], writes=[BS])
                tr.op("pool", lambda: nc.gpsimd.memset(Sbf[0], 0.0), reads=[BSb[0]], writes=[BSb[0]])
                tr.op("pool", lambda: nc.gpsimd.memset(Sbf[1], 0.0), reads=[BSb[1]], writes=[BSb[1]])
                for i in range(min(NL, NT)):
                    loads(i, order[i], bwd)
                for it in range(-3, NT):
                    if bg is not None:
                        next(bg, None)
                    if 0 <= it + 3 < NT:
                        G1(it + 3, order[it + 3], bwd)
                    if 0 <= it + 2 < NT:
                        G2(it + 2, order[it + 2], bwd)
                    if 0 <= it < NT:
                        H2(it, order[it], bwd)
                    if 0 <= it + 1 < NT:
                        H1(it + 1, order[it + 1], bwd)
                    if 0 <= it + 3 < NT:
                        G1b(it + 3, order[it + 3], bwd)
                    if it >= 0 and it + NL < NT:
                        loads(it + NL, order[it + NL], bwd)

    def phase_outproj(self, l, x_src):
        nc, tr, S, I = self.nc, self.tr, self.S, self.I
        T_ = self.T
        NG = T_ // 512
        NTL = T_ // 128
        ps, psb = self.ps, self.psb
        with ExitStack() as es:
            wbf, Bw = self.load_weight(es, I["w_out"][l], DM, DM, None, "wout")
            npb = self.sbt(es, "npb", [128, DM], F32)
            Bc = Buf("c")
            tr.dma("sp", npb, I["n_post"][l], writes=[Bc])
            eps = self.sbt(es, "eps", [128, 1], F32)
            tr.op("pool", lambda: nc.gpsimd.memset(eps, EPS), writes=[Bc], conc=True)
            mix = [self.sbt(es, "mix", [128, 8, 512], BF16) for _ in range(3)]
            Bmix = [Buf("mix") for _ in range(3)]
            NX = 6
            xt = [self.sbt(es, "xt", [128, DM], F32) for _ in range(NX)]
            Bxt = [Buf("xt") for _ in range(NX)]
            t1 = [self.sbt(es, "t1", [128, DM], F32) for _ in range(2)]
            Bt1 = [Buf("t1") for _ in range(2)]
            junk2 = [self.sbt(es, "junk", [128, DM], BF16) for _ in range(2)]
            Bj2 = [Buf("junk") for _ in range(2)]
            ss = [self.sbt(es, "ss", [128, 4], F32) for _ in range(4)]
            rs = [self.sbt(es, "rs", [128, 2], F32) for _ in range(4)]
            Bss = [Buf("ss") for _ in range(4)]
            Brs = [Buf("rs") for _ in range(4)]
            xn = [self.sbt(es, "xn", [128, DM], BF16) for _ in range(2)]
            Bxn = [Buf("xn") for _ in range(2)]
            xT = [self.sbt(es, "xT", [128, 8, 128], BF16) for _ in range(2)]
            BxT = [Buf("xT") for _ in range(2)]
            Bpy = [Buf("py") for _ in range(3)]
            Bpt = [Buf("pt") for _ in range(2)]

            def load_g(g):
                tr.dma("sp", mix[g % 3], S["mixT"][:, g * 512:(g + 1) * 512].rearrange("(kc p) t -> p kc t", p=128), writes=[Bmix[g % 3]])

            def load_x(i):
                tr.dma("sp", xt[i % NX], x_src[i * 128:(i + 1) * 128, :], writes=[Bxt[i % NX]])

            def stA(i):
                g, j = i // 4, i % 4
                if j == 0 and g + 1 < NG:
                    load_g(g + 1)
                p = i % 3
                py = [ps[2 * p], ps[2 * p + 1]]
                for n in range(2):
                    for kc in range(8):
                        tr.op("pe", lambda n=n, kc=kc: nc.tensor.matmul(py[n], lhsT=mix[g % 3][:, kc, j * 128:(j + 1) * 128], rhs=wbf[:, kc, n * 512:(n + 1) * 512],
                                                                        start=(kc == 0), stop=(kc == 7)), reads=[Bmix[g % 3], Bw], writes=[Bpy[p]],
                              inc=(kc == 7 and n == 1))

            def stB(i):
                p, q, a = i % 3, i % 4, i % 2
                py = [ps[2 * p], ps[2 * p + 1]]
                for n in range(2):
                    tr.op("act", lambda n=n: nc.scalar.activation(out=junk2[a][:, n * 512:(n + 1) * 512], in_=py[n], func=AF.Square, accum_out=ss[q][:, n:n + 1]),
                          reads=[Bpy[p]], writes=[Bj2[a], Bss[q]], conc=True)
                tr.op("dve", lambda: nc.vector.tensor_tensor(out=ss[q][:, 3:4], in0=ss[q][:, 0:1], in1=ss[q][:, 1:2], op=ALU.add), reads=[Bss[q]], writes=[Brs[q]])
                tr.op("act", lambda: nc.scalar.activation(out=rs[q][:, 0:1], in_=ss[q][:, 3:4], func=AF.Ln, scale=1.0 / DM, bias=eps), reads=[Brs[q], Bc], writes=[Brs[q]])
                tr.op("act", lambda: nc.scalar.activation(out=rs[q][:, 0:1], in_=rs[q][:, 0:1], func=AF.Exp, scale=-0.5), reads=[Brs[q]], writes=[Brs[q]])

            def stC(i):
                p, q, a, xs_ = i % 3, i % 4, i % 2, i % NX
                py = [ps[2 * p], ps[2 * p + 1]]
                for n in range(2):
                    tr.op("act", lambda n=n: nc.scalar.activation(out=t1[a][:, n * 512:(n + 1) * 512], in_=py[n], func=AF.Copy, scale=rs[q][:, 0:1]),
                          reads=[Bpy[p], Brs[q]], writes=[Bt1[a]], conc=True)
                tr.op("dve", lambda: nc.vector.tensor_tensor(out=t1[a], in0=t1[a], in1=npb, op=ALU.mult), reads=[Bt1[a], Bc], writes=[Bt1[a]])
                tr.op("pool", lambda: nc.gpsimd.tensor_tensor(out=xt[xs_], in0=xt[xs_], in1=t1[a], op=ALU.add), reads=[Bt1[a], Bxt[xs_]], writes=[Bxt[xs_]])
                tr.dma("sp", S["x1"][i * 128:(i + 1) * 128, :], xt[xs_], reads=[Bxt[xs_]])

            def stD(i):
                q, a, xs_ = i % 4, i % 2, i % NX
                tr.op("act", lambda: nc.scalar.activation(out=junk2[a], in_=xt[xs_], func=AF.Square, accum_out=ss[q][:, 2:3]), reads=[Bxt[xs_]], writes=[Bj2[a], Bss[q]])
                tr.op("act", lambda: nc.scalar.activation(out=rs[q][:, 1:2], in_=ss[q][:, 2:3], func=AF.Ln, scale=1.0 / DM, bias=eps), reads=[Bss[q], Bc], writes=[Brs[q]])
                tr.op("act", lambda: nc.scalar.activation(out=rs[q][:, 1:2], in_=rs[q][:, 1:2], func=AF.Exp, scale=-0.5), reads=[Brs[q]], writes=[Brs[q]])
                tr.op("dve", lambda: nc.vector.tensor_scalar_mul(out=xn[a], in0=xt[xs_], scalar1=rs[q][:, 1:2]), reads=[Bxt[xs_], Brs[q]], writes=[Bxn[a]])

            def stE(i):
                a = i % 2
                pt = psb[6 + a]
                for kc in range(8):
                    tr.op("pe", lambda kc=kc: nc.tensor.transpose(out=pt[:, kc * 128:(kc + 1) * 128], in_=xn[a][:, kc * 128:(kc + 1) * 128], identity=self.ident),
                          reads=[Bxn[a], self.B_const], writes=[Bpt[a]], inc=(kc == 7))
                tr.op("dve", lambda: nc.vector.tensor_copy(out=xT[a], in_=pt.rearrange("p (kc t) -> p kc t", kc=8)), reads=[Bpt[a]], writes=[BxT[a]])
                tr.dma("sp", S["xT2"][:, 1 + i * 128:1 + (i + 1) * 128].rearrange("(kc p) t -> p kc t", p=128), xT[a], reads=[BxT[a]])

            load_g(0)
            load_x(0)
            for it in range(-4, NTL):
                if 0 <= it < NTL:
                    stE(it)
                if 0 <= it + 1 < NTL:
                    stD(it + 1)
                if 0 <= it + 2 < NTL:
                    stC(it + 2)
                if 0 <= it + 3 < NTL:
                    stB(it + 3)
                if 0 <= it + 4 < NTL:
                    stA(it + 4)
                if 1 <= it + 5 < NTL:
                    load_x(it + 5)

    def phase_ffn(self, l, x_dst, wup_pre=None):
        nc, tr, S, I = self.nc, self.tr, self.S, self.I
        T_ = self.T
        NU = T_ // 256
        UPS = SEG // 256
        ps = self.ps
        with ExitStack() as es:
            if wup_pre is not None:
                wup = wup_pre
            else:
                wup, Bwu = self.load_weight(es, I["w_up"][l], DM, 2 * DFF, I["g_fpre"][l], "wup")
            wdn, Bwd = self.load_weight(es, I["w_down"][l], DFF, DM, None, "wdn")
            Bw = Buf("w")
            npb = self.sbt(es, "npb", [128, DM], F32)
            Bc = Buf("c")
            tr.dma("sp", npb, I["n_fpost"][l], writes=[Bc])
            cw = self.sbt(es, "cw", [128, NFT, 4], F32)
            tr.dma("sp", cw, I["cw"][l].rearrange("p (f c) -> p f c", c=4), writes=[Bc])
            eps = self.sbt(es, "eps", [128, 1], F32)
            tr.op("pool", lambda: nc.gpsimd.memset(eps, EPS), writes=[Bc])
            xu = [self.sbt(es, "xu", [128, 8, 258], BF16) for _ in range(2)]
            Bxu = [Buf("xu") for _ in range(2)]
            fT = self.sbt(es, "fT", [128, NFT, 256], BF16)
            BfT = Buf("fT")
            NC3 = 5
            c1 = [self.sbt(es, "c1", [128, 256], F32) for _ in range(NC3)]
            ge = [self.sbt(es, "ge", [128, 256], F32) for _ in range(NC3)]
            Bc1 = [Buf("c1") for _ in range(NC3)]
            Bge = [Buf("ge") for _ in range(NC3)]
            asb = [self.sbt(es, "asb", [128, 258], F32) for _ in range(NC3)]
            gsb = [self.sbt(es, "gsb", [128, 256], F32) for _ in range(NC3)]
            Basb = [Buf("asb") for _ in range(NC3)]
            Bgsb = [Buf("gsb") for _ in range(NC3)]
            t1 = [self.sbt(es, "t1", [128, DM], F32) for _ in range(2)]
            Bt1 = [Buf("t1") for _ in range(2)]
            x1t = [self.sbt(es, "x1t", [128, DM], F32) for _ in range(2)]
            Bx1 = [Buf("x1t") for _ in range(2)]
            junk2 = [self.sbt(es, "junk", [128, DM], BF16) for _ in range(2)]
            Bj2 = [Buf("junk") for _ in range(2)]
            ss = [self.sbt(es, "ss", [128, 2], F32) for _ in range(2)]
            rs = [self.sbt(es, "rs", [128, 1], F32) for _ in range(2)]
            Bss = [Buf("ss") for _ in range(2)]
            BpA = [Buf("pA") for _ in range(2)]
            BpG = [Buf("pG") for _ in range(2)]
            Bpy = [Buf("py") for _ in range(2)]
            cnt = {"f": 0, "tt": 0}

            def load_u(u):
                b = u % 2
                tr.dma("sp", xu[b], S["xT2"][:, u * 256:u * 256 + 258].rearrange("(kc p) t -> p kc t", p=128), writes=[Bxu[b]])
                seg = u // UPS
                if u % UPS == 0:
                    fl = self.flg[:, seg:seg + 1]
                    tr.op("pool", lambda: nc.gpsimd.tensor_scalar_mul(out=xu[b][:, :, 0:1], in0=xu[b][:, :, 0:1], scalar1=fl), reads=[Bxu[b], self.B_const], writes=[Bxu[b]])
                if u % UPS == UPS - 1:
                    fl = self.flg[:, seg + 1:seg + 2]
                    tr.op("pool", lambda: nc.gpsimd.tensor_scalar_mul(out=xu[b][:, :, 257:258], in0=xu[b][:, :, 257:258], scalar1=fl), reads=[Bxu[b], self.B_const], writes=[Bxu[b]])

            def up_mm(u, f):
                b = u % 2
                k = cnt["f"] % 2
                pa = ps[k][:, 0:258]
                pg = ps[2 if k == 0 else 7][:, 0:256]
                for kc in range(8):
                    tr.op("pe", lambda kc=kc: nc.tensor.matmul(pa, lhsT=wup[:, kc, f * 128:(f + 1) * 128], rhs=xu[b][:, kc, :], start=(kc == 0), stop=(kc == 7)),
                          reads=[Bxu[b]], writes=[BpA[k]], inc=(kc == 7))
                for kc in range(8):
                    tr.op("pe", lambda kc=kc: nc.tensor.matmul(pg, lhsT=wup[:, kc, DFF + f * 128:DFF + (f + 1) * 128], rhs=xu[b][:, kc, 1:257], start=(kc == 0), stop=(kc == 7)),
                          reads=[Bxu[b]], writes=[BpG[k]], inc=(kc == 7))
                cnt["f"] += 1
                return k

            def ew1(u, f, k):
                pa = ps[k][:, 0:258]
                pg = ps[2 if k == 0 else 7][:, 0:256]
                c = (u * NFT + f) % NC3
                tr.op("act", lambda: nc.scalar.copy(out=asb[c], in_=pa), reads=[BpA[k]], writes=[Basb[c]])
                tr.op("act", lambda: nc.scalar.copy(out=gsb[c], in_=pg), reads=[BpG[k]], writes=[Bgsb[c]])
                tr.op("pool", lambda: nc.gpsimd.tensor_scalar(out=c1[c], in0=asb[c][:, 1:257], scalar1=cw[:, f, 1:2], scalar2=cw[:, f, 3:4], op0=ALU.mult, op1=ALU.add),
                      reads=[Basb[c], Bc], writes=[Bc1[c]])

            def ew2(u, f):
                c = (u * NFT + f) % NC3
                tr.op("dve", lambda: nc.vector.scalar_tensor_tensor(out=c1[c], in0=asb[c][:, 0:256], scalar=cw[:, f, 0:1], in1=c1[c], op0=ALU.mult, op1=ALU.add),
                      reads=[Basb[c], Bc, Bc1[c]], writes=[Bc1[c]])
                tr.op("dve", lambda: nc.vector.scalar_tensor_tensor(out=c1[c], in0=asb[c][:, 2:258], scalar=cw[:, f, 2:3], in1=c1[c], op0=ALU.mult, op1=ALU.add),
                      reads=[Basb[c], Bc, Bc1[c]], writes=[Bc1[c]])

            def ew3(u, f):
                c = (u * NFT + f) % NC3
                tr.op("act", lambda: nc.scalar.activation(out=ge[c], in_=c1[c], func=AF.Gelu_apprx_tanh), reads=[Bc1[c]], writes=[Bge[c]])
                tr.op("pool", lambda: nc.gpsimd.tensor_tensor(out=fT[:, f, :], in0=ge[c], in1=gsb[c], op=ALU.mult), reads=[Bge[c], Bgsb[c]], writes=[BfT], conc=True)

            def down(u):
                for j in range(2):
                    i = u * 2 + j
                    a = i % 2
                    py = [ps[3 + 2 * a], ps[4 + 2 * a]]
                    tr.dma("sp", x1t[a], S["x1"][i * 128:(i + 1) * 128, :], writes=[Bx1[a]])
                    for n in range(2):
                        for f in range(NFT):
                            tr.op("pe", lambda n=n, f=f: nc.tensor.matmul(py[n], lhsT=fT[:, f, j * 128:(j + 1) * 128], rhs=wdn[:, f, n * 512:(n + 1) * 512],
                                                                          start=(f == 0), stop=(f == NFT - 1)), reads=[BfT], writes=[Bpy[a]],
                                  inc=(f == NFT - 1 and n == 1))
                    for n in range(2):
                        tr.op("act", lambda n=n: nc.scalar.activation(out=junk2[a][:, n * 512:(n + 1) * 512], in_=py[n], func=AF.Square, accum_out=ss[a][:, n:n + 1]),
                              reads=[Bpy[a]], writes=[Bj2[a], Bss[a]])
                    tr.op("dve", lambda: nc.vector.tensor_tensor(out=rs[a], in0=ss[a][:, 0:1], in1=ss[a][:, 1:2], op=ALU.add), reads=[Bss[a]], writes=[Bss[a]])
                    tr.op("act", lambda: nc.scalar.activation(out=rs[a], in_=rs[a], func=AF.Ln, scale=1.0 / DM, bias=eps), reads=[Bss[a], Bc], writes=[Bss[a]])
                    tr.op("act", lambda: nc.scalar.activation(out=rs[a], in_=rs[a], func=AF.Exp, scale=-0.5), reads=[Bss[a]], writes=[Bss[a]])
                    for n in range(2):
                        tr.op("act", lambda n=n: nc.scalar.activation(out=t1[a][:, n * 512:(n + 1) * 512], in_=py[n], func=AF.Copy, scale=rs[a][:, 0:1]),
                              reads=[Bpy[a], Bss[a]], writes=[Bt1[a]])
                    tr.op("pool", lambda: nc.gpsimd.tensor_tensor(out=t1[a], in0=t1[a], in1=npb, op=ALU.mult), reads=[Bt1[a], Bc], writes=[Bt1[a]])
                    tr.op("pool", lambda: nc.gpsimd.tensor_tensor(out=x1t[a], in0=x1t[a], in1=t1[a], op=ALU.add), reads=[Bt1[a], Bx1[a]], writes=[Bx1[a]])
                    tr.dma("sp", x_dst[i * 128:(i + 1) * 128, :], x1t[a], reads=[Bx1[a]])

            load_u(0)
            if NU > 1:
                load_u(1)
            k = up_mm(0, 0)
            ew1(0, 0, k)
            k = up_mm(0, 1)
            ew1(0, 1, k)
            ew2(0, 0)
            for u in range(NU):
                for f in range(2, NFT):
                    k = up_mm(u, f)
                    ew1(u, f, k)
                    ew2(u, f - 1)
                    ew3(u, f - 2)
                nxt = u + 1 < NU
                if nxt:
                    k = up_mm(u + 1, 0)
                    ew1(u + 1, 0, k)
                ew2(u, NFT - 1)
                ew3(u, NFT - 2)
                if nxt:
                    k = up_mm(u + 1, 1)
                    ew1(u + 1, 1, k)
                ew3(u, NFT - 1)
                if nxt:
                    ew2(u + 1, 0)
                down(u)
                if u + 2 < NU:
                    load_u(u + 2)


def make_consts():
    c = np.zeros((128, C_END), np.float32)
    k = np.arange(128)[:, None]
    q = np.arange(128)[None, :]
    dA = np.abs(k - 64 - q).astype(np.float32)
    dB = np.abs(k + 64 - q).astype(np.float32)
    c[:, C_DA:C_DA + 128] = np.where(dA <= 64, dA, BIG)
    c[:, C_DB:C_DB + 128] = np.where(dB <= 64, dB, BIG)
    j, i = k, q
    g = -1.0 / 16.0
    c[:, C_LTF:C_LTF + 128] = np.where(j <= i, g, 0.0)
    c[:, C_LUF:C_LUF + 128] = np.where(j > i, g, 0.0)
    c[:, C_LTB:C_LTB + 128] = np.where(j >= i, g, 0.0)
    c[:, C_LLB:C_LLB + 128] = np.where(j < i, g, 0.0)
    c[:, C_MF:C_MF + 128] = np.where(j <= i, 1.0, 0.0)
    c[:, C_MB:C_MB + 128] = np.where(j >= i, 1.0, 0.0)
    c[:, C_ID:C_ID + 128] = np.eye(128, dtype=np.float32)
    return c


def core_layout(nseg=3):
    segs, links = [], []
    for c in range(NCORES):
        if c < 4:
            segs.append([("s", c, 0), ("s", c, 1), ("p", c, 0)])
            links.append([0.0, 1.0, 0.0, 0.0])
        else:
            b0 = 4 + 3 * (c - 4)
            segs.append([("p", b0, 0), ("p", b0 + 1, 0), ("p", b0 + 2, 0)])
            links.append([0.0, 0.0, 0.0, 0.0])
    return segs, links


def host_params(inp, L=2):
    f = lambda a: np.ascontiguousarray(a, dtype=np.float32)
    P = {}
    P["consts"] = make_consts()
    P["w_in"] = f(inp["w_in"])
    P["w_out"] = f(inp["w_out"])
    P["w_up"] = f(inp["w_up"])
    P["w_down"] = f(inp["w_down"])
    P["g_pre"] = f(inp["norm_mix_pre"].reshape(L, 8, 128).transpose(0, 2, 1))
    P["g_fpre"] = f(inp["norm_ffn_pre"].reshape(L, 8, 128).transpose(0, 2, 1))
    P["n_post"] = f(np.broadcast_to(inp["norm_mix_post"][:, None, :], (L, 128, DM)))
    P["n_fpost"] = f(np.broadcast_to(inp["norm_ffn_post"][:, None, :], (L, 128, DM)))
    P["gnorm"] = f(np.broadcast_to(np.tile(inp["gla_norm"], (1, 8))[:, None, :], (L, 128, 512)))
    P["wg_f"] = f(np.concatenate([inp["b_gate_fwd"][:, None, :], inp["w_gate_fwd"]], axis=1))
    P["wg_b"] = f(np.concatenate([inp["b_gate_bwd"][:, None, :], inp["w_gate_bwd"]], axis=1))
    cw = np.concatenate([inp["conv_w"], inp["conv_b"][:, None, :]], axis=1)
    P["cw"] = f(cw.reshape(L, 4, NFT, 128).transpose(0, 3, 2, 1).reshape(L, 128, NFT * 4))
    return P


def flag_arrays(link):
    n = len(link)
    ns = n - 1
    flg = np.broadcast_to(np.asarray(link, np.float32)[None, :], (128, n)).copy()
    c = make_consts()
    DA = c[:, C_DA:C_DA + 128]
    DB = c[:, C_DB:C_DB + 128]
    kk = np.arange(128)[:, None]
    qq = np.arange(128)[None, :]
    DA_first = np.where((kk < 64) | ((qq < 64) & (kk - 64 - qq < 0)), BIG, DA).astype(np.float32)
    DB_last = np.where(kk >= 64, BIG, DB).astype(np.float32)
    tabs = np.zeros((128, 1 + 3 * ns, 256), np.float32)
    tabs[:, 0, :128], tabs[:, 0, 128:] = DA, DB
    for s in range(ns):
        A = DA if link[s] else DA_first
        B = DB if link[s + 1] else DB_last
        tabs[:, 1 + s, :128], tabs[:, 1 + s, 128:] = A, DB
        tabs[:, 1 + ns + s, :128], tabs[:, 1 + ns + s, 128:] = DA, B
        tabs[:, 1 + 2 * ns + s, :128], tabs[:, 1 + 2 * ns + s, 128:] = A, B
    X = np.zeros((8, 1024), np.float64)
    for h in range(8):
        for (d, nres, ntile) in BRANCHES:
            for j in range(1024):
                qi = (1024 + j) // d - SEG // d + 64
                if qi > 0:
                    X[h, j] += np.exp(-SLOPES[h] * d * np.arange(64 - qi, 64, dtype=np.float64)).sum()
    xden = np.zeros((ns, 8, 1024), np.float32)
    for s in range(ns):
        xden[s] = X * (1.0 - link[s + 1])
    return flg, tabs.reshape(128, -1), xden


_CACHE = {}


def kernel(**inputs):
    inp = {k: np.asarray(v) for k, v in inputs.items()}
    xp, xs = inp["x_prompt"], inp["x_sample"]
    segs, links = core_layout()
    P = host_params(inp)
    in_maps = []
    for c in range(NCORES):
        parts = []
        for (grp, b, half) in segs[c]:
            parts.append(xs[b, half * SEG:(half + 1) * SEG] if grp == "s" else xp[b])
        m = dict(P)
        m["x"] = np.ascontiguousarray(np.concatenate(parts, axis=0), dtype=np.float32)
        m["flg"], m["atab"], m["xden"] = flag_arrays(links[c])
        in_maps.append(m)
    if "nc" not in _CACHE:
        _CACHE["nc"] = Builder(nseg=3, nlayers=2).build()
    res = run_bass_kernel_spmd(_CACHE["nc"], in_maps, core_ids=list(range(NCORES)))
    yp = np.empty_like(xp, dtype=np.float32)
    ys = np.empty_like(xs, dtype=np.float32)
    for c in range(NCORES):
        y = res.results[c]["y"]
        for i, (grp, b, half) in enumerate(segs[c]):
            blk = y[i * SEG:(i + 1) * SEG]
            if grp == "s":
                ys[b, half * SEG:(half + 1) * SEG] = blk
            else:
                yp[b] = blk
    return (yp, ys)
```
